# Optimizing a Trainium2 kernel written in Bass

```python
import math
import jax
import jax.numpy as jnp
from jax import lax
import numpy as np


D_MODEL = 1024
BATCH = 8
SEQ = 2048
DEPTH = 4

GRID_W = 64
CTX_LEN = 256
Q_BLOCK = 128
HEAD_DIM = 64
ROPE_BASE = 10000.0
EPS = 1e-6
D_MIX = D_MODEL
GROUP_WIDTH = D_MIX // 4
D_FF = 4 * D_MODEL

SSD_HEAD_DIM = 64
SSD_HEADS = GROUP_WIDTH // SSD_HEAD_DIM
SSD_INNER = SSD_HEADS * SSD_HEAD_DIM
SSD_GROUPS = 2
SSD_STATE = 64
SSD_CONV = 3
SSD_CHUNK = 128
SSD_CONV_CH = SSD_INNER + 2 * SSD_GROUPS * SSD_STATE
SSD_COLS = SSD_INNER + SSD_CONV_CH + 2 * SSD_HEADS

DIFF_V_DIM = HEAD_DIM
DIFF_HEADS = GROUP_WIDTH // DIFF_V_DIM
DIFF_QK_DIM = DIFF_V_DIM // 2
DIFF_WIDTH = DIFF_HEADS * DIFF_V_DIM
DIFF_COLS = 4 * DIFF_HEADS * DIFF_QK_DIM + DIFF_HEADS * DIFF_V_DIM

GQA_HEADS = GROUP_WIDTH // HEAD_DIM
GQA_KV_HEADS = GQA_HEADS // 2
GQA_COLS = (GQA_HEADS + 2 * GQA_KV_HEADS) * HEAD_DIM

MLA_V_DIM = HEAD_DIM
MLA_HEADS = GROUP_WIDTH // MLA_V_DIM
MLA_NOPE = HEAD_DIM
MLA_ROPE = HEAD_DIM // 2
MLA_Q_LORA = 3 * GROUP_WIDTH // 4
MLA_KV_LORA = GROUP_WIDTH // 2
MLA_COLS = MLA_Q_LORA + MLA_KV_LORA + MLA_ROPE

IN_COLS = SSD_COLS + DIFF_COLS + GQA_COLS + MLA_COLS
COL_SPLITS = (SSD_COLS, SSD_COLS + DIFF_COLS, SSD_COLS + DIFF_COLS + GQA_COLS)

kernel_name = "hybrid_parallel_heads_diffusion_block"


def rmsnorm(x, g):
    xf = x.astype(jnp.float32)
    y = xf * lax.rsqrt(jnp.mean(xf * xf, axis=-1, keepdims=True) + EPS)
    return (y * g.astype(jnp.float32)).astype(x.dtype)


def modulate(h, shift, scale):
    return h * (1.0 + scale) + shift


def squared_relu_mlp(h, w1, w2):
    return jnp.square(jax.nn.relu(h @ w1)) @ w2


def axial_rope(seq, rot_dim):
    rows = seq // GRID_W
    row = jnp.repeat(jnp.arange(rows), GRID_W).astype(jnp.float32)
    col = jnp.tile(jnp.arange(GRID_W), rows).astype(jnp.float32)
    n_freq = rot_dim // 4
    inv = ROPE_BASE ** (-jnp.arange(n_freq, dtype=jnp.float32) / n_freq)
    ang = jnp.concatenate([row[:, None] * inv, col[:, None] * inv], axis=-1)
    return jnp.cos(ang), jnp.sin(ang)


def apply_rope(x, cos, sin):
    half = x.shape[-1] // 2
    x1, x2 = x[..., :half], x[..., half:]
    c = cos.astype(x.dtype)
    s = sin.astype(x.dtype)
    return jnp.concatenate([x1 * c - x2 * s, x1 * s + x2 * c], axis=-1)


def attn_probs(q, k, scale):
    s = jnp.einsum('bkgqd,bktd->bkgqt', q, k).astype(jnp.float32) * scale
    return jax.nn.softmax(s, axis=-1)


def softmax_attention(q, k, v, scale):
    p = attn_probs(q, k, scale).astype(v.dtype)
    return jnp.einsum('bkgqt,bktd->bkgqd', p, v)


def differential_attention(q1, q2, k1, k2, v, lam, scale):
    p = attn_probs(q1, k1, scale) - lam * attn_probs(q2, k2, scale)
    return jnp.einsum('bkgqt,bktd->bkgqd', p.astype(v.dtype), v)


def sweep_query_blocks(fn, *qs):
    b, kh, g, s, _ = qs[0].shape
    nb = s // Q_BLOCK
    blocks = tuple(jnp.moveaxis(q.reshape(b, kh, g, nb, Q_BLOCK, q.shape[-1]), 3, 0) for q in qs)
    out = lax.map(lambda blk: fn(*blk), blocks)
    return jnp.moveaxis(out, 0, 3).reshape(b, kh, g, s, out.shape[-1])


def centred_depthwise_conv(x, w, b):
    k = w.shape[-1]
    rhs = jnp.transpose(w)[:, None, :].astype(x.dtype)
    y = lax.conv_general_dilated(x, rhs, window_strides=(1,), padding=[((k - 1) // 2, k // 2)],
                                 dimension_numbers=('NWC', 'WIO', 'NWC'), feature_group_count=x.shape[-1])
    return y + b.astype(x.dtype)


def seq_flip(t, d):
    return jnp.flip(t, axis=1) if d else t


def segsum_exp(a_cs):
    n = a_cs.shape[-1]
    diff = a_cs[..., :, None] - a_cs[..., None, :]
    mask = jnp.tril(jnp.ones((n, n), dtype=bool))
    return jnp.where(mask, jnp.exp(jnp.where(mask, diff, 0.0)), 0.0)


def ssd_chunked_scan(x, a, bm, cm, h0):
    b, s, h, p = x.shape
    n = bm.shape[-1]
    nc = s // SSD_CHUNK
    xc = x.reshape(b, nc, SSD_CHUNK, h, p)
    bc = bm.reshape(b, nc, SSD_CHUNK, h, n)
    cc = cm.reshape(b, nc, SSD_CHUNK, h, n)
    a_cs = jnp.cumsum(jnp.moveaxis(a.reshape(b, nc, SSD_CHUNK, h), -1, 1), axis=-1)
    scores = jnp.einsum('bclhn,bcshn->bhcls', cc, bc) * segsum_exp(a_cs)
    y_diag = jnp.einsum('bhcls,bcshp->bclhp', scores, xc)
    to_end = jnp.exp(a_cs[..., -1:] - a_cs)
    chunk_states = jnp.einsum('bclhn,bhcl,bclhp->bchpn', bc, to_end, xc)
    chunk_decay = jnp.exp(a_cs[..., -1])

    def step(h_prev, inp):
        st, dec = inp
        return h_prev * dec[..., None, None] + st, h_prev

    h_final, h_starts = lax.scan(step, h0, (jnp.moveaxis(chunk_states, 1, 0), jnp.moveaxis(chunk_decay, -1, 0)))
    h_starts = jnp.moveaxis(h_starts, 0, 1)
    y_off = jnp.einsum('bclhn,bchpn,bhcl->bclhp', cc, h_starts, jnp.exp(a_cs))
    return (y_diag + y_off).reshape(b, s, h, p), h_final


def ssd_mixer(u, uc, conv_w, conv_b, dt_bias, a_log, d_skip, norm_g, with_ctx):
    a_neg = -jnp.exp(a_log.astype(jnp.float32))
    d_skip = d_skip.astype(jnp.float32)

    def prep(v):
        bsz, s = v.shape[:2]
        z, xbc, dt = jnp.split(v, [SSD_INNER, SSD_INNER + SSD_CONV_CH], axis=-1)
        xbc = jax.nn.silu(centred_depthwise_conv(xbc, conv_w, conv_b)).astype(jnp.float32)
        xs, bm, cm = jnp.split(xbc, [SSD_INNER, SSD_INNER + SSD_GROUPS * SSD_STATE], axis=-1)
        rep = SSD_HEADS // SSD_GROUPS
        xs = xs.reshape(bsz, s, SSD_HEADS, SSD_HEAD_DIM)
        bm = jnp.repeat(bm.reshape(bsz, s, SSD_GROUPS, SSD_STATE), rep, axis=2)
        cm = jnp.repeat(cm.reshape(bsz, s, SSD_GROUPS, SSD_STATE), rep, axis=2)
        dt = jax.nn.softplus(dt.astype(jnp.float32).reshape(bsz, s, 2, SSD_HEADS) + dt_bias.astype(jnp.float32))
        return z, xs, bm, cm, dt

    z, xs, bm, cm, dt = prep(u)
    zc, xsc, bmc, cmc, dtc = prep(uc)
    bsz = u.shape[0]
    y = jnp.zeros_like(xs)
    yc = jnp.zeros_like(xsc)
    for d in range(2):
        h0 = jnp.zeros((bsz, SSD_HEADS, SSD_HEAD_DIM, SSD_STATE), jnp.float32)
        ycd, h_ctx = ssd_chunked_scan(seq_flip(xsc * dtc[:, :, d, :, None], d), seq_flip(dtc[:, :, d] * a_neg[d], d),
                                      seq_flip(bmc, d), seq_flip(cmc, d), h0)
        yld, _ = ssd_chunked_scan(seq_flip(xs * dt[:, :, d, :, None], d), seq_flip(dt[:, :, d] * a_neg[d], d),
                                  seq_flip(bm, d), seq_flip(cm, d), h_ctx)
        y = y + seq_flip(yld, d) + d_skip[d][:, None] * xs
        if with_ctx:
            yc = yc + seq_flip(ycd, d) + d_skip[d][:, None] * xsc

    def finish(yy, zz):
        b_, s_ = yy.shape[:2]
        gated = yy.reshape(b_, s_, SSD_INNER) * jax.nn.silu(zz.astype(jnp.float32))
        return rmsnorm(gated, norm_g).astype(zz.dtype)

    return finish(y, z), (finish(yc, zc) if with_ctx else None)


def diff_mixer(u, uc, lam_params, norm_g, lam_init, rope, with_ctx):
    qk_cols = 2 * DIFF_HEADS * DIFF_QK_DIM

    def prep(v, positional, need_q):
        bsz, s = v.shape[:2]
        q, k, val = jnp.split(v, [qk_cols, 2 * qk_cols], axis=-1)
        k = k.reshape(bsz, s, DIFF_HEADS, 2, DIFF_QK_DIM).transpose(3, 0, 2, 1, 4)
        q = q.reshape(bsz, s, DIFF_HEADS, 2, DIFF_QK_DIM).transpose(3, 0, 2, 1, 4) if need_q else None
        if positional:
            q = apply_rope(q, *rope)
            k = apply_rope(k, *rope)
        val = val.reshape(bsz, s, DIFF_HEADS, DIFF_V_DIM).transpose(0, 2, 1, 3)
        return q, k, val

    q, k, v = prep(u, True, True)
    qc, kc, vc = prep(uc, False, with_ctx)
    lp = lam_params.astype(jnp.float32)
    lam = jnp.exp(jnp.sum(lp[0] * lp[1])) - jnp.exp(jnp.sum(lp[2] * lp[3])) + lam_init
    scale = DIFF_QK_DIM ** -0.5
    k_all = jnp.concatenate([kc, k], axis=-2)
    v_all = jnp.concatenate([vc, v], axis=-2)

    def core(q1, q2):
        return differential_attention(q1, q2, k_all[0], k_all[1], v_all, lam, scale)

    def finish(o):
        o = rmsnorm(o, norm_g) * (1.0 - lam_init)
        return o.transpose(0, 2, 1, 3).reshape(o.shape[0], o.shape[2], DIFF_WIDTH)

    y = finish(sweep_query_blocks(core, q[0][:, :, None], q[1][:, :, None])[:, :, 0])
    yc = None
    if with_ctx:
        yc = finish(differential_attention(qc[0][:, :, None], qc[1][:, :, None], kc[0], kc[1], vc, lam, scale)[:, :, 0])
    return y, yc


def gqa_mixer(u, uc, q_norm, k_norm, rope, with_ctx):
    grp = GQA_HEADS // GQA_KV_HEADS

    def prep(v, positional, need_q):
        bsz, s = v.shape[:2]
        q, k, val = jnp.split(v, [GQA_HEADS * HEAD_DIM, (GQA_HEADS + GQA_KV_HEADS) * HEAD_DIM], axis=-1)
        k = rmsnorm(k.reshape(bsz, s, GQA_KV_HEADS, HEAD_DIM), k_norm).transpose(0, 2, 1, 3)
        val = val.reshape(bsz, s, GQA_KV_HEADS, HEAD_DIM).transpose(0, 2, 1, 3)
        q = rmsnorm(q.reshape(bsz, s, GQA_KV_HEADS, grp, HEAD_DIM), q_norm).transpose(0, 2, 3, 1, 4) if need_q else None
        if positional:
            q = apply_rope(q, *rope)
            k = apply_rope(k, *rope)
        return q, k, val

    q, k, v = prep(u, True, True)
    qc, kc, vc = prep(uc, False, with_ctx)
    scale = HEAD_DIM ** -0.5
    k_all = jnp.concatenate([kc, k], axis=-2)
    v_all = jnp.concatenate([vc, v], axis=-2)

    def finish(o):
        return o.transpose(0, 3, 1, 2, 4).reshape(o.shape[0], o.shape[3], GQA_HEADS * HEAD_DIM)

    y = finish(sweep_query_blocks(lambda qb: softmax_attention(qb, k_all, v_all, scale), q))
    yc = finish(softmax_attention(qc, kc, vc, scale)) if with_ctx else None
    return y, yc


def mla_mixer(u, uc, q_norm, kv_norm, w_uq, w_ukv, rope, with_ctx):
    def prep(v, positional, need_q):
        bsz, s = v.shape[:2]
        cq, ckv, k_rope = jnp.split(v, [MLA_Q_LORA, MLA_Q_LORA + MLA_KV_LORA], axis=-1)
        kv = (rmsnorm(ckv, kv_norm) @ w_ukv).reshape(bsz, s, MLA_HEADS, MLA_NOPE + MLA_V_DIM).transpose(0, 2, 1, 3)
        k_nope, val = jnp.split(kv, [MLA_NOPE], axis=-1)
        k_rope = k_rope[:, None]
        q = None
        if need_q:
            q = (rmsnorm(cq, q_norm) @ w_uq).reshape(bsz, s, MLA_HEADS, MLA_NOPE + MLA_ROPE).transpose(0, 2, 1, 3)
            q_nope, q_rope = jnp.split(q, [MLA_NOPE], axis=-1)
            if positional:
                q_rope = apply_rope(q_rope, *rope)
            q = jnp.concatenate([q_nope, q_rope], axis=-1)[:, :, None]
        if positional:
            k_rope = apply_rope(k_rope, *rope)
        k = jnp.concatenate([k_nope, jnp.broadcast_to(k_rope, k_nope.shape[:-1] + (MLA_ROPE,))], axis=-1)
        return q, k, val

    q, k, v = prep(u, True, True)
    qc, kc, vc = prep(uc, False, with_ctx)
    scale = (MLA_NOPE + MLA_ROPE) ** -0.5
    k_all = jnp.concatenate([kc, k], axis=-2)
    v_all = jnp.concatenate([vc, v], axis=-2)

    def finish(o):
        o = o[:, :, 0]
        return o.transpose(0, 2, 1, 3).reshape(o.shape[0], o.shape[2], MLA_HEADS * MLA_V_DIM)

    y = finish(sweep_query_blocks(lambda qb: softmax_attention(qb, k_all, v_all, scale), q))
    yc = finish(softmax_attention(qc, kc, vc, scale)) if with_ctx else None
    return y, yc


def setup_inputs(seed: int = 0) -> dict:
    key = jax.random.key(seed)
    ks = jax.random.split(key, 32)
    f32 = jnp.float32

    def nrm(k, shape, scale):
        return jax.random.normal(k, shape, f32) * scale

    def gain(k, shape):
        return 1.0 + 0.05 * jax.random.normal(k, shape, f32)

    dt0 = jnp.exp(jax.random.uniform(ks[11], (DEPTH, 2, SSD_HEADS), f32, math.log(1e-3), math.log(1e-1)))
    return {
        "x": nrm(ks[0], (BATCH, SEQ, D_MODEL), 1.0),
        "c": nrm(ks[1], (BATCH, D_MODEL), 1.0),
        "ctx": nrm(ks[2], (BATCH, CTX_LEN, D_MODEL), 1.0),
        "c_ctx": nrm(ks[3], (D_MODEL,), 1.0),
        "mod_w": nrm(ks[4], (DEPTH, D_MODEL, 6 * D_MODEL), 0.5 * D_MODEL ** -0.5),
        "mod_b": nrm(ks[5], (DEPTH, 6 * D_MODEL), 0.02),
        "norm1_g": gain(ks[6], (DEPTH, D_MODEL)),
        "norm2_g": gain(ks[7], (DEPTH, D_MODEL)),
        "w_in": nrm(ks[8], (DEPTH, D_MODEL, IN_COLS), D_MODEL ** -0.5),
        "ssd_conv_w": nrm(ks[9], (DEPTH, SSD_CONV_CH, SSD_CONV), SSD_CONV ** -0.5),
        "ssd_conv_b": nrm(ks[10], (DEPTH, SSD_CONV_CH), 0.02),
        "ssd_dt_bias": dt0 + jnp.log(-jnp.expm1(-dt0)),
        "ssd_a_log": jnp.log(jax.random.uniform(ks[12], (DEPTH, 2, SSD_HEADS), f32, 1.0, 16.0)),
        "ssd_d": gain(ks[13], (DEPTH, 2, SSD_HEADS)),
        "ssd_norm_g": gain(ks[14], (DEPTH, SSD_INNER)),
        "diff_lambda": nrm(ks[15], (DEPTH, 4, DIFF_QK_DIM), 0.1),
        "diff_norm_g": gain(ks[16], (DEPTH, DIFF_V_DIM)),
        "gqa_q_norm": gain(ks[17], (DEPTH, HEAD_DIM)),
        "gqa_k_norm": gain(ks[18], (DEPTH, HEAD_DIM)),
        "mla_q_norm": gain(ks[19], (DEPTH, MLA_Q_LORA)),
        "mla_kv_norm": gain(ks[20], (DEPTH, MLA_KV_LORA)),
        "mla_w_uq": nrm(ks[21], (DEPTH, MLA_Q_LORA, MLA_HEADS * (MLA_NOPE + MLA_ROPE)), MLA_Q_LORA ** -0.5),
        "mla_w_ukv": nrm(ks[22], (DEPTH, MLA_KV_LORA, MLA_HEADS * (MLA_NOPE + MLA_V_DIM)), MLA_KV_LORA ** -0.5),
        "w_out": nrm(ks[23], (DEPTH, D_MIX, D_MODEL), D_MIX ** -0.5),
        "mlp_w1": nrm(ks[24], (DEPTH, D_MODEL, D_FF), D_MODEL ** -0.5),
        "mlp_w2": nrm(ks[25], (DEPTH, D_FF, D_MODEL), D_FF ** -0.5),
        "final_norm_g": gain(ks[26], (D_MODEL,)),
    }


def reference(x, c, ctx, c_ctx, mod_w, mod_b, norm1_g, norm2_g, w_in, ssd_conv_w, ssd_conv_b, ssd_dt_bias,
              ssd_a_log, ssd_d, ssd_norm_g, diff_lambda, diff_norm_g, gqa_q_norm, gqa_k_norm, mla_q_norm,
              mla_kv_norm, mla_w_uq, mla_w_ukv, w_out, mlp_w1, mlp_w2, final_norm_g):
    seq = x.shape[1]
    rope_head = axial_rope(seq, HEAD_DIM)
    rope_diff = axial_rope(seq, DIFF_QK_DIM)
    rope_mla = axial_rope(seq, MLA_ROPE)
    cond = jax.nn.silu(c)
    cond_ctx = jax.nn.silu(c_ctx)
    h, hc = x, ctx
    for i in range(DEPTH):
        with_ctx = i < DEPTH - 1
        lam_init = 0.8 - 0.6 * math.exp(-0.3 * i)
        mod = (cond @ mod_w[i] + mod_b[i])[:, None, :]
        mod_c = cond_ctx @ mod_w[i] + mod_b[i]
        sh1, sc1, g1, sh2, sc2, g2 = jnp.split(mod, 6, axis=-1)
        csh1, csc1, cg1, csh2, csc2, cg2 = jnp.split(mod_c, 6, axis=-1)

        u = modulate(rmsnorm(h, norm1_g[i]), sh1, sc1) @ w_in[i]
        uc = modulate(rmsnorm(hc, norm1_g[i]), csh1, csc1) @ w_in[i]
        u_ssd, u_diff, u_gqa, u_mla = jnp.split(u, COL_SPLITS, axis=-1)
        uc_ssd, uc_diff, uc_gqa, uc_mla = jnp.split(uc, COL_SPLITS, axis=-1)

        y_a, yc_a = ssd_mixer(u_ssd, uc_ssd, ssd_conv_w[i], ssd_conv_b[i], ssd_dt_bias[i], ssd_a_log[i],
                              ssd_d[i], ssd_norm_g[i], with_ctx)
        y_b, yc_b = diff_mixer(u_diff, uc_diff, diff_lambda[i], diff_norm_g[i], lam_init, rope_diff, with_ctx)
        y_c, yc_c = gqa_mixer(u_gqa, uc_gqa, gqa_q_norm[i], gqa_k_norm[i], rope_head, with_ctx)
        y_d, yc_d = mla_mixer(u_mla, uc_mla, mla_q_norm[i], mla_kv_norm[i], mla_w_uq[i], mla_w_ukv[i],
                              rope_mla, with_ctx)

        h = h + g1 * (jnp.concatenate([y_a, y_b, y_c, y_d], axis=-1) @ w_out[i])
        h = h + g2 * squared_relu_mlp(modulate(rmsnorm(h, norm2_g[i]), sh2, sc2), mlp_w1[i], mlp_w2[i])
        if with_ctx:
            hc = hc + cg1 * (jnp.concatenate([yc_a, yc_b, yc_c, yc_d], axis=-1) @ w_out[i])
            hc = hc + cg2 * squared_relu_mlp(modulate(rmsnorm(hc, norm2_g[i]), csh2, csc2), mlp_w1[i], mlp_w2[i])
    return rmsnorm(h, final_norm_g)
```

```python
import contextlib
import math
import numpy as np
import concourse.bass as bass
import concourse.mybir as mybir
from concourse.bass_utils import run_bass_kernel_spmd

F32 = mybir.dt.float32
BF16 = mybir.dt.bfloat16
AF = mybir.ActivationFunctionType
ALU = mybir.AluOpType
AX = mybir.AxisListType

D = 1024
KC = 8
CTX = 256
SEQ = 2048
T = CTX + SEQ
NCH = T // 128
DEPTH = 4
EPS = 1e-6
SSD_OFF, DIFF_OFF, GQA_OFF, MLA_OFF, IN_COLS = 0, 776, 1544, 2056, 2408
BLKS = [(0, 256)] + [(256 + 512 * i, 512) for i in range(4)]
NPP = 96
NPB = 216
NEG = -1.0e30


class Op:
    __slots__ = ("eng", "fn", "deps", "is_dma", "sem", "val", "needs_inc", "pseudo")


class Prog:
    ENGS = ("pe", "act", "dve", "pool", "sp")

    def __init__(self, nc):
        self.nc = nc
        self.ops = {e: [] for e in self.ENGS}
        self.last_w = {}
        self.readers = {}
        self.dma_cnt = {}
        self.barrier_ops = []
        self.dma_since = []
        self.seen = set()

    def barrier(self):
        b = []
        for e in self.ENGS:
            for o in reversed(self.ops[e]):
                if not o.is_dma:
                    b.append(o)
                    break
        b.extend(self.dma_since)
        self.dma_since = []
        self.barrier_ops = b

    def op(self, eng, fn, reads=(), writes=(), dma=None):
        o = Op()
        o.eng = eng
        o.fn = fn
        o.is_dma = dma is not None
        o.sem = dma
        o.val = 0
        o.needs_inc = o.is_dma
        ps_reads = [r for r in reads if isinstance(r, tuple) and r[0] == "ps" and r not in writes]
        o.pseudo = frozenset(ps_reads)
        writes = list(writes) + ps_reads
        deps = {}
        for r in reads:
            w = self.last_w.get(r)
            if w is not None:
                raw = r not in w.pseudo
                if id(w) not in deps or raw:
                    deps[id(w)] = (w, raw)
        for r in writes:
            if r not in self.seen:
                self.seen.add(r)
                if isinstance(r, tuple) and r[0] == "@":
                    for b in self.barrier_ops:
                        if id(b) not in deps:
                            deps[id(b)] = (b, True)
            w = self.last_w.get(r)
            if w is not None and id(w) not in deps:
                deps[id(w)] = (w, False)
            for rd in self.readers.get(r, ()):
                if id(rd) not in deps:
                    deps[id(rd)] = (rd, False)
        dl = []
        for w, raw in deps.values():
            if not w.is_dma and w.eng == eng and not o.is_dma:
                if eng == "pe":
                    continue
            dl.append(w)
            w.needs_inc = True
        o.deps = dl
        if o.is_dma:
            self.dma_cnt[dma] = self.dma_cnt.get(dma, 0) + 16
            o.val = self.dma_cnt[dma]
            self.dma_since.append(o)
        for r in reads:
            self.readers.setdefault(r, []).append(o)
        for r in writes:
            self.last_w[r] = o
            self.readers[r] = []
        self.ops[eng].append(o)
        return o

    def emit(self):
        nc = self.nc
        with contextlib.ExitStack() as es:
            esem = {e: es.enter_context(nc.semaphore("s_" + e)) for e in self.ENGS}
            dsem = {k: es.enter_context(nc.semaphore("d_%d" % i)) for i, k in enumerate(self.dma_cnt)}
            for e in self.ENGS:
                c = 0
                for o in self.ops[e]:
                    if o.is_dma:
                        continue
                    if o.needs_inc:
                        c += 1
                        o.val = c
            block = es.enter_context(nc.Block())
            prog = self

            def run(e, engobj):
                waited = {}
                for o in prog.ops[e]:
                    for w in o.deps:
                        if w.is_dma:
                            key = ("d", w.sem)
                            s = dsem[w.sem]
                        else:
                            key = ("e", w.eng)
                            s = esem[w.eng]
                        if waited.get(key, 0) < w.val:
                            engobj.wait_ge(s, w.val)
                            waited[key] = w.val
                    ins = o.fn(engobj)
                    if o.is_dma:
                        ins.then_inc(dsem[o.sem], 16)
                    elif o.needs_inc:
                        ins.then_inc(esem[e], 1)
                if e == "sp":
                    for k, c in prog.dma_cnt.items():
                        if waited.get(("d", k), 0) < c:
                            engobj.wait_ge(dsem[k], c)

            @block.tensor
            def _(eng):
                run("pe", eng)

            @block.scalar
            def _(eng):
                run("act", eng)

            @block.vector
            def _(eng):
                run("dve", eng)

            @block.gpsimd
            def _(eng):
                run("pool", eng)

            @block.sync
            def _(eng):
                run("sp", eng)


class KB:
    def __init__(self, n_layers=DEPTH, dumps=(), stop=None):
        self.n_layers = n_layers
        self.dumps = set(dumps)
        self.stop = stop
        self.nc = bass.Bass("TRN2", target_bir_lowering=False)
        self.P = Prog(self.nc)
        self.es = contextlib.ExitStack()
        self.dump_specs = {}
        self._rr = {}
        self.deferred = []
        self.bgseq = 0

    def dram_in(self, name, shape):
        return self.nc.dram_tensor(name, list(shape), F32, kind="ExternalInput").ap()

    def sb(self, name, shape, dt):
        return self.es.enter_context(self.nc.sbuf_tensor("sb_" + name, list(shape), dt))

    def rot(self, name, n):
        i = self._rr.get(name, 0)
        self._rr[name] = i + 1
        return i % n

    def mm(self, out, lhsT, rhs, start, stop, r, w, **kw):
        self.P.op("pe", lambda e: e.matmul(out, lhsT=lhsT, rhs=rhs, start=start, stop=stop, **kw), r, w)

    def tr(self, out, in_, ident, r, w):
        self.P.op("pe", lambda e: e.transpose(out=out, in_=in_, identity=ident), r, w)

    def act(self, out, in_, func, r, w, scale=None, bias=None, accum=None):
        kw = {}
        if scale is not None:
            kw["scale"] = scale
        if bias is not None:
            kw["bias"] = bias
        if accum is not None:
            kw["accum_out"] = accum
        self.P.op("act", lambda e: e.activation(out=out, in_=in_, func=func, **kw), r, w)

    POOL_ENG = "dve"

    def ts(self, eng, out, in0, s1, s2, op0, op1, r, w):
        if eng == "pool":
            eng = self.POOL_ENG
        if op1 is None:
            self.P.op(eng, lambda e: e.tensor_scalar(out=out, in0=in0, scalar1=s1, scalar2=None, op0=op0), r, w)
        else:
            self.P.op(eng, lambda e: e.tensor_scalar(out=out, in0=in0, scalar1=s1, scalar2=s2, op0=op0, op1=op1), r, w)

    def tt(self, eng, out, in0, in1, op, r, w):
        if eng == "pool":
            eng = self.POOL_ENG
        self.P.op(eng, lambda e: e.tensor_tensor(out=out, in0=in0, in1=in1, op=op), r, w)

    def stt(self, out, in0, scalar, in1, op0, op1, r, w):
        self.P.op("dve", lambda e: e.scalar_tensor_tensor(out=out, in0=in0, scalar=scalar, in1=in1, op0=op0, op1=op1), r, w)

    def cp(self, eng, out, in_, r, w):
        if eng == "pool":
            eng = self.POOL_ENG
        if eng == "act":
            self.P.op("act", lambda e: e.activation(out=out, in_=in_, func=AF.Copy), r, w)
        else:
            self.P.op(eng, lambda e: e.tensor_copy(out=out, in_=in_), r, w)

    def recip(self, out, in_, r, w):
        self.P.op("dve", lambda e: e.reciprocal(out=out, in_=in_), r, w)

    def memset(self, eng, ap, val, w):
        if eng == "pool":
            eng = self.POOL_ENG
        self.P.op(eng, lambda e: e.memset(ap, val), (), w)

    def dma(self, q, out, in_, r, w, sem):
        self.P.op(q, lambda e: e.dma_start(out=out, in_=in_), r, w, dma=sem)

    def dump(self, name, ap, reads):
        if name not in self.dumps:
            return
        dt = ap.dtype
        t = self.nc.dram_tensor("dbg_" + name, list(ap.shape), dt, kind="ExternalOutput").ap()
        self.dump_specs[name] = (list(ap.shape), dt)
        self.dma("sp", t, ap, reads, [("dbg", name)], "dbg_" + name)

    def wk(self):
        i = self.rot("wk", 4)
        return self.wks[i], ("wk", i)

    def wkL(self, i):
        return self.wks[4 + i], ("wk", 4 + i)

    def wbt(self):
        i = self.rot("wb", 4)
        return self.wbs[i], ("wb", i)

    def bank(self, grp):
        ids = {"S": (0, 1), "O": (2, 3, 4), "G": (5, 6, 7)}[grp]
        i = ids[self.rot("bank" + grp, len(ids))]
        return self.banks[i], ("ps", i)

    def blk_of(self, tok):
        return 0 if tok < 256 else 1 + (tok - 256) // 512

    def build(self):
        nc = self.nc
        with self.es:
            self.alloc()
            self.setup()
            for l in range(self.n_layers):
                self.layer(l)
                if self.stop is not None and self.stop[0] == l:
                    break
            if self.stop is None:
                self.final()
            self.P.emit()
        return nc

    def alloc(self):
        self.xin = self.dram_in("xin", [T, D])
        self.ccT = self.dram_in("ccT", [128, KC, 2])
        self.mod_w = self.dram_in("mod_w", [DEPTH, D, 6 * D])
        self.ppd = self.dram_in("pp", [128, DEPTH, NPP])
        self.pbd = self.dram_in("pb", [DEPTH, NPB])
        self.cwd = self.dram_in("conv_wT", [DEPTH, 3, 512])
        self.fgd = self.dram_in("final_gT", [128, KC])
        self.w_in = self.dram_in("w_in", [DEPTH, D, IN_COLS])
        self.w_uq = self.dram_in("mla_w_uq", [DEPTH, 192, 384])
        self.w_ukv = self.dram_in("mla_w_ukv", [DEPTH, 128, 512])
        self.w_out = self.dram_in("w_out", [DEPTH, D, D])
        self.w1 = self.dram_in("mlp_w1", [DEPTH, D, 4 * D])
        self.w2 = self.dram_in("mlp_w2", [DEPTH, 4 * D, D])
        self.cst = self.dram_in("cst", [128, 8, 128])
        self.ropd = self.dram_in("rope", [4, 128, SEQ])
        self.out = self.nc.dram_tensor("out", [SEQ, D], F32, kind="ExternalOutput").ap()
        sb = self.sb
        self.hT = sb("hT", [128, KC, T], F32)
        self.xnT = sb("xnT", [128, KC, T], BF16)
        self.cstf = sb("cstf", [128, 8, 128], F32)
        self.cstb = sb("cstb", [128, 3, 128], BF16)
        self.pp = sb("pp", [128, DEPTH, NPP], F32)
        self.pb = sb("pbb", [128, NPB], F32)
        self.condT = sb("condT", [128, KC, 2], F32)
        self.condb = sb("condb", [128, KC, 2], F32)
        self.modT = sb("modT", [128, 48, 2], F32)
        self.vec = sb("vec", [128, 6, KC, 2], F32)
        self.sm = sb("sm", [128, 64], F32)
        self.rt = [sb("rt%d" % i, [128, 2, 512], BF16) for i in range(2)]
        self.WS = [sb("WS%d" % i, [128, KC, 256], BF16) for i in range(2)]
        self.WQ = sb("WQ", [128, KC, 512], BF16)
        self.WO = sb("WO", [128, 4, 1024], BF16)
        self.PTall = sb("PTall", [128, 4, 512], BF16)
        self.PT = [self.PTall[:, i, :] for i in range(4)]
        self.wks = [sb("wk%d" % i, [128, 512], F32) for i in range(5)]
        self.wbs = [sb("wb%d" % i, [128, 512], BF16) for i in range(4)]
        self.yTh = sb("yTh", [128, 4, 512], BF16)
        self.ARENA = 21248
        self.arena = sb("arena", [128, self.ARENA], BF16)
        self.banks = [self.es.enter_context(self.nc.psum_tensor("bank%d" % i, [128, 512], F32)) for i in range(8)]
        self.idf = self.cstf[:, 0, :]
        self.ones_f = self.cstf[:, 1, :]
        self.U = self.cstf[:, 2, :]
        self.Lm = self.cstf[:, 3, :]
        self.nmf = self.cstf[:, 4, :]
        self.nmb = self.cstf[:, 5, :]
        self.idb = self.cstb[:, 0, :]
        self.ones_b = self.cstb[:, 1, :]
        self.bd64 = self.cstb[:, 2, :]

    def ar_bf(self, off, shape):
        n = int(np.prod(shape[1:]))
        assert off + n <= self.ARENA, (off, n)
        ap = self.arena[:, off:off + n]
        if len(shape) == 2:
            return ap
        names = " ".join("d%d" % i for i in range(1, len(shape)))
        kw = {"d%d" % i: shape[i] for i in range(1, len(shape))}
        return ap.rearrange("p (%s) -> p %s" % (names, names), **kw)

    def ar_f32(self, off, shape):
        n = int(np.prod(shape[1:])) * 2
        assert off % 2 == 0 and off + n <= self.ARENA, (off, n)
        ap = self.arena[:, off:off + n].bitcast(F32)
        if len(shape) == 2:
            return ap
        names = " ".join("d%d" % i for i in range(1, len(shape)))
        kw = {"d%d" % i: shape[i] for i in range(1, len(shape))}
        return ap.rearrange("p (%s) -> p %s" % (names, names), **kw)

    def setup(self):
        P = self.P
        self.dma("sp", self.cstf[:], self.cst, (), ["cstf"], "const0")
        self.dma("sp", self.pp[:], self.ppd, (), ["pp"], "const1")
        self.dma("sp", self.condT[:], self.ccT, (), ["condT"], "const2")
        self.cp("dve", self.cstb[:, 0, :], self.cstf[:, 0, :], ["cstf"], ["idb"])
        self.cp("dve", self.cstb[:, 1, :], self.cstf[:, 1, :], ["cstf"], ["onesb"])
        self.cp("dve", self.cstb[:, 2, :], self.cstf[:, 6, :], ["cstf"], ["bd64"])
        self.act(self.condb[:], self.condT[:], AF.Silu, ["condT"], ["condb"])
        P.barrier()
        xt = [self.ar_f32(i * 2048, [128, D]) for i in range(4)]
        for ch in range(NCH):
            s = ch % 4
            key = ("@", "xin", s)
            self.dma("sp", xt[s], self.xin[ch * 128:(ch + 1) * 128, :], (), [key], "xin%d" % s)
            bi = self.blk_of(ch * 128)
            for g in range(2):
                bk, bkey = self.bank("G")
                for i in range(4):
                    f = g * 4 + i
                    self.tr(bk[:, i * 128:(i + 1) * 128], xt[s][:, f * 128:(f + 1) * 128], self.idf, [key, "cstf"], [bkey])
                self.cp("dve" if g == 0 else "act",
                        self.hT[:, g * 4:(g + 1) * 4, ch * 128:(ch + 1) * 128],
                        bk[:].rearrange("p (a b) -> p a b", a=4), [bkey], [("hT", g * 4 + i, bi) for i in range(4)])

    def load_w(self, dst, src_rows_cols, key, q="pool"):
        self.dma(q, dst, src_rows_cols.rearrange("(kc p) n -> p kc n", p=128), (), [key], "W_" + str(key))

    def layer(self, l):
        self.with_ctx = l < DEPTH - 1
        self.mod(l)
        self.norm(l, 0)
        self.dump("xn1_%d" % l, self.xnT[:, :, 0:768], [("xn", c, b) for c in range(KC) for b in range(2)])
        if self.stop == (l, "norm1"):
            return
        self.gqa(l)
        self.dump("h_gqa_%d" % l, self.hT[:, :, :], [("hT", c, b) for c in range(KC) for b in range(5)])
        if self.stop == (l, "gqa"):
            return
        self.diff(l)
        self.dump("h_diff_%d" % l, self.hT[:, :, :], [("hT", c, b) for c in range(KC) for b in range(5)])
        if self.stop == (l, "diff"):
            return
        self.mla(l)
        self.dump("h_mla_%d" % l, self.hT[:, :, :], [("hT", c, b) for c in range(KC) for b in range(5)])
        if self.stop == (l, "mla"):
            return
        self.ssd(l)
        if self.stop is not None and self.stop[0] == l and self.stop[1].startswith("ssd_"):
            return
        self.dump("h_ssd_%d" % l, self.hT[:, :, :], [("hT", c, b) for c in range(KC) for b in range(5)])
        if self.stop == (l, "ssd"):
            return
        self.norm(l, 1)
        self.mlp(l)
        self.dump("h_mlp_%d" % l, self.hT[:, :, :], [("hT", c, b) for c in range(KC) for b in range(5)])

    def qblocks(self):
        return list(range(5)) if self.with_ctx else list(range(1, 5))

    def mod(self, l):
        P = self.P
        P.barrier()
        NPC = 256
        st = [self.ar_f32(i * (KC * NPC * 2), [128, KC, NPC]) for i in range(2)]
        bk, bkey = self.bank("G")
        bv = bk[:, 0:96].rearrange("p (a b) -> p a b", b=2)
        for pc in range(6 * D // NPC):
            s = pc % 2
            key = ("@", "modw", l, s)
            self.dma("sp", st[s], self.mod_w[l, :, pc * NPC:(pc + 1) * NPC].rearrange("(kc p) n -> p kc n", p=128),
                     (), [key], "modw%d" % s)
            br, brk = self.bank("S")
            for kc in range(KC):
                self.mm(br[0:2, 0:NPC], self.condb[:, kc, :], st[s][:, kc, :], kc == 0, kc == KC - 1,
                        [key, "condb"], [brk])
            row, rowk = self.wk()
            self.cp("dve", row[0:2, 0:NPC], br[0:2, 0:NPC], [brk], [rowk])
            for i in range(NPC // 128):
                m = pc * (NPC // 128) + i
                self.tr(bv[:, m, :], row[0:2, i * 128:(i + 1) * 128], self.idf[0:2, 0:2], [rowk, "cstf"], [bkey])
        self.tt("dve", self.modT[:], bv, self.pp[:, l, 0:48].unsqueeze(2).broadcast_to([128, 48, 2]), ALU.add,
                [bkey, "pp"], ["modT"])
        n1 = self.pp[:, l, 48:56].unsqueeze(2).broadcast_to([128, 8, 2])
        n2 = self.pp[:, l, 56:64].unsqueeze(2).broadcast_to([128, 8, 2])
        self.stt(self.vec[:, 0], self.modT[:, 8:16, :], 1.0, n1, ALU.add, ALU.mult, ["modT", "pp"], [("vec", 0)])
        self.cp("dve", self.vec[:, 1], self.modT[:, 0:8, :], ["modT"], [("vec", 1)])
        self.cp("dve", self.vec[:, 2], self.modT[:, 16:24, :], ["modT"], [("vec", 2)])
        self.stt(self.vec[:, 3], self.modT[:, 32:40, :], 1.0, n2, ALU.add, ALU.mult, ["modT", "pp"], [("vec", 3)])
        self.cp("dve", self.vec[:, 4], self.modT[:, 24:32, :], ["modT"], [("vec", 4)])
        self.cp("dve", self.vec[:, 5], self.modT[:, 40:48, :], ["modT"], [("vec", 5)])
        self.dma("sp", self.pb[:], self.pbd[l].partition_broadcast(128), (), ["pb"], "pb")

    def rstd_from_bank(self, bk, bkey, rows, W, n, out=None):
        sd, sdk = self.wk()
        self.act(sd[0:rows, :W], bk[0:rows, :W], AF.Ln, [bkey, "epsT"], [sdk], scale=1.0 / n, bias=self.epsT[0:rows, :])
        rs, rsk = out if out is not None else self.wk()
        self.act(rs[0:rows, :W], sd[0:rows, :W], AF.Exp, [sdk], [rsk], scale=-0.5)
        return rs, rsk

    def norm(self, l, which, blocks=None):
        si, bi_ = (0, 1) if which == 0 else (3, 4)
        for bi in (blocks if blocks is not None else range(5)):
            s, W = BLKS[bi]
            j = 1 if bi == 0 else 0
            bk, bkey = self.bank("G")
            for c in range(KC):
                sq, sqk = self.wbt()
                self.act(sq[:, :W], self.hT[:, c, s:s + W], AF.Square, [("hT", c, bi)], [sqk])
                self.mm(bk[:, :W], self.ones_b, sq[:, :W], c == 0, c == KC - 1, [sqk, "onesb"], [bkey])
            rs, rsk = self.rstd_from_bank(bk, bkey, 128, W, D, out=self.wkL(0))
            for c in range(KC):
                t, tk = self.wk()
                self.stt(t[:, :W], self.hT[:, c, s:s + W], self.vec[:, si, c, j:j + 1], rs[:, :W], ALU.mult, ALU.mult,
                         [("hT", c, bi), ("vec", si), rsk], [tk])
                self.act(self.xnT[:, c, s:s + W], t[:, :W], AF.Identity, [tk, ("vec", bi_)], [("xn", c, bi)],
                         bias=self.vec[:, bi_, c, j:j + 1])

    def proj(self, bk_ap, bkey, wfn, wkeys, s, W, start=True, stop=True, bi=None, **kw):
        if bi is None:
            bi = self.blk_of(s)
        for kc in range(KC):
            self.mm(bk_ap, wfn(kc), self.xnT[:, kc, s:s + W], start and kc == 0, stop and kc == KC - 1,
                    list(wkeys) + [("xn", kc, bi)], [bkey], **kw)

    def load_rope(self, which, s, W):
        i = self.rot("rt", 2)
        key = ("rt", i)
        src = self.ropd[2 * which:2 * which + 2, :, s - CTX:s - CTX + W].rearrange("a p n -> p a n")
        self.dma("pool", self.rt[i][:, :, :W], src, (), [key], "rt%d" % i)
        return self.rt[i], key

    def attention(self, q_ap, qkeys, k_fn, kkeys, v_fn, vkeys, pbase, W, nkc, scale, ob, obkey):
        LA = 1
        kw = {}
        if pbase != 0:
            kw["tile_position"] = (pbase, 0)
        sbs = {}

        def qk(kc):
            sb_, sk = self.bank("S")
            sbs[kc] = (sb_, sk)
            self.mm(sb_[:, :W], k_fn(kc), q_ap, True, True, list(kkeys) + list(qkeys), [sk], **kw)

        seq0 = self.bgseq
        for kc in range(min(LA, nkc)):
            qk(kc)
        for kc in range(nkc):
            if kc % 5 == 4:
                self.inject()
            if kc + LA < nkc:
                qk(kc + LA)
            sb_, sk = sbs.pop(kc)
            pi = self.rot("PT", 4)
            pt, pk = self.PT[pi], ("PT", pi)
            self.act(pt[:, :W], sb_[:, :W], AF.Exp, [sk], [pk], scale=scale)
            self.mm(ob[0:65, :W], v_fn(kc), pt[:, :W], kc == 0, kc == nkc - 1, [pk] + list(vkeys), [obkey])
        self.flush(older_than=seq0)

    def bcast_row(self, row_tile, row_key, W):
        bB, bBk = self.bank("G")
        self.mm(bB[0:64, :W], self.ones_f[64:65, 0:64], row_tile[64:65, :W], True, True, [row_key, "cstf"], [bBk],
                tile_position=(64, 0))
        c, ck = self.wk()
        self.cp("dve", c[0:64, :W], bB[0:64, :W], [bBk], [ck])
        return c, ck

    def defer(self, gen):
        self.bgseq += 1
        self.deferred.append((self.bgseq, gen))

    def inject(self):
        while self.deferred:
            try:
                next(self.deferred[0][1])
                return
            except StopIteration:
                self.deferred.pop(0)

    def flush(self, older_than=None):
        while self.deferred and (older_than is None or self.deferred[0][0] <= older_than):
            for _ in self.deferred[0][1]:
                pass
            self.deferred.pop(0)

    def finish_head_softmax(self, ob, obk, h, W):
        r, rk = self.wk()
        self.recip(r[64:65, :W], ob[64:65, :W], [obk], [rk])
        yield
        bB, bBk = self.bcast_mm(r, rk, W)
        yield
        c, ck = self.wk()
        self.cp("dve", c[0:64, :W], bB[0:64, :W], [bBk], [ck])
        yield
        self.tt("dve", self.yTh[0:64, h, :W], ob[0:64, :W], c[0:64, :W], ALU.mult, [obk, ck], [("yTh", h)])

    def bcast_mm(self, row_tile, row_key, W):
        bB, bBk = self.bank("G")
        self.mm(bB[0:64, :W], self.ones_f[64:65, 0:64], row_tile[64:65, :W], True, True, [row_key, "cstf"], [bBk],
                tile_position=(64, 0))
        return bB, bBk

    def outproj_heads_gen(self, l, bi, wo_key):
        s, W = BLKS[bi]
        j = 1 if bi == 0 else 0
        for d in range(KC):
            bk, bkey = self.bank("G")
            for h in range(4):
                self.mm(bk[:, :W], self.WO[0:64, h, d * 128:(d + 1) * 128], self.yTh[0:64, h, :W], h == 0, h == 3,
                        [wo_key, ("yTh", h)], [bkey])
            yield
            self.stt(self.hT[:, d, s:s + W], bk[:, :W], self.vec[:, 2, d, j:j + 1], self.hT[:, d, s:s + W],
                     ALU.mult, ALU.add, [bkey, ("vec", 2), ("hT", d, bi)], [("hT", d, bi)])

    def outproj_heads(self, l, bi, wo_key):
        s, W = BLKS[bi]
        j = 1 if bi == 0 else 0
        for d in range(KC):
            bk, bkey = self.bank("G")
            for h in range(4):
                self.mm(bk[:, :W], self.WO[0:64, h, d * 128:(d + 1) * 128], self.yTh[0:64, h, :W], h == 0, h == 3,
                        [wo_key, ("yTh", h)], [bkey])
            self.stt(self.hT[:, d, s:s + W], bk[:, :W], self.vec[:, 2, d, j:j + 1], self.hT[:, d, s:s + W],
                     ALU.mult, ALU.add, [bkey, ("vec", 2), ("hT", d, bi)], [("hT", d, bi)])

    def load_wo_heads(self, l, row0):
        key = ("WO",)
        self.dma("pool", self.WO[0:64, :, :], self.w_out[l, row0:row0 + 256, :].rearrange("(h p) n -> p h n", p=64),
                 (), [key], "WO")
        return key

    def finish_block(self, l, bi, nsub, wo_key):
        s, W = BLKS[bi]
        j = 1 if bi == 0 else 0
        for ft in range(2):
            bk, bkey = self.bank("G")
            bb = bk[:].bitcast(BF16)
            for sub in range(nsub):
                self.tr(bb[:, sub * 128:(sub + 1) * 128], self.ytok[:, sub, ft * 128:(ft + 1) * 128], self.idb,
                        ["ytok", "idb"], [bkey])
            self.cp("act" if ft == 0 else "dve", self.yTh[:, ft, :W], bb[:, :W], [bkey], [("yT", ft)])
        self.outproj(l, bi, wo_key, 2)

    def outproj(self, l, bi, wo_key, gate):
        s, W = BLKS[bi]
        j = 1 if bi == 0 else 0
        for d in range(KC):
            bk, bkey = self.bank("G")
            for kt in range(2):
                self.mm(bk[:, :W], self.WO[:, kt, d * 128:(d + 1) * 128], self.yTh[:, kt, :W], kt == 0, kt == 1,
                        [wo_key, ("yTh", kt)], [bkey])
            self.stt(self.hT[:, d, s:s + W], bk[:, :W], self.vec[:, gate, d, j:j + 1], self.hT[:, d, s:s + W],
                     ALU.mult, ALU.add, [bkey, ("vec", gate), ("hT", d, bi)], [("hT", d, bi)])

    def load_wo(self, l, row0):
        key = ("WO",)
        self.dma("pool", self.WO[:, 0:2, :], self.w_out[l, row0:row0 + 256, :].rearrange("(kt p) n -> p kt n", p=128),
                 (), [key], "WO")
        return key

    def qk_norm_rope(self, l, bA, bAk, bB, bBk, W, seq, gcol, rt, rtk, out_ap, out_key, outs=None):
        if outs is None:
            outs = [(slice(0, 128), out_ap, out_key)]
        sq, sqk = self.wbt()
        self.act(sq[:, :W], bA[:, :W], AF.Square, [bAk], [sqk])
        bC, bCk = self.bank("G")
        self.mm(bC[:, :W], self.bd64, sq[:, :W], True, True, [sqk, "bd64"], [bCk])
        rs, rsk = self.rstd_from_bank(bC, bCk, 128, W, 64)
        g = self.pp[:, l, gcol:gcol + 1]
        gs = self.pp[:, l, gcol + 1:gcol + 2]
        if not seq:
            for rows, oap, okey in outs:
                self.stt(oap, bA[rows, :W], g[rows, :], rs[rows, :W], ALU.mult, ALU.mult, [bAk, rsk, "pp"], [okey])
            return
        a, ak = self.wk()
        self.stt(a[:, :W], bA[:, :W], g, rt[:, 0, :W], ALU.mult, ALU.mult, [bAk, rtk, "pp"], [ak])
        b, bk_ = self.wk()
        self.stt(b[:, :W], bB[:, :W], gs, rt[:, 1, :W], ALU.mult, ALU.mult, [bBk, rtk, "pp"], [bk_])
        self.tt("pool", a[:, :W], a[:, :W], b[:, :W], ALU.add, [ak, bk_], [ak])
        for rows, oap, okey in outs:
            self.tt("pool", oap, a[rows, :W], rs[rows, :W], ALU.mult, [ak, rsk], [okey])

    def qk_norm_rope_gen(self, l, bA, bAk, bB, bBk, W, seq, gcol, rt, rtk, outs):
        sq, sqk = self.wbt()
        self.act(sq[:, :W], bA[:, :W], AF.Square, [bAk], [sqk])
        yield
        bC, bCk = self.bank("G")
        self.mm(bC[:, :W], self.bd64, sq[:, :W], True, True, [sqk, "bd64"], [bCk])
        yield
        sd, sdk = self.wk()
        self.act(sd[:, :W], bC[:, :W], AF.Ln, [bCk, "epsT"], [sdk], scale=1.0 / 64, bias=self.epsT[:, :])
        yield
        rs, rsk = self.wk()
        self.act(rs[:, :W], sd[:, :W], AF.Exp, [sdk], [rsk], scale=-0.5)
        yield
        g = self.pp[:, l, gcol:gcol + 1]
        gs = self.pp[:, l, gcol + 1:gcol + 2]
        if not seq:
            for rows, oap, okey in outs:
                self.stt(oap, bA[rows, :W], g[rows, :], rs[rows, :W], ALU.mult, ALU.mult, [bAk, rsk, "pp"], [okey])
            return
        a, ak = self.wk()
        self.stt(a[:, :W], bA[:, :W], g, rt[:, 0, :W], ALU.mult, ALU.mult, [bAk, rtk, "pp"], [ak])
        b, bk_ = self.wk()
        self.stt(b[:, :W], bB[:, :W], gs, rt[:, 1, :W], ALU.mult, ALU.mult, [bBk, rtk, "pp"], [bk_])
        yield
        self.tt("pool", a[:, :W], a[:, :W], b[:, :W], ALU.add, [ak, bk_], [ak])
        yield
        for rows, oap, okey in outs:
            self.tt("pool", oap, a[rows, :W], rs[rows, :W], ALU.mult, [ak, rsk], [okey])

    def rope32_gen(self, bA, bAk, bB, bBk, W, seq, rt, rtk, outs):
        if not seq:
            for rws, oap, okey in outs:
                self.cp("act", oap, bA[rws, :W], [bAk], [okey])
            return
        lo = min(r.start for r, _, _ in outs)
        hi = max(r.stop for r, _, _ in outs)
        rr = slice(lo, hi)
        a, ak = self.wk()
        self.tt("dve", a[rr, :W], bA[rr, :W], rt[rr, 0, :W], ALU.mult, [bAk, rtk], [ak])
        b, bk_ = self.wk()
        self.tt("dve", b[rr, :W], bB[rr, :W], rt[rr, 1, :W], ALU.mult, [bBk, rtk], [bk_])
        yield
        for rws, oap, okey in outs:
            self.tt("pool", oap, a[rws, :W], b[rws, :W], ALU.add, [ak, bk_], [okey])

    def run_gen(self, g):
        for _ in g:
            pass

    def gqa(self, l):
        P = self.P
        P.barrier()
        kT = self.ar_bf(0, [128, T])
        V = self.ar_bf(T, [128, NCH, 2, 65])
        Vk = ("@", "gqaV", l)
        self.memset("pool", V[:, :, :, 64:65], 1.0, [Vk])
        wkv, wkvk = self.WS[0], ("WS", 0)
        self.load_w(wkv[:], self.w_in[l, :, GQA_OFF + 256:GQA_OFF + 512], wkvk)
        wsw, wswk = self.WS[1], ("WS", 1)
        src = wkv[:, :, 0:128].rearrange("p k (h a d) -> p k h a d", h=2, a=2)
        dst = wsw[:, :, 0:128].rearrange("p k (h a d) -> p k h a d", h=2, a=2)
        self.cp("pool", dst[:, :, :, 0, :], src[:, :, :, 1, :], [wkvk], [wswk])
        self.cp("pool", dst[:, :, :, 1, :], src[:, :, :, 0, :], [wkvk], [wswk])
        wo_key = self.load_wo_heads(l, 512)
        for bi in range(5):
            s, W = BLKS[bi]
            seq = bi > 0
            bA, bAk = self.bank("G")
            self.proj(bA[:, :W], bAk, lambda kc: wkv[:, kc, 0:128], [wkvk], s, W)
            bB = bBk = rt = rtk = None
            if seq:
                bB, bBk = self.bank("G")
                self.proj(bB[:, :W], bBk, lambda kc: wsw[:, kc, 0:128], [wswk], s, W)
                rt, rtk = self.load_rope(1, s, W)
            self.qk_norm_rope(l, bA, bAk, bB, bBk, W, seq, 76, rt, rtk, kT[:, s:s + W], ("@", "gqak", l, bi))
            nchb = W // 128
            bV, bVk = self.bank("G")
            for i in range(nchb):
                ch = s // 128 + i
                for kc in range(KC):
                    self.mm(bV[:, i * 128:(i + 1) * 128], self.xnT[:, kc, ch * 128:(ch + 1) * 128], wkv[:, kc, 128:256],
                            i == 0 and kc == 0, kc == KC - 1, [("xn", kc, bi), wkvk], [bVk], skip_group_check=True)
            self.cp("act", V[:, s // 128:s // 128 + nchb, :, 0:64],
                    bV[:, :W].rearrange("p (c h d) -> p c h d", c=nchb, h=2), [bVk], [Vk])
        wq, wqk = self.WQ, ("WQ",)
        self.load_w(wq[:, :, 0:256], self.w_in[l, :, GQA_OFF:GQA_OFF + 256], wqk)
        wr, wrk = self.WS[0], ("WS", 0)
        wrs, wrsk = self.WS[1], ("WS", 1)
        srcq = wq[:, :, 0:256].rearrange("p k (h a d) -> p k h a d", h=4, a=2)
        for tile, heads in ((0, (0, 2)), (1, (1, 3))):
            for pos, h in enumerate(heads):
                o0 = tile * 128 + pos * 64
                self.cp("pool", wr[:, :, o0:o0 + 64], wq[:, :, h * 64:(h + 1) * 64], [wqk], [wrk])
                self.cp("pool", wrs[:, :, o0:o0 + 32], srcq[:, :, h, 1, :], [wqk], [wrsk])
                self.cp("pool", wrs[:, :, o0 + 32:o0 + 64], srcq[:, :, h, 0, :], [wqk], [wrsk])
        qms = [self.ar_bf(T + NCH * 130 + i * 2048, [128, 4, 512]) for i in range(2)]
        qmks = [("@", "gqaqm", l, i) for i in range(2)]
        for i in range(2):
            self.memset("pool", qms[i], 0.0, [qmks[i]])

        def qproj_gen(bi, buf):
            s, W = BLKS[bi]
            seq = bi > 0
            qm, qmk = qms[buf], qmks[buf]
            rt = rtk = None
            if seq:
                rt, rtk = self.load_rope(1, s, W)
            for qt in range(2):
                bA, bAk = self.bank("G")
                self.proj(bA[:, :W], bAk, lambda kc: wr[:, kc, qt * 128:(qt + 1) * 128], [wrk], s, W)
                yield
                bB = bBk = None
                if seq:
                    bB, bBk = self.bank("G")
                    self.proj(bB[:, :W], bBk, lambda kc: wrs[:, kc, qt * 128:(qt + 1) * 128], [wrsk], s, W)
                    yield
                yield from self.qk_norm_rope_gen(l, bA, bAk, bB, bBk, W, seq, 74, rt, rtk,
                                                 [(slice(0, 64), qm[0:64, qt, :W], qmk),
                                                  (slice(64, 128), qm[64:128, qt + 2, :W], qmk)])
                yield

        blocks = self.qblocks()
        self.run_gen(qproj_gen(blocks[0], 0))
        for idx, bi in enumerate(blocks):
            s, W = BLKS[bi]
            seq = bi > 0
            nkc = NCH if seq else 2
            buf = idx % 2
            qm, qmk = qms[buf], qmks[buf]
            kkeys = [("@", "gqak", l, b) for b in range(5 if seq else 1)]
            for h in range(4):
                if h == 1 and idx + 1 < len(blocks):
                    self.defer(qproj_gen(blocks[idx + 1], 1 - buf))
                ob, obk = self.bank("O")
                self.attention(qm[:, h, :W], [qmk],
                               lambda kc: kT[:, kc * 128:(kc + 1) * 128], kkeys,
                               lambda kc: V[:, kc, h // 2, :], [Vk], 0, W, nkc, 0.125, ob, obk)
                self.defer(self.finish_head_softmax(ob, obk, h, W))
            self.defer(self.outproj_heads_gen(l, bi, wo_key))
        self.flush()

    def rope32(self, bA, bAk, bB, bBk, W, seq, rt, rtk, out_ap, out_key, rows=slice(0, 128), outs=None):
        if outs is None:
            outs = [(rows, out_ap, out_key)]
        if not seq:
            for rws, oap, okey in outs:
                self.cp("act", oap, bA[rws, :W], [bAk], [okey])
            return
        a, ak = self.wk()
        self.tt("dve", a[rows, :W], bA[rows, :W], rt[rows, 0, :W], ALU.mult, [bAk, rtk], [ak])
        b, bk_ = self.wk()
        self.tt("dve", b[rows, :W], bB[rows, :W], rt[rows, 1, :W], ALU.mult, [bBk, rtk], [bk_])
        for rws, oap, okey in outs:
            self.tt("pool", oap, a[rws, :W], b[rws, :W], ALU.add, [ak, bk_], [okey])

    def swap16(self, dst, src, rkey, wkey, ncols):
        s5 = src.rearrange("p k (b a d) -> p k b a d", a=2, d=16)
        d5 = dst.rearrange("p k (b a d) -> p k b a d", a=2, d=16)
        self.cp("pool", d5[:, :, :, 0, :], s5[:, :, :, 1, :], [rkey], [wkey])
        self.cp("pool", d5[:, :, :, 1, :], s5[:, :, :, 0, :], [rkey], [wkey])

    def diff(self, l):
        P = self.P
        P.barrier()
        lam_init = 0.8 - 0.6 * math.exp(-0.3 * l)
        kT = self.ar_bf(0, [128, 2, T])
        V = self.ar_bf(2 * T, [128, NCH, 4, 65])
        Vk = ("@", "diffV", l)
        self.memset("pool", V[:, :, :, 64:65], 1.0, [Vk])
        lp = self.pb[:, 16:144].rearrange("p (a d) -> p a d", a=4)
        sm = self.sm
        t1, t1k = self.wk()
        self.tt("dve", t1[:, 0:32], lp[:, 0, :], lp[:, 1, :], ALU.mult, ["pb"], [t1k])
        self.tt("dve", t1[:, 32:64], lp[:, 2, :], lp[:, 3, :], ALU.mult, ["pb"], [t1k])
        self.P.op("dve", lambda e: e.tensor_reduce(out=sm[:, 0:2], in_=t1[:, 0:64].rearrange("p (a d) -> p a d", a=2),
                                                    axis=AX.X, op=ALU.add), [t1k], ["sm_lam"])
        self.act(sm[:, 2:4], sm[:, 0:2], AF.Exp, ["sm_lam"], ["sm_lam2"])
        self.tt("dve", sm[:, 4:5], sm[:, 3:4], sm[:, 2:3], ALU.subtract, ["sm_lam2"], ["sm_lam3"])
        self.ts("dve", sm[:, 5:6], sm[:, 4:5], -lam_init, None, ALU.add, None, ["sm_lam3"], ["neglam"])
        neglam = sm[:, 5:6]
        gdp = sm[:, 6:7]
        self.ts("dve", gdp, self.pp[:, l, 81:82], 1.0 - lam_init, None, ALU.mult, None, ["pp"], ["gdp"])
        wk_, wkk = self.WS[0], ("WS", 0)
        self.load_w(wk_[:], self.w_in[l, :, DIFF_OFF + 256:DIFF_OFF + 512], wkk)
        wsw, wswk = self.WS[1], ("WS", 1)
        self.swap16(wsw[:], wk_[:], wkk, wswk, 256)
        wv, wvk = self.WQ, ("WQ",)
        self.load_w(wv[:, :, 0:256], self.w_in[l, :, DIFF_OFF + 512:DIFF_OFF + 768], wvk)
        wo_key = self.load_wo_heads(l, 256)
        for bi in range(5):
            s, W = BLKS[bi]
            seq = bi > 0
            rt = rtk = None
            if seq:
                rt, rtk = self.load_rope(0, s, W)
            for kt in range(2):
                bA, bAk = self.bank("G")
                self.proj(bA[:, :W], bAk, lambda kc: wk_[:, kc, kt * 128:(kt + 1) * 128], [wkk], s, W)
                bB = bBk = None
                if seq:
                    bB, bBk = self.bank("G")
                    self.proj(bB[:, :W], bBk, lambda kc: wsw[:, kc, kt * 128:(kt + 1) * 128], [wswk], s, W)
                self.rope32(bA, bAk, bB, bBk, W, seq, rt, rtk, kT[:, kt, s:s + W], ("@", "diffk", l, kt, bi))
            nchb = W // 128
            for i2 in range(0, nchb, 2):
                bV, bVk = self.bank("G")
                for i in range(2):
                    ch = s // 128 + i2 + i
                    for kc in range(KC):
                        self.mm(bV[:, i * 256:(i + 1) * 256], self.xnT[:, kc, ch * 128:(ch + 1) * 128], wv[:, kc, 0:256],
                                i == 0 and kc == 0, kc == KC - 1, [("xn", kc, bi), wvk], [bVk], skip_group_check=True)
                c0 = s // 128 + i2
                self.cp("act", V[:, c0:c0 + 2, :, 0:64], bV[:].rearrange("p (c h d) -> p c h d", c=2, h=4), [bVk], [Vk])
        wq, wqk = self.WQ, ("WQ",)
        self.load_w(wq[:, :, 0:256], self.w_in[l, :, DIFF_OFF:DIFF_OFF + 256], wqk)
        self.swap16(wq[:, :, 256:512], wq[:, :, 0:256], wqk, wqk, 256)
        qms = [self.ar_bf(2 * T + NCH * 260 + i * 4096, [128, 8, 512]) for i in range(2)]
        qmks = [("@", "diffqm", l, i) for i in range(2)]
        for i in range(2):
            self.memset("pool", qms[i], 0.0, [qmks[i]])

        def qproj_gen(bi, buf):
            s, W = BLKS[bi]
            seq = bi > 0
            qm, qmk = qms[buf], qmks[buf]
            rt = rtk = None
            if seq:
                rt, rtk = self.load_rope(0, s, W)
            for qt in range(2):
                bA, bAk = self.bank("G")
                self.proj(bA[:, :W], bAk, lambda kc: wq[:, kc, qt * 128:(qt + 1) * 128], [wqk], s, W)
                yield
                bB = bBk = None
                if seq:
                    bB, bBk = self.bank("G")
                    self.proj(bB[:, :W], bBk, lambda kc: wq[:, kc, 256 + qt * 128:256 + (qt + 1) * 128], [wqk], s, W)
                    yield
                yield from self.rope32_gen(bA, bAk, bB, bBk, W, seq, rt, rtk,
                                           [(slice(32 * j, 32 * j + 32), qm[32 * j:32 * j + 32, 4 * qt + j, :W], qmk)
                                            for j in range(4)])
                yield

        def diff_finish(obs, h, W):
            (o1, o1k), (o2, o2k) = obs
            r, rk = self.wk()
            self.recip(r[64:65, :W], o1[64:65, :W], [o1k], [rk])
            r2, r2k = self.wk()
            self.recip(r2[64:65, :W], o2[64:65, :W], [o2k], [r2k])
            self.ts("dve", r2[64:65, :W], r2[64:65, :W], neglam[64:65, :], None, ALU.mult, None, [r2k, "neglam"], [r2k])
            yield
            b1, b1k = self.bcast_mm(r, rk, W)
            b2, b2k = self.bcast_mm(r2, r2k, W)
            yield
            c1, c1k = self.wk()
            self.cp("dve", c1[0:64, :W], b1[0:64, :W], [b1k], [c1k])
            c2, c2k = self.wk()
            self.cp("dve", c2[0:64, :W], b2[0:64, :W], [b2k], [c2k])
            yield
            self.tt("dve", c1[0:64, :W], o1[0:64, :W], c1[0:64, :W], ALU.mult, [o1k, c1k], [c1k])
            self.tt("dve", c2[0:64, :W], o2[0:64, :W], c2[0:64, :W], ALU.mult, [o2k, c2k], [c2k])
            yield
            self.tt("pool", c1[0:64, :W], c1[0:64, :W], c2[0:64, :W], ALU.add, [c1k, c2k], [c1k])
            yield
            sqb, sqbk = self.wbt()
            self.tt("pool", sqb[0:64, :W], c1[0:64, :W], c1[0:64, :W], ALU.mult, [c1k], [sqbk])
            yield
            bN, bNk = self.bank("G")
            self.mm(bN[0:64, :W], self.ones_b[0:64, 0:64], sqb[0:64, :W], True, True, [sqbk, "onesb"], [bNk])
            yield
            sd, sdk = self.wkL(0)
            self.act(sd[0:64, :W], bN[0:64, :W], AF.Ln, [bNk, "epsT"], [sdk], scale=1.0 / 64, bias=self.epsT[0:64, :])
            yield
            self.act(sd[0:64, :W], sd[0:64, :W], AF.Exp, [sdk], [sdk], scale=-0.5)
            yield
            self.stt(self.yTh[0:64, h, :W], c1[0:64, :W], gdp[0:64, :], sd[0:64, :W], ALU.mult, ALU.mult,
                     [c1k, sdk, "gdp"], [("yTh", h)])
        blocks = self.qblocks()
        self.run_gen(qproj_gen(blocks[0], 0))
        for idx, bi in enumerate(blocks):
            s, W = BLKS[bi]
            seq = bi > 0
            nkc = NCH if seq else 2
            buf = idx % 2
            qm, qmk = qms[buf], qmks[buf]
            for h in range(4):
                if h == 1 and idx + 1 < len(blocks):
                    self.defer(qproj_gen(blocks[idx + 1], 1 - buf))
                tile = h // 2
                kkeys = [("@", "diffk", l, tile, b) for b in range(5 if seq else 1)]
                obs = []
                for m in range(2):
                    ob, obk = self.bank("O")
                    self.attention(qm[:, 2 * h + m, :W], [qmk],
                                   lambda kc: kT[:, tile, kc * 128:(kc + 1) * 128], kkeys,
                                   lambda kc: V[:, kc, h, :], [Vk], 0, W, nkc, 32 ** -0.5, ob, obk)
                    obs.append((ob, obk))
                self.defer(diff_finish(obs, h, W))
            self.defer(self.outproj_heads_gen(l, bi, wo_key))
        self.flush()


    def mla(self, l):
        P = self.P
        P.barrier()
        kTh = self.ar_bf(0, [128, 4, T])
        V = self.ar_bf(4 * T, [128, NCH, 4, 65])
        Vk = ("@", "mlaV", l)
        self.memset("pool", V[:, :, :, 64:65], 1.0, [Vk])
        off = 4 * T + NCH * 260
        wukv = self.ar_bf(off, [128, 512]); off += 512
        wuq = self.ar_bf(off, [128, 2, 384]); off += 768
        wuqs = self.ar_bf(off, [128, 2, 128]); off += 256
        qTh = self.ar_bf(off, [128, 4, 512]); off += 2048
        qTh2 = self.ar_bf(off, [128, 4, 512]); off += 2048
        cqn = self.ar_bf(off, [128, 2, 512]); off += 1024
        wukvk, wuqk, wuqsk = ("@", "wukv", l), ("@", "wuq", l), ("@", "wuqs", l)
        self.dma("pool", wukv, self.w_ukv[l], (), [wukvk], "wukv")
        self.dma("pool", wuq[:, 0, :], self.w_uq[l, 0:128, :], (), [wuqk], "wuq")
        self.dma("pool", wuq[0:64, 1, :], self.w_uq[l, 128:192, :], (), [wuqk], "wuq")
        for kt in range(2):
            rows = slice(0, 128) if kt == 0 else slice(0, 64)
            for h in range(4):
                c0 = h * 96 + 64
                self.cp("pool", wuqs[rows, kt, h * 32:h * 32 + 16], wuq[rows, kt, c0 + 16:c0 + 32], [wuqk], [wuqsk])
                self.cp("pool", wuqs[rows, kt, h * 32 + 16:h * 32 + 32], wuq[rows, kt, c0:c0 + 16], [wuqk], [wuqsk])
        wkv, wkvk = self.WS[0], ("WS", 0)
        self.load_w(wkv[:, :, 0:160], self.w_in[l, :, MLA_OFF + 192:MLA_OFF + 352], wkvk)
        self.cp("pool", wkv[:, :, 160:176], wkv[:, :, 144:160], [wkvk], [wkvk])
        self.cp("pool", wkv[:, :, 176:192], wkv[:, :, 128:144], [wkvk], [wkvk])
        wo_key = self.load_wo_heads(l, 768)
        gkv = self.pp[:, l, 80:81]
        for bi in range(5):
            s, W = BLKS[bi]
            seq = bi > 0
            bA, bAk = self.bank("G")
            self.proj(bA[:, :W], bAk, lambda kc: wkv[:, kc, 0:128], [wkvk], s, W)
            sq, sqk = self.wbt()
            self.act(sq[:, :W], bA[:, :W], AF.Square, [bAk], [sqk])
            bC, bCk = self.bank("G")
            self.mm(bC[:, :W], self.ones_b, sq[:, :W], True, True, [sqk, "onesb"], [bCk])
            rs, rsk = self.rstd_from_bank(bC, bCk, 128, W, 128)
            ckvn, ckvnk = self.wbt()
            self.stt(ckvn[:, :W], bA[:, :W], gkv, rs[:, :W], ALU.mult, ALU.mult, [bAk, rsk, "pp"], [ckvnk])
            for h in range(4):
                bK, bKk = self.bank("G")
                self.mm(bK[0:64, :W], wukv[:, h * 128:h * 128 + 64], ckvn[:, :W], True, True, [wukvk, ckvnk], [bKk])
                self.cp("act" if h % 2 == 0 else "dve", kTh[0:64, h, s:s + W], bK[0:64, :W], [bKk], [("@", "mlak", l, h, bi)])
            bR, bRk = self.bank("G")
            self.proj(bR[64:96, :W], bRk, lambda kc: wkv[:, kc, 128:160], [wkvk], s, W, tile_position=(0, 64))
            bR2 = bR2k = rt = rtk = None
            if seq:
                bR2, bR2k = self.bank("G")
                self.proj(bR2[64:96, :W], bR2k, lambda kc: wkv[:, kc, 160:192], [wkvk], s, W,
                          tile_position=(0, 64))
                rt, rtk = self.load_rope(0, s, W)
            self.rope32(bR, bRk, bR2, bR2k, W, seq, rt, rtk, kTh[64:96, 0, s:s + W], ("@", "mlakr", l, 0, bi),
                        rows=slice(64, 96))
            for h in range(1, 4):
                self.cp("pool", kTh[64:96, h, s:s + W], kTh[64:96, 0, s:s + W], [("@", "mlakr", l, 0, bi)],
                        [("@", "mlakr", l, h, bi)])
            nchb = W // 128
            for i2 in range(0, nchb, 2):
                bV, bVk = self.bank("G")
                for i in range(2):
                    cl = i2 + i
                    self.mm(bV[:, i * 256:(i + 1) * 256].rearrange("p (h d) -> p h d", h=4),
                            ckvn[:, cl * 128:(cl + 1) * 128],
                            wukv.rearrange("p (h a d) -> p h a d", h=4, a=2)[:, :, 1, :],
                            i == 0, True, [ckvnk, wukvk], [bVk], skip_group_check=True)
                c0 = s // 128 + i2
                self.cp("act", V[:, c0:c0 + 2, :, 0:64], bV[:].rearrange("p (c h d) -> p c h d", c=2, h=4), [bVk], [Vk])
        wq, wqk = self.WQ, ("WQ",)
        self.load_w(wq[:, :, 0:192], self.w_in[l, :, MLA_OFF:MLA_OFF + 192], wqk)
        qThs = [qTh, qTh2]

        def qproj_gen(bi, buf):
            s, W = BLKS[bi]
            seq = bi > 0
            qT_ = qThs[buf]
            bA, bAk = self.bank("G")
            self.proj(bA[:, :W], bAk, lambda kc: wq[:, kc, 0:128], [wqk], s, W)
            yield
            bB, bBk = self.bank("G")
            self.proj(bB[0:64, :W], bBk, lambda kc: wq[:, kc, 128:192], [wqk], s, W)
            yield
            sqA, sqAk = self.wbt()
            self.act(sqA[:, :W], bA[:, :W], AF.Square, [bAk], [sqAk])
            yield
            sqB, sqBk = self.wbt()
            self.act(sqB[0:64, :W], bB[0:64, :W], AF.Square, [bBk], [sqBk])
            yield
            bC, bCk = self.bank("G")
            self.mm(bC[:, :W], self.ones_b, sqA[:, :W], True, False, [sqAk, "onesb"], [bCk])
            self.mm(bC[:, :W], self.ones_b[0:64, :], sqB[0:64, :W], False, True, [sqBk, "onesb"], [bCk])
            yield
            sd, sdk = self.wk()
            self.act(sd[:, :W], bC[:, :W], AF.Ln, [bCk, "epsT"], [sdk], scale=1.0 / 192, bias=self.epsT[:, :])
            yield
            rs, rsk = self.wk()
            self.act(rs[:, :W], sd[:, :W], AF.Exp, [sdk], [rsk], scale=-0.5)
            yield
            cqk = ("@", "cqn", l)
            self.stt(cqn[:, 0, :W], bA[:, :W], self.pp[:, l, 78:79], rs[:, :W], ALU.mult, ALU.mult, [bAk, rsk, "pp"], [cqk])
            self.stt(cqn[0:64, 1, :W], bB[0:64, :W], self.pp[0:64, l, 79:80], rs[0:64, :W], ALU.mult, ALU.mult,
                     [bBk, rsk, "pp"], [cqk])
            yield
            rt = rtk = None
            if seq:
                rt, rtk = self.load_rope(0, s, W)
            for h in range(4):
                bQ, bQk = self.bank("G")
                self.mm(bQ[0:96, :W], wuq[:, 0, h * 96:(h + 1) * 96], cqn[:, 0, :W], True, False, [wuqk, cqk], [bQk])
                self.mm(bQ[0:96, :W], wuq[0:64, 1, h * 96:(h + 1) * 96], cqn[0:64, 1, :W], False, True, [wuqk, cqk], [bQk])
                bQ2 = bQ2k = None
                if seq:
                    bQ2, bQ2k = self.bank("G")
                    self.mm(bQ2[64:96, :W], wuqs[:, 0, h * 32:(h + 1) * 32], cqn[:, 0, :W], True, False, [wuqsk, cqk], [bQ2k],
                            tile_position=(0, 64))
                    self.mm(bQ2[64:96, :W], wuqs[0:64, 1, h * 32:(h + 1) * 32], cqn[0:64, 1, :W], False, True,
                            [wuqsk, cqk], [bQ2k], tile_position=(0, 64))
                yield
                self.cp("dve", qT_[0:64, h, :W], bQ[0:64, :W], [bQk], [("@", "mlaq", l, h, buf)])
                yield from self.rope32_gen(bQ, bQk, bQ2, bQ2k, W, seq, rt, rtk,
                                           [(slice(64, 96), qT_[64:96, h, :W], ("@", "mlaqr", l, h, buf))])
                yield

        blocks = self.qblocks()
        self.run_gen(qproj_gen(blocks[0], 0))
        for idx, bi in enumerate(blocks):
            s, W = BLKS[bi]
            seq = bi > 0
            nkc = NCH if seq else 2
            buf = idx % 2
            qT_ = qThs[buf]
            for h in range(4):
                kkeys = [("@", "mlak", l, h, b) for b in range(5 if seq else 1)] + \
                        [("@", "mlakr", l, h, b) for b in range(5 if seq else 1)]
                if h == 1 and idx + 1 < len(blocks):
                    self.defer(qproj_gen(blocks[idx + 1], 1 - buf))
                ob, obk = self.bank("O")
                self.attention(qT_[0:96, h, :W], [("@", "mlaq", l, h, buf), ("@", "mlaqr", l, h, buf)],
                               lambda kc: kTh[0:96, h, kc * 128:(kc + 1) * 128], kkeys,
                               lambda kc: V[:, kc, h, :], [Vk], 0, W, nkc, 96 ** -0.5, ob, obk)
                self.defer(self.finish_head_softmax(ob, obk, h, W))
            self.defer(self.outproj_heads_gen(l, bi, wo_key))
        self.flush()

    def ssd(self, l):
        P = self.P
        P.barrier()
        off = 0
        XBC = self.ar_bf(off, [128, 4, T]); off += 4 * T
        Bt = self.ar_bf(off, [128, NCH, 128]); off += NCH * 128
        HT = self.ar_bf(off, [128, NCH, 4, 64]); off += NCH * 256
        dI = self.ar_bf(off, [128, 4, 128]); off += 512
        dtt = self.ar_f32(off, [128, NCH, 8]); off += NCH * 16
        at = self.ar_f32(off, [128, NCH, 8]); off += NCH * 16
        cst_ = self.ar_f32(off, [128, NCH, 8]); off += NCH * 16
        tott = self.ar_f32(off, [128, NCH, 8]); off += NCH * 16
        dtw = self.ar_f32(off, [128, NCH, 8]); off += NCH * 16
        decp = self.ar_f32(off, [128, NCH, 2, 2]); off += NCH * 8
        Hc = self.ar_f32(off, [128, 4, 64]); off += 512
        cw = self.ar_f32(off, [128, 3, 128]); off += 768
        aneg = self.ar_f32(off, [128, 8]); off += 16
        dsb = self.ar_f32(off, [128, 4]); off += 8
        assert off <= self.ARENA, off
        MT = self.PTall[:, 0:2, :].rearrange("p a (j n) -> p (a j) n", j=4)
        CD = self.PTall[:, 2:4, :].rearrange("p a (j n) -> p (a j) n", j=4)
        MTk = [("PT", 0), ("PT", 1)]
        CDk = [("PT", 2), ("PT", 3)]
        k = lambda *a: ("@", "ssd", l) + a

        def xs_tok(ch):
            return XBC[:, 0:2, ch * 128:(ch + 1) * 128]

        def xs_tok_head(ch, h):
            return XBC[:, h // 2, ch * 128 + (h % 2) * 64:ch * 128 + (h % 2) * 64 + 64]

        self.act(aneg, self.pb[:, 8:16], AF.Exp, ["pb"], [k("aneg")])
        self.ts("dve", aneg, aneg, -1.0, None, ALU.mult, None, [k("aneg")], [k("aneg")])
        self.tt("dve", dsb, self.pb[:, 208:212], self.pb[:, 212:216], ALU.add, ["pb"], [k("dsb")])
        for h in range(4):
            self.ts("dve", dI[:, h, :], self.idf, dsb[:, h:h + 1], None, ALU.mult, None, ["cstf", k("dsb")], [k("dI")])
        wo_key = self.load_wo(l, 0)
        WB = self.WQ.rearrange("p k (a n) -> p k a n", a=4)
        for ti in range(4):
            ws, wsk = self.WS[ti % 2], ("WS", ti % 2)
            c0 = SSD_OFF + 256 + ti * 128
            self.load_w(ws[:, :, 0:128], self.w_in[l, :, c0:c0 + 128], wsk)
            self.dma("sp", cw, self.cwd[l, :, ti * 128:(ti + 1) * 128].partition_broadcast(128), (), [k("cw")], "cw")
            wbk = ("WQ",)
            for kk in range(3):
                self.tt("pool" if kk < 2 else "dve", WB[:, :, kk, :], ws[:, :, 0:128],
                        cw[:, kk, :].unsqueeze(1).broadcast_to([128, KC, 128]), ALU.mult, [wsk, k("cw")], [wbk])
            for bi in range(5):
                s, W = BLKS[bi]
                seg0, seg1 = (0, 256) if bi == 0 else (256, T)
                bk, bkey = self.bank("G")
                first = True
                for kk in (1, 0, 2):
                    lo, hi = 0, W
                    if kk == 0 and s == seg0:
                        lo = 1
                    if kk == 2 and s + W == seg1:
                        hi = W - 1
                    t0 = s + lo + kk - 1
                    n = hi - lo
                    bset = sorted({self.blk_of(t0), self.blk_of(t0 + n - 1)})
                    for kc in range(KC):
                        self.mm(bk[:, lo:hi], WB[:, kc, kk, :], self.xnT[:, kc, t0:t0 + n], first,
                                kk == 2 and kc == KC - 1, [wbk] + [("xn", kc, b) for b in bset], [bkey],
                                skip_group_check=True)
                        first = False
                self.act(XBC[:, ti, s:s + W], bk[:, :W], AF.Silu, [bkey, "pp"], [k("xbc", ti, bi)],
                         bias=self.pp[:, l, 64 + ti:65 + ti])
        self.dump("xbc_%d" % l, XBC[:, :, :], [k("xbc", ti, b) for ti in range(4) for b in range(5)])
        if self.stop == (l, "ssd_xbc"):
            return
        wdt, wdtk = self.WS[0], ("WS", 0)
        self.load_w(wdt[:, :, 0:8], self.w_in[l, :, SSD_OFF + 768:SSD_OFF + 776], wdtk)
        bk, bkey = self.bank("G")
        for ch in range(NCH):
            bi = self.blk_of(ch * 128)
            for kc in range(KC):
                self.mm(bk[:, ch * 8:(ch + 1) * 8], self.xnT[:, kc, ch * 128:(ch + 1) * 128], wdt[:, kc, 0:8],
                        ch == 0 and kc == 0, kc == KC - 1, [("xn", kc, bi), wdtk], [bkey], skip_group_check=True)
        b3 = bk[:, 0:NCH * 8].rearrange("p (c j) -> p c j", j=8)
        self.tt("dve", dtt, b3, self.pb[:, 0:8].unsqueeze(1).broadcast_to([128, NCH, 8]), ALU.add, [bkey, "pb"], [k("dt")])
        self.act(dtt, dtt, AF.Exp, [k("dt")], [k("dt")])
        self.act(dtt, dtt, AF.Ln, [k("dt"), "oneT"], [k("dt")], bias=self.oneT[:, :])
        self.tt("dve", at, dtt, aneg.unsqueeze(1).broadcast_to([128, NCH, 8]), ALU.mult, [k("dt"), k("aneg")], [k("at")])
        self.dump("dt_%d" % l, dtt, [k("dt")])
        if self.stop == (l, "ssd_dt"):
            return
        bk, bkey = self.bank("G")
        bk2, bkey2 = self.bank("G")
        c3 = bk[:, 0:NCH * 8].rearrange("p (c j) -> p c j", j=8)
        t3 = bk2[:, 0:NCH * 8].rearrange("p (c j) -> p c j", j=8)
        for ch in range(NCH):
            self.mm(c3[:, ch, 0:4], self.U, at[:, ch, 0:4], ch == 0, True, [k("at"), "cstf"], [bkey], skip_group_check=True)
            self.mm(c3[:, ch, 4:8], self.Lm, at[:, ch, 4:8], False, True, [k("at"), "cstf"], [bkey], skip_group_check=True)
            self.mm(t3[:, ch, :], self.ones_f, at[:, ch, :], ch == 0, True, [k("at"), "cstf"], [bkey2], skip_group_check=True)
        self.cp("dve", cst_, c3, [bkey], [k("cs")])
        self.cp("dve", tott, t3, [bkey2], [k("tot")])
        self.tt("dve", dtw, tott, cst_, ALU.subtract, [k("tot"), k("cs")], [k("dtw")])
        self.act(dtw, dtw, AF.Exp, [k("dtw")], [k("dtw")])
        self.tt("dve", dtw, dtw, dtt, ALU.mult, [k("dtw"), k("dt")], [k("dtw")])
        t4 = tott.rearrange("p c (d h) -> p c d h", d=2)
        self.act(decp[0:64], t4[0:64, :, :, 0:2], AF.Exp, [k("tot")], [k("decp")])
        self.act(decp[64:128], t4[64:128, :, :, 2:4], AF.Exp, [k("tot")], [k("decp")])
        self.dump("cs_%d" % l, cst_, [k("cs")])
        if self.stop == (l, "ssd_cs"):
            return
        for ch in range(NCH):
            bi = self.blk_of(ch * 128)
            bk, bkey = self.bank("G")
            bb = bk[:].bitcast(BF16)
            for ti in range(3):
                self.tr(bb[:, ti * 128:(ti + 1) * 128], XBC[:, ti, ch * 128:(ch + 1) * 128], self.idb,
                        [k("xbc", ti, bi), "idb"], [bkey])
            self.cp("act", xs_tok(ch), bb[:, 0:256].rearrange("p (a n) -> p a n", a=2), [bkey], [k("xst", ch)])
            self.cp("dve", Bt[:, ch, :], bb[:, 256:384], [bkey], [k("bt", ch)])
        self.dump("xst_%d" % l, XBC[:, 0:2, :], [k("xst", ch) for ch in range(NCH)])
        self.dump("bt_%d" % l, Bt, [k("bt", ch) for ch in range(NCH)])
        if self.stop == (l, "ssd_tr"):
            return
        self.memset("dve", Hc, 0.0, [k("Hc", 0), k("Hc", 1)])
        orders = [list(range(NCH)), [1, 0] + list(range(NCH - 1, 1, -1))]
        for d in range(2):
            for ch in orders[d]:
                xwt, xwk = self.wbt()
                xw = xwt[:, 0:256]
                self.tt("pool", xw.rearrange("p (a h e) -> p a h e", a=2, h=2),
                        xs_tok(ch).rearrange("p a (h e) -> p a h e", h=2),
                        dtw[:, ch, d * 4:(d + 1) * 4].rearrange("p (a h) -> p a h", a=2).unsqueeze(3).broadcast_to([128, 2, 2, 64]),
                        ALU.mult, [k("xst", ch), k("dtw")], [xwk])
                bk, bkey = self.bank("S")
                for g in range(2):
                    self.mm(bk[64 * g:64 * g + 64, 0:128], Bt[:, ch, 64 * g:64 * g + 64],
                            xw[:, g * 128:(g + 1) * 128], True, True, [k("bt", ch), xwk], [bkey],
                            tile_position=(0, 64 * g), skip_group_check=True)
                hc = Hc[:, 2 * d:2 * d + 2, :]
                self.cp("act", HT[:, ch, 2 * d:2 * d + 2, :], hc, [k("Hc", d)], [k("HT", ch, d)])
                self.tt("dve", hc, hc, decp[:, ch, d, :].unsqueeze(2).broadcast_to([128, 2, 64]), ALU.mult,
                        [k("Hc", d), k("decp")], [k("Hc", d)])
                self.tt("dve", hc, hc, bk[:, 0:128].rearrange("p (h e) -> p h e", h=2), ALU.add, [k("Hc", d), bkey],
                        [k("Hc", d)])
        self.dump("HT_%d" % l, HT, [k("HT", ch, d) for ch in range(NCH) for d in range(2)])
        if self.stop == (l, "ssd_state"):
            return
        wz, wzk = self.WQ, ("WQ",)
        self.load_w(wz[:, :, 0:256], self.w_in[l, :, SSD_OFF:SSD_OFF + 256], wzk)
        ysub = getattr(self, "ysub", 9)
        for bi in self.qblocks()[:getattr(self, "yblk", 9)]:
            s, W = BLKS[bi]
            for half in range(W // 256):
                hs = s + half * 256
                yb, ybk = self.bank("O")
                for ci in range(2):
                    ch = hs // 128 + ci
                    tok = slice(ch * 128, (ch + 1) * 128)
                    bG, bGk = self.bank("G")
                    czt, czk = self.wbt()
                    cz = czt[:, 0:256].rearrange("p (g n) -> p g n", g=2)
                    self.memset("pool", czt[:, 0:256], 0.0, [czk])
                    self.cp("act", cz[0:64, 0, :], XBC[0:64, 3, tok], [k("xbc", 3, bi), czk], [czk])
                    self.cp("act", cz[64:128, 1, :], XBC[64:128, 3, tok], [k("xbc", 3, bi), czk], [czk])
                    self.mm(bG[:, 0:256], XBC[:, 2, tok], czt[:, 0:256], True, True, [k("xbc", 2, bi), czk], [bGk])
                    xdt, xdk = self.wbt()
                    abcs, bCs, dfs, es = {}, {}, {}, {}
                    for d in range(2):
                        abc, abck = self.wk()
                        self.cp("act", abc[:, :].rearrange("p (h n) -> p h n", h=4),
                                at[:, ch, d * 4:(d + 1) * 4].unsqueeze(2).broadcast_to([128, 4, 128]), [k("at")], [abck])
                        abcs[d] = (abc, abck)
                    for d in range(2):
                        abc, abck = abcs[d]
                        bC, bCk = self.bank("S")
                        for hh in range(4):
                            self.mm(bC[:, hh * 128:(hh + 1) * 128], abc[:, hh * 128:(hh + 1) * 128],
                                    self.U if d == 0 else self.Lm, hh == 0, True, [abck, "cstf"], [bCk], skip_group_check=True)
                        bCs[d] = (bC, bCk)
                    for d in range(2):
                        bC, bCk = bCs[d]
                        df, dfk = self.wk()
                        for hh in range(4):
                            jj = d * 4 + hh
                            self.stt(df[:, hh * 128:(hh + 1) * 128], bC[:, hh * 128:(hh + 1) * 128],
                                     cst_[:, ch, jj:jj + 1], self.nmf if d == 0 else self.nmb, ALU.subtract, ALU.add,
                                     [bCk, k("cs"), "cstf"], [dfk])
                        dfs[d] = (df, dfk)
                    for d in range(2):
                        df, dfk = dfs[d]
                        self.act(df[:, :], df[:, :], AF.Exp, [dfk], [dfk])
                    for d in range(2):
                        bC, bCk = bCs[d]
                        e, ek = self.wk()
                        self.act(e[:, :], bC[:, :], AF.Exp, [bCk], [ek])
                        es[d] = (e, ek)
                    for d in range(2):
                        df, dfk = dfs[d]
                        self.tt("dve", MT[:, d * 4:(d + 1) * 4, :].rearrange("p (g h) n -> p g h n", g=2),
                                df[:, :].rearrange("p (g h n) -> p g h n", g=2, h=2),
                                bG[:, 0:256].rearrange("p (g n) -> p g n", g=2).unsqueeze(2).broadcast_to([128, 2, 2, 128]),
                                ALU.mult, [dfk, bGk], [MTk[d]])
                    for d in range(2):
                        e, ek = es[d]
                        self.tt("dve", CD[:, d * 4:(d + 1) * 4, :], e[:, :].rearrange("p (h n) -> p h n", h=4),
                                XBC[:, 3, tok].unsqueeze(1).broadcast_to([128, 4, 128]), ALU.mult,
                                [ek, k("xbc", 3, bi)], [CDk[d]])
                        self.memset("pool", CD[64:128, d * 4:d * 4 + 2, :], 0.0, [CDk[d]])
                        self.memset("pool", CD[0:64, d * 4 + 2:d * 4 + 4, :], 0.0, [CDk[d]])
                    for d in range(2):
                        self.tt("dve", xdt[:, d * 256:(d + 1) * 256].rearrange("p (a h e) -> p a h e", a=2, h=2),
                                xs_tok(ch).rearrange("p a (h e) -> p a h e", h=2),
                                dtt[:, ch, d * 4:(d + 1) * 4].rearrange("p (a h) -> p a h", a=2).unsqueeze(3).broadcast_to([128, 2, 2, 64]),
                                ALU.mult, [k("xst", ch), k("dt")], [xdk])
                    if l == 0:
                        self.dump("MT_%d" % ch, MT, MTk)
                        self.dump("CD_%d" % ch, CD, CDk)
                        self.dump("XD_%d" % ch, xdt[:, :], [xdk])
                        self.dump("CZ_%d" % ch, czt[:, 0:256], [czk])
                    ylev = getattr(self, "ylev", 9)
                    for h in range(4):
                        if ylev < 1:
                            break
                        hh, g, ytile = h % 2, h // 2, h // 2
                        oap = yb[64 * hh:64 * hh + 64, ci * 256 + ytile * 128:ci * 256 + (ytile + 1) * 128]
                        st = (ci == 0 and ytile == 0)
                        self.mm(oap, xs_tok_head(ch, h), dI[:, h, :], st, False, [k("xst", ch), k("dI")], [ybk],
                                tile_position=(0, 64 * hh), skip_group_check=True)
                        for d in range(2):
                            jj = d * 4 + h
                            if ylev < 2:
                                break
                            self.mm(oap, xdt[:, d * 256 + h * 64:d * 256 + (h + 1) * 64], MT[:, jj, :], False, False,
                                    [xdk, MTk[d]], [ybk], tile_position=(0, 64 * hh), skip_group_check=True)
                            if ylev < 3:
                                continue
                            self.mm(oap, HT[:, ch, 2 * d + hh, :], CD[:, jj, :], False,
                                    d == 1, [k("HT", ch, d), CDk[d]], [ybk], tile_position=(0, 64 * hh),
                                    skip_group_check=True)
                if ylev < 4:
                    continue
                if l == 0:
                    ybs, ybsk = self.wk()
                    if ("YB_%d" % (hs // 128)) in self.dumps:
                        self.cp("act", ybs[:, :], yb[:, :], [ybk], [ybsk])
                        self.dump("YB_%d" % (hs // 128), ybs[:, :], [ybsk])
                y4 = yb[:, :].rearrange("p (c t n) -> p c t n", c=2, t=2)
                gts = []
                for zt in range(2):
                    bZ, bZk = self.bank("G")
                    self.proj(bZ[:, :256], bZk, lambda kc: wz[:, kc, zt * 128:(zt + 1) * 128], [wzk], hs, 256, bi=bi)
                    zs, zsk = self.wk()
                    self.act(zs[:, :256], bZ[:, :256], AF.Silu, [bZk], [zsk])
                    self.tt("dve", zs[:, :256].rearrange("p (c n) -> p c n", c=2), zs[:, :256].rearrange("p (c n) -> p c n", c=2),
                            y4[:, :, zt, :], ALU.mult, [zsk, ybk], [zsk])
                    gts.append((zs, zsk))
                bN, bNk = self.bank("G")
                for zt in range(2):
                    sq, sqk = self.wbt()
                    self.act(sq[:, :256], gts[zt][0][:, :256], AF.Square, [gts[zt][1]], [sqk])
                    self.mm(bN[:, :256], self.ones_b, sq[:, :256], zt == 0, zt == 1, [sqk, "onesb"], [bNk])
                rs, rsk = self.rstd_from_bank(bN, bNk, 128, 256, 256)
                for zt in range(2):
                    self.stt(self.yTh[:, zt, half * 256:(half + 1) * 256], gts[zt][0][:, :256], self.pp[:, l, 72 + zt:73 + zt],
                             rs[:, :256], ALU.mult, ALU.mult, [gts[zt][1], rsk, "pp"], [("yTh", zt)])
            if ylev < 4:
                continue
            self.dump("ssd_yT_%d_%d" % (l, bi), self.yTh[:, :, :W], [("yTh", 0), ("yTh", 1)])
            self.outproj(l, bi, wo_key, 2)

    def mlp(self, l):
        P = self.P
        P.barrier()
        hid = self.ar_bf(0, [128, 4, T])
        W2 = [self.ar_bf(4 * T + i * 4096, [128, 4, 1024]) for i in range(2)]
        blocks = self.qblocks()
        for f in range(8):
            sl = f % 2
            w2k = ("@", "w2", l, sl)
            if sl == 0:
                w1keys = [("WQ",)] * 4
                self.load_w(self.WQ[:], self.w1[l, :, f * 512:(f + 1) * 512], w1keys[0])
                w1aps = [self.WQ[:, :, mt * 128:(mt + 1) * 128] for mt in range(4)]
            else:
                self.load_w(self.WS[0][:], self.w1[l, :, f * 512:f * 512 + 256], ("WS", 0))
                self.load_w(self.WS[1][:], self.w1[l, :, f * 512 + 256:(f + 1) * 512], ("WS", 1))
                w1keys = [("WS", 0), ("WS", 0), ("WS", 1), ("WS", 1)]
                w1aps = [self.WS[mt // 2][:, :, (mt % 2) * 128:(mt % 2 + 1) * 128] for mt in range(4)]
            self.dma("pool", W2[sl], self.w2[l, f * 512:(f + 1) * 512, :].rearrange("(kt p) n -> p kt n", p=128),
                     (), [w2k], "W2_%d" % sl)
            for bi in blocks:
                s, W = BLKS[bi]
                for mt in range(4):
                    bk, bkey = self.bank("S")
                    wap = w1aps[mt]
                    self.proj(bk[:, :W], bkey, lambda kc, wap=wap: wap[:, kc, :], [w1keys[mt]], s, W)
                    r, rk = self.wbt()
                    self.act(r[:, :W], bk[:, :W], AF.Relu, [bkey], [rk])
                    self.tt("pool", hid[:, mt, s:s + W], r[:, :W], r[:, :W], ALU.mult, [rk], [("@", "hid", l, mt, bi)])
            for bi in blocks:
                s, W = BLKS[bi]
                j = 1 if bi == 0 else 0
                for d in range(KC):
                    bk, bkey = self.bank("G")
                    for mt in range(4):
                        self.mm(bk[:, :W], W2[sl][:, mt, d * 128:(d + 1) * 128], hid[:, mt, s:s + W], mt == 0, mt == 3,
                                [w2k, ("@", "hid", l, mt, bi)], [bkey])
                    self.stt(self.hT[:, d, s:s + W], bk[:, :W], self.vec[:, 5, d, j:j + 1], self.hT[:, d, s:s + W],
                             ALU.mult, ALU.add, [bkey, ("vec", 5), ("hT", d, bi)], [("hT", d, bi)])

    def final(self):
        P = self.P
        P.barrier()
        fg = self.ar_f32(0, [128, KC])
        self.dma("sp", fg, self.fgd, (), [("@", "fg")], "fg")
        ot = [self.ar_f32(64 + i * 2048, [128, D]) for i in range(2)]
        for bi in range(1, 5):
            s, W = BLKS[bi]
            bk, bkey = self.bank("G")
            for c in range(KC):
                sq, sqk = self.wbt()
                self.act(sq[:, :W], self.hT[:, c, s:s + W], AF.Square, [("hT", c, bi)], [sqk])
                self.mm(bk[:, :W], self.ones_b, sq[:, :W], c == 0, c == KC - 1, [sqk, "onesb"], [bkey])
            rs, rsk = self.rstd_from_bank(bk, bkey, 128, W, D, out=self.wkL(0))
            for c in range(KC):
                self.stt(self.hT[:, c, s:s + W], self.hT[:, c, s:s + W], fg[:, c:c + 1], rs[:, :W], ALU.mult, ALU.mult,
                         [("hT", c, bi), ("@", "fg"), rsk], [("hT", c, bi)])
            for sub in range(4):
                t0 = s + sub * 128
                i = self.rot("ot", 2)
                okey = ("@", "ot", i)
                for g in range(2):
                    bT, bTk = self.bank("S")
                    for c4 in range(4):
                        c = g * 4 + c4
                        self.tr(bT[:, c4 * 128:(c4 + 1) * 128], self.hT[:, c, t0:t0 + 128], self.idf,
                                [("hT", c, bi), "cstf"], [bTk])
                    self.cp("act" if g == 0 else "dve", ot[i][:, g * 512:(g + 1) * 512], bT[:], [bTk], [okey])
                self.dma("sp", self.out[t0 - CTX:t0 - CTX + 128, :], ot[i], [okey], [("out",)], "out%d" % i)


def _build(n_layers=DEPTH, dumps=(), stop=None):
    kb = KB(n_layers, dumps, stop)
    orig_alloc = kb.alloc

    def alloc2():
        orig_alloc()
        kb.epsT = kb.sb("epsT", [128, 1], F32)
        kb.oneT = kb.sb("oneT", [128, 1], F32)
        kb.WS_all = None
        kb.memset("dve", kb.epsT[:], EPS, ["epsT"])
        kb.memset("dve", kb.oneT[:], 1.0, ["oneT"])
    kb.alloc = alloc2
    nc = kb.build()
    return nc, kb


def _rope_tables():
    def tab(rot):
        rows = SEQ // 64
        row = np.repeat(np.arange(rows), 64).astype(np.float32)
        col = np.tile(np.arange(64), rows).astype(np.float32)
        nf = rot // 4
        inv = (10000.0 ** (-np.arange(nf, dtype=np.float32) / nf)).astype(np.float32)
        ang = np.concatenate([row[:, None] * inv, col[:, None] * inv], axis=-1).astype(np.float32)
        cos, sin = np.cos(ang), np.sin(ang)
        half = rot // 2
        cosT = np.zeros((128, SEQ), np.float32)
        sinT = np.zeros((128, SEQ), np.float32)
        for p in range(128):
            d = p % rot
            i = d % half
            cosT[p] = cos[:, i]
            sinT[p] = -sin[:, i] if d < half else sin[:, i]
        return cosT, sinT
    c32, s32 = tab(32)
    c64, s64 = tab(64)
    return np.stack([c32, s32, c64, s64]).astype(np.float32)


def _consts():
    c = np.zeros((128, 8, 128), np.float32)
    k = np.arange(128)
    c[:, 0, :] = np.eye(128)
    c[:, 1, :] = 1.0
    c[:, 2, :] = (k[:, None] <= k[None, :])
    c[:, 3, :] = (k[:, None] >= k[None, :])
    c[:, 4, :] = np.where(k[None, :] >= k[:, None], 0.0, NEG)
    c[:, 5, :] = np.where(k[None, :] <= k[:, None], 0.0, NEG)
    c[:, 6, :] = (k[:, None] // 64 == k[None, :] // 64)
    return c


def _prep_shared(inp):
    f = lambda a: np.ascontiguousarray(np.asarray(a, dtype=np.float32))
    pp = np.zeros((128, DEPTH, NPP), np.float32)
    pb = np.zeros((DEPTH, NPB), np.float32)
    p = np.arange(128)
    for l in range(DEPTH):
        pp[:, l, 0:48] = f(inp["mod_b"])[l].reshape(48, 128).T
        pp[:, l, 48:56] = f(inp["norm1_g"])[l].reshape(8, 128).T
        pp[:, l, 56:64] = f(inp["norm2_g"])[l].reshape(8, 128).T
        pp[:, l, 64:68] = f(inp["ssd_conv_b"])[l].reshape(4, 128).T
        dd = f(inp["ssd_d"])[l]
        for d in range(2):
            for t in range(2):
                pp[:, l, 68 + d * 2 + t] = dd[d][2 * t + p // 64]
        pp[:, l, 72:74] = f(inp["ssd_norm_g"])[l].reshape(2, 128).T
        gq = f(inp["gqa_q_norm"])[l]
        gk = f(inp["gqa_k_norm"])[l]
        pp[:, l, 74] = gq[p % 64]
        pp[:, l, 75] = gq[(p % 64 + 32) % 64]
        pp[:, l, 76] = gk[p % 64]
        pp[:, l, 77] = gk[(p % 64 + 32) % 64]
        mq = f(inp["mla_q_norm"])[l]
        pp[:, l, 78] = mq[0:128]
        pp[0:64, l, 79] = mq[128:192]
        pp[:, l, 80] = f(inp["mla_kv_norm"])[l]
        pp[:, l, 81] = f(inp["diff_norm_g"])[l][p % 64]
        pb[l, 0:8] = f(inp["ssd_dt_bias"])[l].reshape(8)
        pb[l, 8:16] = f(inp["ssd_a_log"])[l].reshape(8)
        pb[l, 16:144] = f(inp["diff_lambda"])[l].reshape(128)
        pb[l, 144:208] = f(inp["diff_norm_g"])[l]
        pb[l, 208:216] = f(inp["ssd_d"])[l].reshape(8)
    sh = {
        "mod_w": f(inp["mod_w"]), "pp": pp, "pb": pb,
        "conv_wT": np.ascontiguousarray(f(inp["ssd_conv_w"]).transpose(0, 2, 1)),
        "final_gT": np.ascontiguousarray(f(inp["final_norm_g"]).reshape(8, 128).T),
        "w_in": f(inp["w_in"]), "mla_w_uq": f(inp["mla_w_uq"]), "mla_w_ukv": f(inp["mla_w_ukv"]),
        "w_out": f(inp["w_out"]), "mlp_w1": f(inp["mlp_w1"]), "mlp_w2": f(inp["mlp_w2"]),
        "cst": _consts(), "rope": _rope_tables(),
    }
    return sh


def _prep_core(inp, b, sh):
    f = lambda a: np.ascontiguousarray(np.asarray(a, dtype=np.float32))
    m = dict(sh)
    m["xin"] = np.ascontiguousarray(np.concatenate([f(inp["ctx"])[b], f(inp["x"])[b]], axis=0))
    cc = np.stack([f(inp["c"])[b], f(inp["c_ctx"])], axis=0)
    m["ccT"] = np.ascontiguousarray(cc.reshape(2, 8, 128).transpose(2, 1, 0))
    return m


_NC_CACHE = {}


def kernel(**inputs):
    if "nc" not in _NC_CACHE:
        _NC_CACHE["nc"] = _build()[0]
    nc = _NC_CACHE["nc"]
    sh = _prep_shared(inputs)
    in_maps = [_prep_core(inputs, b, sh) for b in range(8)]
    res = run_bass_kernel_spmd(nc, in_maps, core_ids=list(range(8)))
    out = np.stack([np.asarray(r["out"], dtype=np.float32) for r in res.results], axis=0)
    return out
```

```python
import contextlib
import math
import numpy as np
import concourse.bass as bass
import concourse.mybir as mybir
from concourse.bass_utils import run_bass_kernel_spmd

F32 = mybir.dt.float32
BF16 = mybir.dt.bfloat16
AF = mybir.ActivationFunctionType
ALU = mybir.AluOpType
AX = mybir.AxisListType

D = 1024
KC = 8
CTX = 256
SEQ = 2048
T = CTX + SEQ
NCH = T // 128
DEPTH = 4
EPS = 1e-6
SSD_OFF, DIFF_OFF, GQA_OFF, MLA_OFF, IN_COLS = 0, 776, 1544, 2056, 2408
BLKS = [(0, 256)] + [(256 + 512 * i, 512) for i in range(4)]
NPP = 96
NPB = 216
NEG = -1.0e30


class Op:
    __slots__ = ("eng", "fn", "deps", "is_dma", "sem", "val", "needs_inc", "pseudo")


class Prog:
    ENGS = ("pe", "act", "dve", "pool", "sp")

    def __init__(self, nc):
        self.nc = nc
        self.ops = {e: [] for e in self.ENGS}
        self.last_w = {}
        self.readers = {}
        self.dma_cnt = {}
        self.barrier_ops = []
        self.dma_since = []
        self.seen = set()

    def barrier(self):
        b = []
        for e in self.ENGS:
            for o in reversed(self.ops[e]):
                if not o.is_dma:
                    b.append(o)
                    break
        b.extend(self.dma_since)
        self.dma_since = []
        self.barrier_ops = b

    def op(self, eng, fn, reads=(), writes=(), dma=None):
        o = Op()
        o.eng = eng
        o.fn = fn
        o.is_dma = dma is not None
        o.sem = dma
        o.val = 0
        o.needs_inc = o.is_dma
        ps_reads = [r for r in reads if isinstance(r, tuple) and r[0] == "ps" and r not in writes]
        o.pseudo = frozenset(ps_reads)
        writes = list(writes) + ps_reads
        deps = {}
        for r in reads:
            w = self.last_w.get(r)
            if w is not None:
                raw = r not in w.pseudo
                if id(w) not in deps or raw:
                    deps[id(w)] = (w, raw)
        for r in writes:
            if r not in self.seen:
                self.seen.add(r)
                if isinstance(r, tuple) and r[0] == "@":
                    for b in self.barrier_ops:
                        if id(b) not in deps:
                            deps[id(b)] = (b, True)
            w = self.last_w.get(r)
            if w is not None and id(w) not in deps:
                deps[id(w)] = (w, False)
            for rd in self.readers.get(r, ()):
                if id(rd) not in deps:
                    deps[id(rd)] = (rd, False)
        dl = []
        for w, raw in deps.values():
            if not w.is_dma and w.eng == eng and not o.is_dma:
                if eng == "pe":
                    continue
            dl.append(w)
            w.needs_inc = True
        o.deps = dl
        if o.is_dma:
            self.dma_cnt[dma] = self.dma_cnt.get(dma, 0) + 16
            o.val = self.dma_cnt[dma]
            self.dma_since.append(o)
        for r in reads:
            self.readers.setdefault(r, []).append(o)
        for r in writes:
            self.last_w[r] = o
            self.readers[r] = []
        self.ops[eng].append(o)
        return o

    def emit(self):
        nc = self.nc
        with contextlib.ExitStack() as es:
            esem = {e: es.enter_context(nc.semaphore("s_" + e)) for e in self.ENGS}
            dsem = {k: es.enter_context(nc.semaphore("d_%d" % i)) for i, k in enumerate(self.dma_cnt)}
            for e in self.ENGS:
                c = 0
                for o in self.ops[e]:
                    if o.is_dma:
                        continue
                    if o.needs_inc:
                        c += 1
                        o.val = c
            block = es.enter_context(nc.Block())
            prog = self

            def run(e, engobj):
                waited = {}
                for o in prog.ops[e]:
                    for w in o.deps:
                        if w.is_dma:
                            key = ("d", w.sem)
                            s = dsem[w.sem]
                        else:
                            key = ("e", w.eng)
                            s = esem[w.eng]
                        if waited.get(key, 0) < w.val:
                            engobj.wait_ge(s, w.val)
                            waited[key] = w.val
                    ins = o.fn(engobj)
                    if o.is_dma:
                        ins.then_inc(dsem[o.sem], 16)
                    elif o.needs_inc:
                        ins.then_inc(esem[e], 1)
                if e == "sp":
                    for k, c in prog.dma_cnt.items():
                        if waited.get(("d", k), 0) < c:
                            engobj.wait_ge(dsem[k], c)

            @block.tensor
            def _(eng):
                run("pe", eng)

            @block.scalar
            def _(eng):
                run("act", eng)

            @block.vector
            def _(eng):
                run("dve", eng)

            @block.gpsimd
            def _(eng):
                run("pool", eng)

            @block.sync
            def _(eng):
                run("sp", eng)


class KB:
    def __init__(self, n_layers=DEPTH, dumps=(), stop=None):
        self.n_layers = n_layers
        self.dumps = set(dumps)
        self.stop = stop
        self.nc = bass.Bass("TRN2", target_bir_lowering=False)
        self.P = Prog(self.nc)
        self.es = contextlib.ExitStack()
        self.dump_specs = {}
        self._rr = {}
        self.deferred = []
        self.bgseq = 0

    def dram_in(self, name, shape):
        return self.nc.dram_tensor(name, list(shape), F32, kind="ExternalInput").ap()

    def sb(self, name, shape, dt):
        return self.es.enter_context(self.nc.sbuf_tensor("sb_" + name, list(shape), dt))

    def rot(self, name, n):
        i = self._rr.get(name, 0)
        self._rr[name] = i + 1
        return i % n

    def mm(self, out, lhsT, rhs, start, stop, r, w, **kw):
        self.P.op("pe", lambda e: e.matmul(out, lhsT=lhsT, rhs=rhs, start=start, stop=stop, **kw), r, w)

    def tr(self, out, in_, ident, r, w):
        self.P.op("pe", lambda e: e.transpose(out=out, in_=in_, identity=ident), r, w)

    def act(self, out, in_, func, r, w, scale=None, bias=None, accum=None):
        kw = {}
        if scale is not None:
            kw["scale"] = scale
        if bias is not None:
            kw["bias"] = bias
        if accum is not None:
            kw["accum_out"] = accum
        self.P.op("act", lambda e: e.activation(out=out, in_=in_, func=func, **kw), r, w)

    POOL_ENG = "dve"

    def ts(self, eng, out, in0, s1, s2, op0, op1, r, w):
        if eng == "pool":
            eng = self.POOL_ENG
        if op1 is None:
            self.P.op(eng, lambda e: e.tensor_scalar(out=out, in0=in0, scalar1=s1, scalar2=None, op0=op0), r, w)
        else:
            self.P.op(eng, lambda e: e.tensor_scalar(out=out, in0=in0, scalar1=s1, scalar2=s2, op0=op0, op1=op1), r, w)

    def tt(self, eng, out, in0, in1, op, r, w):
        if eng == "pool":
            eng = self.POOL_ENG
        self.P.op(eng, lambda e: e.tensor_tensor(out=out, in0=in0, in1=in1, op=op), r, w)

    def stt(self, out, in0, scalar, in1, op0, op1, r, w):
        self.P.op("dve", lambda e: e.scalar_tensor_tensor(out=out, in0=in0, scalar=scalar, in1=in1, op0=op0, op1=op1), r, w)

    def cp(self, eng, out, in_, r, w):
        if eng == "pool":
            eng = self.POOL_ENG
        if eng == "act":
            self.P.op("act", lambda e: e.activation(out=out, in_=in_, func=AF.Copy), r, w)
        else:
            self.P.op(eng, lambda e: e.tensor_copy(out=out, in_=in_), r, w)

    def recip(self, out, in_, r, w):
        self.P.op("dve", lambda e: e.reciprocal(out=out, in_=in_), r, w)

    def memset(self, eng, ap, val, w):
        if eng == "pool":
            eng = self.POOL_ENG
        self.P.op(eng, lambda e: e.memset(ap, val), (), w)

    def dma(self, q, out, in_, r, w, sem):
        self.P.op(q, lambda e: e.dma_start(out=out, in_=in_), r, w, dma=sem)

    def dump(self, name, ap, reads):
        if name not in self.dumps:
            return
        dt = ap.dtype
        t = self.nc.dram_tensor("dbg_" + name, list(ap.shape), dt, kind="ExternalOutput").ap()
        self.dump_specs[name] = (list(ap.shape), dt)
        self.dma("sp", t, ap, reads, [("dbg", name)], "dbg_" + name)

    def wk(self):
        i = self.rot("wk", 4)
        return self.wks[i], ("wk", i)

    def wkL(self, i):
        return self.wks[4 + i], ("wk", 4 + i)

    def wbt(self):
        i = self.rot("wb", 4)
        return self.wbs[i], ("wb", i)

    def bank(self, grp):
        ids = {"S": (0, 1), "O": (2, 3, 4), "G": (5, 6, 7)}[grp]
        i = ids[self.rot("bank" + grp, len(ids))]
        return self.banks[i], ("ps", i)

    def blk_of(self, tok):
        return 0 if tok < 256 else 1 + (tok - 256) // 512

    def build(self):
        nc = self.nc
        with self.es:
            self.alloc()
            self.setup()
            for l in range(self.n_layers):
                self.layer(l)
                if self.stop is not None and self.stop[0] == l:
                    break
            if self.stop is None:
                self.final()
            self.P.emit()
        return nc

    def alloc(self):
        self.xin = self.dram_in("xin", [T, D])
        self.ccT = self.dram_in("ccT", [128, KC, 2])
        self.mod_w = self.dram_in("mod_w", [DEPTH, D, 6 * D])
        self.ppd = self.dram_in("pp", [128, DEPTH, NPP])
        self.pbd = self.dram_in("pb", [DEPTH, NPB])
        self.cwd = self.dram_in("conv_wT", [DEPTH, 3, 512])
        self.fgd = self.dram_in("final_gT", [128, KC])
        self.w_in = self.dram_in("w_in", [DEPTH, D, IN_COLS])
        self.w_uq = self.dram_in("mla_w_uq", [DEPTH, 192, 384])
        self.w_ukv = self.dram_in("mla_w_ukv", [DEPTH, 128, 512])
        self.w_out = self.dram_in("w_out", [DEPTH, D, D])
        self.w1 = self.dram_in("mlp_w1", [DEPTH, D, 4 * D])
        self.w2 = self.dram_in("mlp_w2", [DEPTH, 4 * D, D])
        self.cst = self.dram_in("cst", [128, 8, 128])
        self.ropd = self.dram_in("rope", [4, 128, SEQ])
        self.out = self.nc.dram_tensor("out", [SEQ, D], F32, kind="ExternalOutput").ap()
        sb = self.sb
        self.hT = sb("hT", [128, KC, T], F32)
        self.xnT = sb("xnT", [128, KC, T], BF16)
        self.cstf = sb("cstf", [128, 8, 128], F32)
        self.cstb = sb("cstb", [128, 3, 128], BF16)
        self.pp = sb("pp", [128, DEPTH, NPP], F32)
        self.pb = sb("pbb", [128, NPB], F32)
        self.condT = sb("condT", [128, KC, 2], F32)
        self.condb = sb("condb", [128, KC, 2], F32)
        self.modT = sb("modT", [128, 48, 2], F32)
        self.vec = sb("vec", [128, 6, KC, 2], F32)
        self.sm = sb("sm", [128, 64], F32)
        self.rt = [sb("rt%d" % i, [128, 2, 512], BF16) for i in range(2)]
        self.WS = [sb("WS%d" % i, [128, KC, 256], BF16) for i in range(2)]
        self.WQ = sb("WQ", [128, KC, 512], BF16)
        self.WO = sb("WO", [128, 4, 1024], BF16)
        self.PTall = sb("PTall", [128, 4, 512], BF16)
        self.PT = [self.PTall[:, i, :] for i in range(4)]
        self.wks = [sb("wk%d" % i, [128, 512], F32) for i in range(5)]
        self.wbs = [sb("wb%d" % i, [128, 512], BF16) for i in range(4)]
        self.yTh = sb("yTh", [64, 4, 512], BF16)
        self.yT = sb("yT", [128, 2, 512], BF16)
        self.ARENA = 20224
        self.arena = sb("arena", [128, self.ARENA], BF16)
        self.banks = [self.es.enter_context(self.nc.psum_tensor("bank%d" % i, [128, 512], F32)) for i in range(8)]
        self.idf = self.cstf[:, 0, :]
        self.ones_f = self.cstf[:, 1, :]
        self.U = self.cstf[:, 2, :]
        self.Lm = self.cstf[:, 3, :]
        self.nmf = self.cstf[:, 4, :]
        self.nmb = self.cstf[:, 5, :]
        self.idb = self.cstb[:, 0, :]
        self.ones_b = self.cstb[:, 1, :]
        self.bd64 = self.cstb[:, 2, :]

    def ar_bf(self, off, shape):
        n = int(np.prod(shape[1:]))
        assert off + n <= self.ARENA, (off, n)
        ap = self.arena[:, off:off + n]
        if len(shape) == 2:
            return ap
        names = " ".join("d%d" % i for i in range(1, len(shape)))
        kw = {"d%d" % i: shape[i] for i in range(1, len(shape))}
        return ap.rearrange("p (%s) -> p %s" % (names, names), **kw)

    def ar_f32(self, off, shape):
        n = int(np.prod(shape[1:])) * 2
        assert off % 2 == 0 and off + n <= self.ARENA, (off, n)
        ap = self.arena[:, off:off + n].bitcast(F32)
        if len(shape) == 2:
            return ap
        names = " ".join("d%d" % i for i in range(1, len(shape)))
        kw = {"d%d" % i: shape[i] for i in range(1, len(shape))}
        return ap.rearrange("p (%s) -> p %s" % (names, names), **kw)

    def setup(self):
        P = self.P
        self.dma("sp", self.cstf[:], self.cst, (), ["cstf"], "const0")
        self.dma("sp", self.pp[:], self.ppd, (), ["pp"], "const1")
        self.dma("sp", self.condT[:], self.ccT, (), ["condT"], "const2")
        self.cp("dve", self.cstb[:, 0, :], self.cstf[:, 0, :], ["cstf"], ["idb"])
        self.cp("dve", self.cstb[:, 1, :], self.cstf[:, 1, :], ["cstf"], ["onesb"])
        self.cp("dve", self.cstb[:, 2, :], self.cstf[:, 6, :], ["cstf"], ["bd64"])
        self.act(self.condb[:], self.condT[:], AF.Silu, ["condT"], ["condb"])
        P.barrier()
        xt = [self.ar_f32(i * 2048, [128, D]) for i in range(4)]
        for ch in range(NCH):
            s = ch % 4
            key = ("@", "xin", s)
            self.dma("sp", xt[s], self.xin[ch * 128:(ch + 1) * 128, :], (), [key], "xin%d" % s)
            bi = self.blk_of(ch * 128)
            for g in range(2):
                bk, bkey = self.bank("G")
                for i in range(4):
                    f = g * 4 + i
                    self.tr(bk[:, i * 128:(i + 1) * 128], xt[s][:, f * 128:(f + 1) * 128], self.idf, [key, "cstf"], [bkey])
                self.cp("dve" if g == 0 else "act",
                        self.hT[:, g * 4:(g + 1) * 4, ch * 128:(ch + 1) * 128],
                        bk[:].rearrange("p (a b) -> p a b", a=4), [bkey], [("hT", g * 4 + i, bi) for i in range(4)])

    def load_w(self, dst, src_rows_cols, key, q="pool"):
        self.dma(q, dst, src_rows_cols.rearrange("(kc p) n -> p kc n", p=128), (), [key], "W_" + str(key))

    def layer(self, l):
        self.with_ctx = l < DEPTH - 1
        self.mod(l)
        self.norm(l, 0)
        self.dump("xn1_%d" % l, self.xnT[:, :, 0:768], [("xn", c, b) for c in range(KC) for b in range(2)])
        if self.stop == (l, "norm1"):
            return
        self.gqa(l)
        self.dump("h_gqa_%d" % l, self.hT[:, :, :], [("hT", c, b) for c in range(KC) for b in range(5)])
        if self.stop == (l, "gqa"):
            return
        self.diff(l)
        self.dump("h_diff_%d" % l, self.hT[:, :, :], [("hT", c, b) for c in range(KC) for b in range(5)])
        if self.stop == (l, "diff"):
            return
        self.mla(l)
        self.dump("h_mla_%d" % l, self.hT[:, :, :], [("hT", c, b) for c in range(KC) for b in range(5)])
        if self.stop == (l, "mla"):
            return
        self.ssd(l)
        if self.stop is not None and self.stop[0] == l and self.stop[1].startswith("ssd_"):
            return
        self.dump("h_ssd_%d" % l, self.hT[:, :, :], [("hT", c, b) for c in range(KC) for b in range(5)])
        if self.stop == (l, "ssd"):
            return
        self.norm(l, 1)
        self.mlp(l)
        self.dump("h_mlp_%d" % l, self.hT[:, :, :], [("hT", c, b) for c in range(KC) for b in range(5)])

    def qblocks(self):
        return list(range(5)) if self.with_ctx else list(range(1, 5))

    def mod(self, l):
        P = self.P
        P.barrier()
        NPC = 256
        st = [self.ar_f32(i * (KC * NPC * 2), [128, KC, NPC]) for i in range(2)]
        bk, bkey = self.bank("G")
        bv = bk[:, 0:96].rearrange("p (a b) -> p a b", b=2)
        for pc in range(6 * D // NPC):
            s = pc % 2
            key = ("@", "modw", l, s)
            self.dma("sp", st[s], self.mod_w[l, :, pc * NPC:(pc + 1) * NPC].rearrange("(kc p) n -> p kc n", p=128),
                     (), [key], "modw%d" % s)
            br, brk = self.bank("S")
            for kc in range(KC):
                self.mm(br[0:2, 0:NPC], self.condb[:, kc, :], st[s][:, kc, :], kc == 0, kc == KC - 1,
                        [key, "condb"], [brk])
            row, rowk = self.wk()
            self.cp("dve", row[0:2, 0:NPC], br[0:2, 0:NPC], [brk], [rowk])
            for i in range(NPC // 128):
                m = pc * (NPC // 128) + i
                self.tr(bv[:, m, :], row[0:2, i * 128:(i + 1) * 128], self.idf[0:2, 0:2], [rowk, "cstf"], [bkey])
        self.tt("dve", self.modT[:], bv, self.pp[:, l, 0:48].unsqueeze(2).broadcast_to([128, 48, 2]), ALU.add,
                [bkey, "pp"], ["modT"])
        n1 = self.pp[:, l, 48:56].unsqueeze(2).broadcast_to([128, 8, 2])
        n2 = self.pp[:, l, 56:64].unsqueeze(2).broadcast_to([128, 8, 2])
        self.stt(self.vec[:, 0], self.modT[:, 8:16, :], 1.0, n1, ALU.add, ALU.mult, ["modT", "pp"], [("vec", 0)])
        self.cp("dve", self.vec[:, 1], self.modT[:, 0:8, :], ["modT"], [("vec", 1)])
        self.cp("dve", self.vec[:, 2], self.modT[:, 16:24, :], ["modT"], [("vec", 2)])
        self.stt(self.vec[:, 3], self.modT[:, 32:40, :], 1.0, n2, ALU.add, ALU.mult, ["modT", "pp"], [("vec", 3)])
        self.cp("dve", self.vec[:, 4], self.modT[:, 24:32, :], ["modT"], [("vec", 4)])
        self.cp("dve", self.vec[:, 5], self.modT[:, 40:48, :], ["modT"], [("vec", 5)])
        self.dma("sp", self.pb[:], self.pbd[l].partition_broadcast(128), (), ["pb"], "pb")

    def rstd_from_bank(self, bk, bkey, rows, W, n, out=None):
        sd, sdk = self.wk()
        self.act(sd[0:rows, :W], bk[0:rows, :W], AF.Ln, [bkey, "epsT"], [sdk], scale=1.0 / n, bias=self.epsT[0:rows, :])
        rs, rsk = out if out is not None else self.wk()
        self.act(rs[0:rows, :W], sd[0:rows, :W], AF.Exp, [sdk], [rsk], scale=-0.5)
        return rs, rsk

    def norm(self, l, which, blocks=None):
        si, bi_ = (0, 1) if which == 0 else (3, 4)
        for bi in (blocks if blocks is not None else range(5)):
            s, W = BLKS[bi]
            j = 1 if bi == 0 else 0
            bk, bkey = self.bank("G")
            for c in range(KC):
                sq, sqk = self.wbt()
                self.act(sq[:, :W], self.hT[:, c, s:s + W], AF.Square, [("hT", c, bi)], [sqk])
                self.mm(bk[:, :W], self.ones_b, sq[:, :W], c == 0, c == KC - 1, [sqk, "onesb"], [bkey])
            rs, rsk = self.rstd_from_bank(bk, bkey, 128, W, D, out=self.wkL(0))
            for c in range(KC):
                t, tk = self.wk()
                self.stt(t[:, :W], self.hT[:, c, s:s + W], self.vec[:, si, c, j:j + 1], rs[:, :W], ALU.mult, ALU.mult,
                         [("hT", c, bi), ("vec", si), rsk], [tk])
                self.act(self.xnT[:, c, s:s + W], t[:, :W], AF.Identity, [tk, ("vec", bi_)], [("xn", c, bi)],
                         bias=self.vec[:, bi_, c, j:j + 1])

    def proj(self, bk_ap, bkey, wfn, wkeys, s, W, start=True, stop=True, bi=None, **kw):
        if bi is None:
            bi = self.blk_of(s)
        for kc in range(KC):
            self.mm(bk_ap, wfn(kc), self.xnT[:, kc, s:s + W], start and kc == 0, stop and kc == KC - 1,
                    list(wkeys) + [("xn", kc, bi)], [bkey], **kw)

    def load_rope(self, which, s, W):
        i = self.rot("rt", 2)
        key = ("rt", i)
        src = self.ropd[2 * which:2 * which + 2, :, s - CTX:s - CTX + W].rearrange("a p n -> p a n")
        self.dma("pool", self.rt[i][:, :, :W], src, (), [key], "rt%d" % i)
        return self.rt[i], key

    def attention(self, q_ap, qkeys, k_fn, kkeys, v_fn, vkeys, pbase, W, nkc, scale, ob, obkey):
        LA = 1
        kw = {}
        if pbase != 0:
            kw["tile_position"] = (pbase, 0)
        sbs = {}

        def qk(kc):
            sb_, sk = self.bank("S")
            sbs[kc] = (sb_, sk)
            self.mm(sb_[:, :W], k_fn(kc), q_ap, True, True, list(kkeys) + list(qkeys), [sk], **kw)

        seq0 = self.bgseq
        for kc in range(min(LA, nkc)):
            qk(kc)
        for kc in range(nkc):
            if kc % 6 == 5:
                self.inject()
            if kc + LA < nkc:
                qk(kc + LA)
            sb_, sk = sbs.pop(kc)
            pi = self.rot("PT", 4)
            pt, pk = self.PT[pi], ("PT", pi)
            self.act(pt[:, :W], sb_[:, :W], AF.Exp, [sk], [pk], scale=scale)
            self.mm(ob[0:65, :W], v_fn(kc), pt[:, :W], kc == 0, kc == nkc - 1, [pk] + list(vkeys), [obkey])
        self.flush(older_than=seq0)

    def bcast_row(self, row_tile, row_key, W):
        bB, bBk = self.bank("G")
        self.mm(bB[0:64, :W], self.ones_f[64:65, 0:64], row_tile[64:65, :W], True, True, [row_key, "cstf"], [bBk],
                tile_position=(64, 0))
        c, ck = self.wk()
        self.cp("dve", c[0:64, :W], bB[0:64, :W], [bBk], [ck])
        return c, ck

    def defer(self, gen):
        self.bgseq += 1
        self.deferred.append((self.bgseq, gen))

    def inject(self):
        while self.deferred:
            try:
                next(self.deferred[0][1])
                return
            except StopIteration:
                self.deferred.pop(0)

    def flush(self, older_than=None):
        while self.deferred and (older_than is None or self.deferred[0][0] <= older_than):
            for _ in self.deferred[0][1]:
                pass
            self.deferred.pop(0)

    def finish_head_softmax(self, ob, obk, h, W):
        r, rk = self.wk()
        self.recip(r[64:65, :W], ob[64:65, :W], [obk], [rk])
        yield
        bB, bBk = self.bcast_mm(r, rk, W)
        yield
        c, ck = self.wk()
        self.cp("dve", c[0:64, :W], bB[0:64, :W], [bBk], [ck])
        yield
        self.tt("dve", self.yTh[0:64, h, :W], ob[0:64, :W], c[0:64, :W], ALU.mult, [obk, ck], [("yTh", h)])

    def bcast_mm(self, row_tile, row_key, W):
        bB, bBk = self.bank("G")
        self.mm(bB[0:64, :W], self.ones_f[64:65, 0:64], row_tile[64:65, :W], True, True, [row_key, "cstf"], [bBk],
                tile_position=(64, 0))
        return bB, bBk

    def outproj_heads_gen(self, l, bi, wo_key):
        s, W = BLKS[bi]
        j = 1 if bi == 0 else 0
        for d in range(KC):
            bk, bkey = self.bank("G")
            for h in range(4):
                self.mm(bk[:, :W], self.WO[0:64, h, d * 128:(d + 1) * 128], self.yTh[0:64, h, :W], h == 0, h == 3,
                        [wo_key, ("yTh", h)], [bkey])
            yield
            self.stt(self.hT[:, d, s:s + W], bk[:, :W], self.vec[:, 2, d, j:j + 1], self.hT[:, d, s:s + W],
                     ALU.mult, ALU.add, [bkey, ("vec", 2), ("hT", d, bi)], [("hT", d, bi)])

    def outproj_heads(self, l, bi, wo_key):
        s, W = BLKS[bi]
        j = 1 if bi == 0 else 0
        for d in range(KC):
            bk, bkey = self.bank("G")
            for h in range(4):
                self.mm(bk[:, :W], self.WO[0:64, h, d * 128:(d + 1) * 128], self.yTh[0:64, h, :W], h == 0, h == 3,
                        [wo_key, ("yTh", h)], [bkey])
            self.stt(self.hT[:, d, s:s + W], bk[:, :W], self.vec[:, 2, d, j:j + 1], self.hT[:, d, s:s + W],
                     ALU.mult, ALU.add, [bkey, ("vec", 2), ("hT", d, bi)], [("hT", d, bi)])

    def load_wo_heads(self, l, row0):
        key = ("WO",)
        self.dma("pool", self.WO[0:64, :, :], self.w_out[l, row0:row0 + 256, :].rearrange("(h p) n -> p h n", p=64),
                 (), [key], "WO")
        return key

    def finish_block(self, l, bi, nsub, wo_key):
        s, W = BLKS[bi]
        j = 1 if bi == 0 else 0
        for ft in range(2):
            bk, bkey = self.bank("G")
            bb = bk[:].bitcast(BF16)
            for sub in range(nsub):
                self.tr(bb[:, sub * 128:(sub + 1) * 128], self.ytok[:, sub, ft * 128:(ft + 1) * 128], self.idb,
                        ["ytok", "idb"], [bkey])
            self.cp("act" if ft == 0 else "dve", self.yT[:, ft, :W], bb[:, :W], [bkey], [("yT", ft)])
        self.outproj(l, bi, wo_key, 2)

    def outproj(self, l, bi, wo_key, gate):
        s, W = BLKS[bi]
        j = 1 if bi == 0 else 0
        for d in range(KC):
            bk, bkey = self.bank("G")
            for kt in range(2):
                self.mm(bk[:, :W], self.WO[:, kt, d * 128:(d + 1) * 128], self.yT[:, kt, :W], kt == 0, kt == 1,
                        [wo_key, ("yT", kt)], [bkey])
            self.stt(self.hT[:, d, s:s + W], bk[:, :W], self.vec[:, gate, d, j:j + 1], self.hT[:, d, s:s + W],
                     ALU.mult, ALU.add, [bkey, ("vec", gate), ("hT", d, bi)], [("hT", d, bi)])

    def load_wo(self, l, row0):
        key = ("WO",)
        self.dma("pool", self.WO[:, 0:2, :], self.w_out[l, row0:row0 + 256, :].rearrange("(kt p) n -> p kt n", p=128),
                 (), [key], "WO")
        return key

    def qk_norm_rope(self, l, bA, bAk, bB, bBk, W, seq, gcol, rt, rtk, out_ap, out_key, outs=None):
        if outs is None:
            outs = [(slice(0, 128), out_ap, out_key)]
        sq, sqk = self.wbt()
        self.act(sq[:, :W], bA[:, :W], AF.Square, [bAk], [sqk])
        bC, bCk = self.bank("G")
        self.mm(bC[:, :W], self.bd64, sq[:, :W], True, True, [sqk, "bd64"], [bCk])
        rs, rsk = self.rstd_from_bank(bC, bCk, 128, W, 64)
        g = self.pp[:, l, gcol:gcol + 1]
        gs = self.pp[:, l, gcol + 1:gcol + 2]
        if not seq:
            for rows, oap, okey in outs:
                self.stt(oap, bA[rows, :W], g[rows, :], rs[rows, :W], ALU.mult, ALU.mult, [bAk, rsk, "pp"], [okey])
            return
        a, ak = self.wk()
        self.stt(a[:, :W], bA[:, :W], g, rt[:, 0, :W], ALU.mult, ALU.mult, [bAk, rtk, "pp"], [ak])
        b, bk_ = self.wk()
        self.stt(b[:, :W], bB[:, :W], gs, rt[:, 1, :W], ALU.mult, ALU.mult, [bBk, rtk, "pp"], [bk_])
        self.tt("pool", a[:, :W], a[:, :W], b[:, :W], ALU.add, [ak, bk_], [ak])
        for rows, oap, okey in outs:
            self.tt("pool", oap, a[rows, :W], rs[rows, :W], ALU.mult, [ak, rsk], [okey])

    def qk_norm_rope_gen(self, l, bA, bAk, bB, bBk, W, seq, gcol, rt, rtk, outs):
        sq, sqk = self.wbt()
        self.act(sq[:, :W], bA[:, :W], AF.Square, [bAk], [sqk])
        yield
        bC, bCk = self.bank("G")
        self.mm(bC[:, :W], self.bd64, sq[:, :W], True, True, [sqk, "bd64"], [bCk])
        yield
        sd, sdk = self.wk()
        self.act(sd[:, :W], bC[:, :W], AF.Ln, [bCk, "epsT"], [sdk], scale=1.0 / 64, bias=self.epsT[:, :])
        yield
        rs, rsk = self.wk()
        self.act(rs[:, :W], sd[:, :W], AF.Exp, [sdk], [rsk], scale=-0.5)
        yield
        g = self.pp[:, l, gcol:gcol + 1]
        gs = self.pp[:, l, gcol + 1:gcol + 2]
        if not seq:
            for rows, oap, okey in outs:
                self.stt(oap, bA[rows, :W], g[rows, :], rs[rows, :W], ALU.mult, ALU.mult, [bAk, rsk, "pp"], [okey])
            return
        a, ak = self.wk()
        self.stt(a[:, :W], bA[:, :W], g, rt[:, 0, :W], ALU.mult, ALU.mult, [bAk, rtk, "pp"], [ak])
        b, bk_ = self.wk()
        self.stt(b[:, :W], bB[:, :W], gs, rt[:, 1, :W], ALU.mult, ALU.mult, [bBk, rtk, "pp"], [bk_])
        yield
        self.tt("pool", a[:, :W], a[:, :W], b[:, :W], ALU.add, [ak, bk_], [ak])
        yield
        for rows, oap, okey in outs:
            self.tt("pool", oap, a[rows, :W], rs[rows, :W], ALU.mult, [ak, rsk], [okey])

    def rope32_gen(self, bA, bAk, bB, bBk, W, seq, rt, rtk, outs):
        if not seq:
            for rws, oap, okey in outs:
                self.cp("act", oap, bA[rws, :W], [bAk], [okey])
            return
        a, ak = self.wk()
        self.tt("dve", a[:, :W], bA[:, :W], rt[:, 0, :W], ALU.mult, [bAk, rtk], [ak])
        b, bk_ = self.wk()
        self.tt("dve", b[:, :W], bB[:, :W], rt[:, 1, :W], ALU.mult, [bBk, rtk], [bk_])
        yield
        for rws, oap, okey in outs:
            self.tt("pool", oap, a[rws, :W], b[rws, :W], ALU.add, [ak, bk_], [okey])

    def run_gen(self, g):
        for _ in g:
            pass

    def gqa(self, l):
        P = self.P
        P.barrier()
        kT = self.ar_bf(0, [128, T])
        V = self.ar_bf(T, [128, NCH, 2, 65])
        Vk = ("@", "gqaV", l)
        self.memset("pool", V[:, :, :, 64:65], 1.0, [Vk])
        wkv, wkvk = self.WS[0], ("WS", 0)
        self.load_w(wkv[:], self.w_in[l, :, GQA_OFF + 256:GQA_OFF + 512], wkvk)
        wsw, wswk = self.WS[1], ("WS", 1)
        src = wkv[:, :, 0:128].rearrange("p k (h a d) -> p k h a d", h=2, a=2)
        dst = wsw[:, :, 0:128].rearrange("p k (h a d) -> p k h a d", h=2, a=2)
        self.cp("pool", dst[:, :, :, 0, :], src[:, :, :, 1, :], [wkvk], [wswk])
        self.cp("pool", dst[:, :, :, 1, :], src[:, :, :, 0, :], [wkvk], [wswk])
        wo_key = self.load_wo_heads(l, 512)
        for bi in range(5):
            s, W = BLKS[bi]
            seq = bi > 0
            bA, bAk = self.bank("G")
            self.proj(bA[:, :W], bAk, lambda kc: wkv[:, kc, 0:128], [wkvk], s, W)
            bB = bBk = rt = rtk = None
            if seq:
                bB, bBk = self.bank("G")
                self.proj(bB[:, :W], bBk, lambda kc: wsw[:, kc, 0:128], [wswk], s, W)
                rt, rtk = self.load_rope(1, s, W)
            self.qk_norm_rope(l, bA, bAk, bB, bBk, W, seq, 76, rt, rtk, kT[:, s:s + W], ("@", "gqak", l, bi))
            nchb = W // 128
            bV, bVk = self.bank("G")
            for i in range(nchb):
                ch = s // 128 + i
                for kc in range(KC):
                    self.mm(bV[:, i * 128:(i + 1) * 128], self.xnT[:, kc, ch * 128:(ch + 1) * 128], wkv[:, kc, 128:256],
                            i == 0 and kc == 0, kc == KC - 1, [("xn", kc, bi), wkvk], [bVk], skip_group_check=True)
            self.cp("act", V[:, s // 128:s // 128 + nchb, :, 0:64],
                    bV[:, :W].rearrange("p (c h d) -> p c h d", c=nchb, h=2), [bVk], [Vk])
        wq, wqk = self.WQ, ("WQ",)
        self.load_w(wq[:, :, 0:256], self.w_in[l, :, GQA_OFF:GQA_OFF + 256], wqk)
        wr, wrk = self.WS[0], ("WS", 0)
        wrs, wrsk = self.WS[1], ("WS", 1)
        srcq = wq[:, :, 0:256].rearrange("p k (h a d) -> p k h a d", h=4, a=2)
        for tile, heads in ((0, (0, 2)), (1, (1, 3))):
            for pos, h in enumerate(heads):
                o0 = tile * 128 + pos * 64
                self.cp("pool", wr[:, :, o0:o0 + 64], wq[:, :, h * 64:(h + 1) * 64], [wqk], [wrk])
                self.cp("pool", wrs[:, :, o0:o0 + 32], srcq[:, :, h, 1, :], [wqk], [wrsk])
                self.cp("pool", wrs[:, :, o0 + 32:o0 + 64], srcq[:, :, h, 0, :], [wqk], [wrsk])
        qms = [self.ar_bf(T + NCH * 130 + i * 2048, [128, 4, 512]) for i in range(2)]
        qmks = [("@", "gqaqm", l, i) for i in range(2)]
        for i in range(2):
            self.memset("pool", qms[i], 0.0, [qmks[i]])

        def qproj_gen(bi, buf):
            s, W = BLKS[bi]
            seq = bi > 0
            qm, qmk = qms[buf], qmks[buf]
            rt = rtk = None
            if seq:
                rt, rtk = self.load_rope(1, s, W)
            for qt in range(2):
                bA, bAk = self.bank("G")
                self.proj(bA[:, :W], bAk, lambda kc: wr[:, kc, qt * 128:(qt + 1) * 128], [wrk], s, W)
                yield
                bB = bBk = None
                if seq:
                    bB, bBk = self.bank("G")
                    self.proj(bB[:, :W], bBk, lambda kc: wrs[:, kc, qt * 128:(qt + 1) * 128], [wrsk], s, W)
                    yield
                yield from self.qk_norm_rope_gen(l, bA, bAk, bB, bBk, W, seq, 74, rt, rtk,
                                                 [(slice(0, 64), qm[0:64, qt, :W], qmk),
                                                  (slice(64, 128), qm[64:128, qt + 2, :W], qmk)])
                yield

        blocks = self.qblocks()
        self.run_gen(qproj_gen(blocks[0], 0))
        for idx, bi in enumerate(blocks):
            s, W = BLKS[bi]
            seq = bi > 0
            nkc = NCH if seq else 2
            buf = idx % 2
            qm, qmk = qms[buf], qmks[buf]
            kkeys = [("@", "gqak", l, b) for b in range(5 if seq else 1)]
            for h in range(4):
                if h == 1 and idx + 1 < len(blocks):
                    self.defer(qproj_gen(blocks[idx + 1], 1 - buf))
                ob, obk = self.bank("O")
                self.attention(qm[:, h, :W], [qmk],
                               lambda kc: kT[:, kc * 128:(kc + 1) * 128], kkeys,
                               lambda kc: V[:, kc, h // 2, :], [Vk], 0, W, nkc, 0.125, ob, obk)
                self.defer(self.finish_head_softmax(ob, obk, h, W))
            self.defer(self.outproj_heads_gen(l, bi, wo_key))
        self.flush()

    def rope32(self, bA, bAk, bB, bBk, W, seq, rt, rtk, out_ap, out_key, rows=slice(0, 128), outs=None):
        if outs is None:
            outs = [(rows, out_ap, out_key)]
        if not seq:
            for rws, oap, okey in outs:
                self.cp("act", oap, bA[rws, :W], [bAk], [okey])
            return
        a, ak = self.wk()
        self.tt("dve", a[rows, :W], bA[rows, :W], rt[rows, 0, :W], ALU.mult, [bAk, rtk], [ak])
        b, bk_ = self.wk()
        self.tt("dve", b[rows, :W], bB[rows, :W], rt[rows, 1, :W], ALU.mult, [bBk, rtk], [bk_])
        for rws, oap, okey in outs:
            self.tt("pool", oap, a[rws, :W], b[rws, :W], ALU.add, [ak, bk_], [okey])

    def swap16(self, dst, src, rkey, wkey, ncols):
        s5 = src.rearrange("p k (b a d) -> p k b a d", a=2, d=16)
        d5 = dst.rearrange("p k (b a d) -> p k b a d", a=2, d=16)
        self.cp("pool", d5[:, :, :, 0, :], s5[:, :, :, 1, :], [rkey], [wkey])
        self.cp("pool", d5[:, :, :, 1, :], s5[:, :, :, 0, :], [rkey], [wkey])

    def diff(self, l):
        P = self.P
        P.barrier()
        lam_init = 0.8 - 0.6 * math.exp(-0.3 * l)
        kT = self.ar_bf(0, [128, 2, T])
        V = self.ar_bf(2 * T, [128, NCH, 4, 65])
        Vk = ("@", "diffV", l)
        self.memset("pool", V[:, :, :, 64:65], 1.0, [Vk])
        lp = self.pb[:, 16:144].rearrange("p (a d) -> p a d", a=4)
        sm = self.sm
        t1, t1k = self.wk()
        self.tt("dve", t1[:, 0:32], lp[:, 0, :], lp[:, 1, :], ALU.mult, ["pb"], [t1k])
        self.tt("dve", t1[:, 32:64], lp[:, 2, :], lp[:, 3, :], ALU.mult, ["pb"], [t1k])
        self.P.op("dve", lambda e: e.tensor_reduce(out=sm[:, 0:2], in_=t1[:, 0:64].rearrange("p (a d) -> p a d", a=2),
                                                    axis=AX.X, op=ALU.add), [t1k], ["sm_lam"])
        self.act(sm[:, 2:4], sm[:, 0:2], AF.Exp, ["sm_lam"], ["sm_lam2"])
        self.tt("dve", sm[:, 4:5], sm[:, 3:4], sm[:, 2:3], ALU.subtract, ["sm_lam2"], ["sm_lam3"])
        self.ts("dve", sm[:, 5:6], sm[:, 4:5], -lam_init, None, ALU.add, None, ["sm_lam3"], ["neglam"])
        neglam = sm[:, 5:6]
        gdp = sm[:, 6:7]
        self.ts("dve", gdp, self.pp[:, l, 81:82], 1.0 - lam_init, None, ALU.mult, None, ["pp"], ["gdp"])
        wk_, wkk = self.WS[0], ("WS", 0)
        self.load_w(wk_[:], self.w_in[l, :, DIFF_OFF + 256:DIFF_OFF + 512], wkk)
        wsw, wswk = self.WS[1], ("WS", 1)
        self.swap16(wsw[:], wk_[:], wkk, wswk, 256)
        wv, wvk = self.WQ, ("WQ",)
        self.load_w(wv[:, :, 0:256], self.w_in[l, :, DIFF_OFF + 512:DIFF_OFF + 768], wvk)
        wo_key = self.load_wo_heads(l, 256)
        for bi in range(5):
            s, W = BLKS[bi]
            seq = bi > 0
            rt = rtk = None
            if seq:
                rt, rtk = self.load_rope(0, s, W)
            for kt in range(2):
                bA, bAk = self.bank("G")
                self.proj(bA[:, :W], bAk, lambda kc: wk_[:, kc, kt * 128:(kt + 1) * 128], [wkk], s, W)
                bB = bBk = None
                if seq:
                    bB, bBk = self.bank("G")
                    self.proj(bB[:, :W], bBk, lambda kc: wsw[:, kc, kt * 128:(kt + 1) * 128], [wswk], s, W)
                self.rope32(bA, bAk, bB, bBk, W, seq, rt, rtk, kT[:, kt, s:s + W], ("@", "diffk", l, kt, bi))
            nchb = W // 128
            for i2 in range(0, nchb, 2):
                bV, bVk = self.bank("G")
                for i in range(2):
                    ch = s // 128 + i2 + i
                    for kc in range(KC):
                        self.mm(bV[:, i * 256:(i + 1) * 256], self.xnT[:, kc, ch * 128:(ch + 1) * 128], wv[:, kc, 0:256],
                                i == 0 and kc == 0, kc == KC - 1, [("xn", kc, bi), wvk], [bVk], skip_group_check=True)
                c0 = s // 128 + i2
                self.cp("act", V[:, c0:c0 + 2, :, 0:64], bV[:].rearrange("p (c h d) -> p c h d", c=2, h=4), [bVk], [Vk])
        wq, wqk = self.WQ, ("WQ",)
        self.load_w(wq[:, :, 0:256], self.w_in[l, :, DIFF_OFF:DIFF_OFF + 256], wqk)
        self.swap16(wq[:, :, 256:512], wq[:, :, 0:256], wqk, wqk, 256)
        qms = [self.ar_bf(2 * T + NCH * 260 + i * 4096, [128, 8, 512]) for i in range(2)]
        qmks = [("@", "diffqm", l, i) for i in range(2)]
        for i in range(2):
            self.memset("pool", qms[i], 0.0, [qmks[i]])

        def qproj_gen(bi, buf):
            s, W = BLKS[bi]
            seq = bi > 0
            qm, qmk = qms[buf], qmks[buf]
            rt = rtk = None
            if seq:
                rt, rtk = self.load_rope(0, s, W)
            for qt in range(2):
                bA, bAk = self.bank("G")
                self.proj(bA[:, :W], bAk, lambda kc: wq[:, kc, qt * 128:(qt + 1) * 128], [wqk], s, W)
                yield
                bB = bBk = None
                if seq:
                    bB, bBk = self.bank("G")
                    self.proj(bB[:, :W], bBk, lambda kc: wq[:, kc, 256 + qt * 128:256 + (qt + 1) * 128], [wqk], s, W)
                    yield
                yield from self.rope32_gen(bA, bAk, bB, bBk, W, seq, rt, rtk,
                                           [(slice(32 * j, 32 * j + 32), qm[32 * j:32 * j + 32, 4 * qt + j, :W], qmk)
                                            for j in range(4)])
                yield

        def diff_finish(obs, h, W):
            (o1, o1k), (o2, o2k) = obs
            r, rk = self.wk()
            self.recip(r[64:65, :W], o1[64:65, :W], [o1k], [rk])
            r2, r2k = self.wk()
            self.recip(r2[64:65, :W], o2[64:65, :W], [o2k], [r2k])
            self.ts("dve", r2[64:65, :W], r2[64:65, :W], neglam[64:65, :], None, ALU.mult, None, [r2k, "neglam"], [r2k])
            yield
            b1, b1k = self.bcast_mm(r, rk, W)
            b2, b2k = self.bcast_mm(r2, r2k, W)
            yield
            c1, c1k = self.wk()
            self.cp("dve", c1[0:64, :W], b1[0:64, :W], [b1k], [c1k])
            c2, c2k = self.wk()
            self.cp("dve", c2[0:64, :W], b2[0:64, :W], [b2k], [c2k])
            yield
            self.tt("dve", c1[0:64, :W], o1[0:64, :W], c1[0:64, :W], ALU.mult, [o1k, c1k], [c1k])
            self.tt("dve", c2[0:64, :W], o2[0:64, :W], c2[0:64, :W], ALU.mult, [o2k, c2k], [c2k])
            yield
            self.tt("pool", c1[0:64, :W], c1[0:64, :W], c2[0:64, :W], ALU.add, [c1k, c2k], [c1k])
            yield
            sqb, sqbk = self.wbt()
            self.tt("pool", sqb[0:64, :W], c1[0:64, :W], c1[0:64, :W], ALU.mult, [c1k], [sqbk])
            yield
            bN, bNk = self.bank("G")
            self.mm(bN[0:64, :W], self.ones_b[0:64, 0:64], sqb[0:64, :W], True, True, [sqbk, "onesb"], [bNk])
            yield
            sd, sdk = self.wkL(0)
            self.act(sd[0:64, :W], bN[0:64, :W], AF.Ln, [bNk, "epsT"], [sdk], scale=1.0 / 64, bias=self.epsT[0:64, :])
            yield
            self.act(sd[0:64, :W], sd[0:64, :W], AF.Exp, [sdk], [sdk], scale=-0.5)
            yield
            self.stt(self.yTh[0:64, h, :W], c1[0:64, :W], gdp[0:64, :], sd[0:64, :W], ALU.mult, ALU.mult,
                     [c1k, sdk, "gdp"], [("yTh", h)])
        blocks = self.qblocks()
        self.run_gen(qproj_gen(blocks[0], 0))
        for idx, bi in enumerate(blocks):
            s, W = BLKS[bi]
            seq = bi > 0
            nkc = NCH if seq else 2
            buf = idx % 2
            qm, qmk = qms[buf], qmks[buf]
            for h in range(4):
                if h == 1 and idx + 1 < len(blocks):
                    self.defer(qproj_gen(blocks[idx + 1], 1 - buf))
                tile = h // 2
                kkeys = [("@", "diffk", l, tile, b) for b in range(5 if seq else 1)]
                obs = []
                for m in range(2):
                    ob, obk = self.bank("O")
                    self.attention(qm[:, 2 * h + m, :W], [qmk],
                                   lambda kc: kT[:, tile, kc * 128:(kc + 1) * 128], kkeys,
                                   lambda kc: V[:, kc, h, :], [Vk], 0, W, nkc, 32 ** -0.5, ob, obk)
                    obs.append((ob, obk))
                self.defer(diff_finish(obs, h, W))
            self.defer(self.outproj_heads_gen(l, bi, wo_key))
        self.flush()


    def mla(self, l):
        P = self.P
        P.barrier()
        kTh = self.ar_bf(0, [128, 4, T])
        V = self.ar_bf(4 * T, [128, NCH, 4, 65])
        Vk = ("@", "mlaV", l)
        self.memset("pool", V[:, :, :, 64:65], 1.0, [Vk])
        off = 4 * T + NCH * 260
        wukv = self.ar_bf(off, [128, 512]); off += 512
        wuq = self.ar_bf(off, [128, 2, 384]); off += 768
        wuqs = self.ar_bf(off, [128, 2, 128]); off += 256
        qTh = self.ar_bf(off, [128, 4, 512]); off += 2048
        cqn = self.ar_bf(off, [128, 2, 512]); off += 1024
        wukvk, wuqk, wuqsk = ("@", "wukv", l), ("@", "wuq", l), ("@", "wuqs", l)
        self.dma("pool", wukv, self.w_ukv[l], (), [wukvk], "wukv")
        self.dma("pool", wuq[:, 0, :], self.w_uq[l, 0:128, :], (), [wuqk], "wuq")
        self.dma("pool", wuq[0:64, 1, :], self.w_uq[l, 128:192, :], (), [wuqk], "wuq")
        for kt in range(2):
            rows = slice(0, 128) if kt == 0 else slice(0, 64)
            for h in range(4):
                c0 = h * 96 + 64
                self.cp("pool", wuqs[rows, kt, h * 32:h * 32 + 16], wuq[rows, kt, c0 + 16:c0 + 32], [wuqk], [wuqsk])
                self.cp("pool", wuqs[rows, kt, h * 32 + 16:h * 32 + 32], wuq[rows, kt, c0:c0 + 16], [wuqk], [wuqsk])
        wkv, wkvk = self.WS[0], ("WS", 0)
        self.load_w(wkv[:, :, 0:160], self.w_in[l, :, MLA_OFF + 192:MLA_OFF + 352], wkvk)
        self.cp("pool", wkv[:, :, 160:176], wkv[:, :, 144:160], [wkvk], [wkvk])
        self.cp("pool", wkv[:, :, 176:192], wkv[:, :, 128:144], [wkvk], [wkvk])
        wo_key = self.load_wo_heads(l, 768)
        gkv = self.pp[:, l, 80:81]
        for bi in range(5):
            s, W = BLKS[bi]
            seq = bi > 0
            bA, bAk = self.bank("G")
            self.proj(bA[:, :W], bAk, lambda kc: wkv[:, kc, 0:128], [wkvk], s, W)
            sq, sqk = self.wbt()
            self.act(sq[:, :W], bA[:, :W], AF.Square, [bAk], [sqk])
            bC, bCk = self.bank("G")
            self.mm(bC[:, :W], self.ones_b, sq[:, :W], True, True, [sqk, "onesb"], [bCk])
            rs, rsk = self.rstd_from_bank(bC, bCk, 128, W, 128)
            ckvn, ckvnk = self.wbt()
            self.stt(ckvn[:, :W], bA[:, :W], gkv, rs[:, :W], ALU.mult, ALU.mult, [bAk, rsk, "pp"], [ckvnk])
            for h in range(4):
                bK, bKk = self.bank("G")
                self.mm(bK[0:64, :W], wukv[:, h * 128:h * 128 + 64], ckvn[:, :W], True, True, [wukvk, ckvnk], [bKk])
                self.cp("act" if h % 2 == 0 else "dve", kTh[0:64, h, s:s + W], bK[0:64, :W], [bKk], [("@", "mlak", l, h, bi)])
            bR, bRk = self.bank("G")
            self.proj(bR[64:96, :W], bRk, lambda kc: wkv[:, kc, 128:160], [wkvk], s, W, tile_position=(0, 64))
            bR2 = bR2k = rt = rtk = None
            if seq:
                bR2, bR2k = self.bank("G")
                self.proj(bR2[64:96, :W], bR2k, lambda kc: wkv[:, kc, 160:192], [wkvk], s, W,
                          tile_position=(0, 64))
                rt, rtk = self.load_rope(0, s, W)
            self.rope32(bR, bRk, bR2, bR2k, W, seq, rt, rtk, kTh[64:96, 0, s:s + W], ("@", "mlakr", l, 0, bi),
                        rows=slice(64, 96))
            for h in range(1, 4):
                self.cp("pool", kTh[64:96, h, s:s + W], kTh[64:96, 0, s:s + W], [("@", "mlakr", l, 0, bi)],
                        [("@", "mlakr", l, h, bi)])
            nchb = W // 128
            for i2 in range(0, nchb, 2):
                bV, bVk = self.bank("G")
                for i in range(2):
                    cl = i2 + i
                    self.mm(bV[:, i * 256:(i + 1) * 256].rearrange("p (h d) -> p h d", h=4),
                            ckvn[:, cl * 128:(cl + 1) * 128],
                            wukv.rearrange("p (h a d) -> p h a d", h=4, a=2)[:, :, 1, :],
                            i == 0, True, [ckvnk, wukvk], [bVk], skip_group_check=True)
                c0 = s // 128 + i2
                self.cp("act", V[:, c0:c0 + 2, :, 0:64], bV[:].rearrange("p (c h d) -> p c h d", c=2, h=4), [bVk], [Vk])
        wq, wqk = self.WQ, ("WQ",)
        self.load_w(wq[:, :, 0:192], self.w_in[l, :, MLA_OFF:MLA_OFF + 192], wqk)
        for bi in self.qblocks():
            s, W = BLKS[bi]
            seq = bi > 0
            nkc = NCH if seq else 2
            nsub = W // 128
            bA, bAk = self.bank("G")
            self.proj(bA[:, :W], bAk, lambda kc: wq[:, kc, 0:128], [wqk], s, W)
            bB, bBk = self.bank("G")
            self.proj(bB[0:64, :W], bBk, lambda kc: wq[:, kc, 128:192], [wqk], s, W)
            sqA, sqAk = self.wbt()
            self.act(sqA[:, :W], bA[:, :W], AF.Square, [bAk], [sqAk])
            sqB, sqBk = self.wbt()
            self.act(sqB[0:64, :W], bB[0:64, :W], AF.Square, [bBk], [sqBk])
            bC, bCk = self.bank("G")
            self.mm(bC[:, :W], self.ones_b, sqA[:, :W], True, False, [sqAk, "onesb"], [bCk])
            self.mm(bC[:, :W], self.ones_b[0:64, :], sqB[0:64, :W], False, True, [sqBk, "onesb"], [bCk])
            rs, rsk = self.rstd_from_bank(bC, bCk, 128, W, 192)
            cqk = ("@", "cqn", l)
            self.stt(cqn[:, 0, :W], bA[:, :W], self.pp[:, l, 78:79], rs[:, :W], ALU.mult, ALU.mult, [bAk, rsk, "pp"], [cqk])
            self.stt(cqn[0:64, 1, :W], bB[0:64, :W], self.pp[0:64, l, 79:80], rs[0:64, :W], ALU.mult, ALU.mult,
                     [bBk, rsk, "pp"], [cqk])
            rt = rtk = None
            if seq:
                rt, rtk = self.load_rope(0, s, W)
            for h in range(4):
                bQ, bQk = self.bank("G")
                self.mm(bQ[0:96, :W], wuq[:, 0, h * 96:(h + 1) * 96], cqn[:, 0, :W], True, False, [wuqk, cqk], [bQk])
                self.mm(bQ[0:96, :W], wuq[0:64, 1, h * 96:(h + 1) * 96], cqn[0:64, 1, :W], False, True, [wuqk, cqk], [bQk])
                bQ2 = bQ2k = None
                if seq:
                    bQ2, bQ2k = self.bank("G")
                    self.mm(bQ2[64:96, :W], wuqs[:, 0, h * 32:(h + 1) * 32], cqn[:, 0, :W], True, False, [wuqsk, cqk], [bQ2k],
                            tile_position=(0, 64))
                    self.mm(bQ2[64:96, :W], wuqs[0:64, 1, h * 32:(h + 1) * 32], cqn[0:64, 1, :W], False, True,
                            [wuqsk, cqk], [bQ2k], tile_position=(0, 64))
                qk_ = ("@", "mlaq", l, h)
                self.cp("act", qTh[0:64, h, :W], bQ[0:64, :W], [bQk], [qk_])
                self.rope32(bQ, bQk, bQ2, bQ2k, W, seq, rt, rtk, qTh[64:96, h, :W], ("@", "mlaqr", l, h), rows=slice(64, 96))
            for h in range(4):
                kkeys = [("@", "mlak", l, h, b) for b in range(5 if seq else 1)] + \
                        [("@", "mlakr", l, h, b) for b in range(5 if seq else 1)]
                ob, obk = self.bank("O")
                self.attention(qTh[0:96, h, :W], [("@", "mlaq", l, h), ("@", "mlaqr", l, h)],
                               lambda kc: kTh[0:96, h, kc * 128:(kc + 1) * 128], kkeys,
                               lambda kc: V[:, kc, h, :], [Vk], 0, W, nkc, 96 ** -0.5, ob, obk)
                self.defer(self.finish_head_softmax(ob, obk, h, W))
            self.defer(self.outproj_heads_gen(l, bi, wo_key))
        self.flush()

    def ssd(self, l):
        P = self.P
        P.barrier()
        off = 0
        XBC = self.ar_bf(off, [128, 4, T]); off += 4 * T
        Bt = self.ar_bf(off, [128, NCH, 128]); off += NCH * 128
        HT = self.ar_bf(off, [128, NCH, 4, 64]); off += NCH * 256
        dI = self.ar_bf(off, [128, 4, 128]); off += 512
        dtt = self.ar_f32(off, [128, NCH, 8]); off += NCH * 16
        at = self.ar_f32(off, [128, NCH, 8]); off += NCH * 16
        cst_ = self.ar_f32(off, [128, NCH, 8]); off += NCH * 16
        tott = self.ar_f32(off, [128, NCH, 8]); off += NCH * 16
        dtw = self.ar_f32(off, [128, NCH, 8]); off += NCH * 16
        decp = self.ar_f32(off, [128, NCH, 2, 2]); off += NCH * 8
        Hc = self.ar_f32(off, [128, 4, 64]); off += 512
        cw = self.ar_f32(off, [128, 3, 128]); off += 768
        aneg = self.ar_f32(off, [128, 8]); off += 16
        dsb = self.ar_f32(off, [128, 4]); off += 8
        assert off <= self.ARENA, off
        MT = self.PTall[:, 0:2, :].rearrange("p a (j n) -> p (a j) n", j=4)
        CD = self.PTall[:, 2:4, :].rearrange("p a (j n) -> p (a j) n", j=4)
        MTk = [("PT", 0), ("PT", 1)]
        CDk = [("PT", 2), ("PT", 3)]
        k = lambda *a: ("@", "ssd", l) + a

        def xs_tok(ch):
            return XBC[:, 0:2, ch * 128:(ch + 1) * 128]

        def xs_tok_head(ch, h):
            return XBC[:, h // 2, ch * 128 + (h % 2) * 64:ch * 128 + (h % 2) * 64 + 64]

        self.act(aneg, self.pb[:, 8:16], AF.Exp, ["pb"], [k("aneg")])
        self.ts("dve", aneg, aneg, -1.0, None, ALU.mult, None, [k("aneg")], [k("aneg")])
        self.tt("dve", dsb, self.pb[:, 208:212], self.pb[:, 212:216], ALU.add, ["pb"], [k("dsb")])
        for h in range(4):
            self.ts("dve", dI[:, h, :], self.idf, dsb[:, h:h + 1], None, ALU.mult, None, ["cstf", k("dsb")], [k("dI")])
        wo_key = self.load_wo(l, 0)
        WB = self.WQ.rearrange("p k (a n) -> p k a n", a=4)
        for ti in range(4):
            ws, wsk = self.WS[ti % 2], ("WS", ti % 2)
            c0 = SSD_OFF + 256 + ti * 128
            self.load_w(ws[:, :, 0:128], self.w_in[l, :, c0:c0 + 128], wsk)
            self.dma("sp", cw, self.cwd[l, :, ti * 128:(ti + 1) * 128].partition_broadcast(128), (), [k("cw")], "cw")
            wbk = ("WQ",)
            for kk in range(3):
                self.tt("pool" if kk < 2 else "dve", WB[:, :, kk, :], ws[:, :, 0:128],
                        cw[:, kk, :].unsqueeze(1).broadcast_to([128, KC, 128]), ALU.mult, [wsk, k("cw")], [wbk])
            for bi in range(5):
                s, W = BLKS[bi]
                seg0, seg1 = (0, 256) if bi == 0 else (256, T)
                bk, bkey = self.bank("G")
                first = True
                for kk in (1, 0, 2):
                    lo, hi = 0, W
                    if kk == 0 and s == seg0:
                        lo = 1
                    if kk == 2 and s + W == seg1:
                        hi = W - 1
                    t0 = s + lo + kk - 1
                    n = hi - lo
                    bset = sorted({self.blk_of(t0), self.blk_of(t0 + n - 1)})
                    for kc in range(KC):
                        self.mm(bk[:, lo:hi], WB[:, kc, kk, :], self.xnT[:, kc, t0:t0 + n], first,
                                kk == 2 and kc == KC - 1, [wbk] + [("xn", kc, b) for b in bset], [bkey],
                                skip_group_check=True)
                        first = False
                self.act(XBC[:, ti, s:s + W], bk[:, :W], AF.Silu, [bkey, "pp"], [k("xbc", ti, bi)],
                         bias=self.pp[:, l, 64 + ti:65 + ti])
        self.dump("xbc_%d" % l, XBC[:, :, :], [k("xbc", ti, b) for ti in range(4) for b in range(5)])
        if self.stop == (l, "ssd_xbc"):
            return
        wdt, wdtk = self.WS[0], ("WS", 0)
        self.load_w(wdt[:, :, 0:8], self.w_in[l, :, SSD_OFF + 768:SSD_OFF + 776], wdtk)
        bk, bkey = self.bank("G")
        for ch in range(NCH):
            bi = self.blk_of(ch * 128)
            for kc in range(KC):
                self.mm(bk[:, ch * 8:(ch + 1) * 8], self.xnT[:, kc, ch * 128:(ch + 1) * 128], wdt[:, kc, 0:8],
                        ch == 0 and kc == 0, kc == KC - 1, [("xn", kc, bi), wdtk], [bkey], skip_group_check=True)
        b3 = bk[:, 0:NCH * 8].rearrange("p (c j) -> p c j", j=8)
        self.tt("dve", dtt, b3, self.pb[:, 0:8].unsqueeze(1).broadcast_to([128, NCH, 8]), ALU.add, [bkey, "pb"], [k("dt")])
        self.act(dtt, dtt, AF.Exp, [k("dt")], [k("dt")])
        self.act(dtt, dtt, AF.Ln, [k("dt"), "oneT"], [k("dt")], bias=self.oneT[:, :])
        self.tt("dve", at, dtt, aneg.unsqueeze(1).broadcast_to([128, NCH, 8]), ALU.mult, [k("dt"), k("aneg")], [k("at")])
        self.dump("dt_%d" % l, dtt, [k("dt")])
        if self.stop == (l, "ssd_dt"):
            return
        bk, bkey = self.bank("G")
        bk2, bkey2 = self.bank("G")
        c3 = bk[:, 0:NCH * 8].rearrange("p (c j) -> p c j", j=8)
        t3 = bk2[:, 0:NCH * 8].rearrange("p (c j) -> p c j", j=8)
        for ch in range(NCH):
            self.mm(c3[:, ch, 0:4], self.U, at[:, ch, 0:4], ch == 0, True, [k("at"), "cstf"], [bkey], skip_group_check=True)
            self.mm(c3[:, ch, 4:8], self.Lm, at[:, ch, 4:8], False, True, [k("at"), "cstf"], [bkey], skip_group_check=True)
            self.mm(t3[:, ch, :], self.ones_f, at[:, ch, :], ch == 0, True, [k("at"), "cstf"], [bkey2], skip_group_check=True)
        self.cp("dve", cst_, c3, [bkey], [k("cs")])
        self.cp("dve", tott, t3, [bkey2], [k("tot")])
        self.tt("dve", dtw, tott, cst_, ALU.subtract, [k("tot"), k("cs")], [k("dtw")])
        self.act(dtw, dtw, AF.Exp, [k("dtw")], [k("dtw")])
        self.tt("dve", dtw, dtw, dtt, ALU.mult, [k("dtw"), k("dt")], [k("dtw")])
        t4 = tott.rearrange("p c (d h) -> p c d h", d=2)
        self.act(decp[0:64], t4[0:64, :, :, 0:2], AF.Exp, [k("tot")], [k("decp")])
        self.act(decp[64:128], t4[64:128, :, :, 2:4], AF.Exp, [k("tot")], [k("decp")])
        self.dump("cs_%d" % l, cst_, [k("cs")])
        if self.stop == (l, "ssd_cs"):
            return
        for ch in range(NCH):
            bi = self.blk_of(ch * 128)
            bk, bkey = self.bank("G")
            bb = bk[:].bitcast(BF16)
            for ti in range(3):
                self.tr(bb[:, ti * 128:(ti + 1) * 128], XBC[:, ti, ch * 128:(ch + 1) * 128], self.idb,
                        [k("xbc", ti, bi), "idb"], [bkey])
            self.cp("act", xs_tok(ch), bb[:, 0:256].rearrange("p (a n) -> p a n", a=2), [bkey], [k("xst", ch)])
            self.cp("dve", Bt[:, ch, :], bb[:, 256:384], [bkey], [k("bt", ch)])
        self.dump("xst_%d" % l, XBC[:, 0:2, :], [k("xst", ch) for ch in range(NCH)])
        self.dump("bt_%d" % l, Bt, [k("bt", ch) for ch in range(NCH)])
        if self.stop == (l, "ssd_tr"):
            return
        self.memset("dve", Hc, 0.0, [k("Hc", 0), k("Hc", 1)])
        orders = [list(range(NCH)), [1, 0] + list(range(NCH - 1, 1, -1))]
        for d in range(2):
            for ch in orders[d]:
                xwt, xwk = self.wbt()
                xw = xwt[:, 0:256]
                self.tt("pool", xw.rearrange("p (a h e) -> p a h e", a=2, h=2),
                        xs_tok(ch).rearrange("p a (h e) -> p a h e", h=2),
                        dtw[:, ch, d * 4:(d + 1) * 4].rearrange("p (a h) -> p a h", a=2).unsqueeze(3).broadcast_to([128, 2, 2, 64]),
                        ALU.mult, [k("xst", ch), k("dtw")], [xwk])
                bk, bkey = self.bank("S")
                for g in range(2):
                    self.mm(bk[64 * g:64 * g + 64, 0:128], Bt[:, ch, 64 * g:64 * g + 64],
                            xw[:, g * 128:(g + 1) * 128], True, True, [k("bt", ch), xwk], [bkey],
                            tile_position=(0, 64 * g), skip_group_check=True)
                hc = Hc[:, 2 * d:2 * d + 2, :]
                self.cp("act", HT[:, ch, 2 * d:2 * d + 2, :], hc, [k("Hc", d)], [k("HT", ch, d)])
                self.tt("dve", hc, hc, decp[:, ch, d, :].unsqueeze(2).broadcast_to([128, 2, 64]), ALU.mult,
                        [k("Hc", d), k("decp")], [k("Hc", d)])
                self.tt("dve", hc, hc, bk[:, 0:128].rearrange("p (h e) -> p h e", h=2), ALU.add, [k("Hc", d), bkey],
                        [k("Hc", d)])
        self.dump("HT_%d" % l, HT, [k("HT", ch, d) for ch in range(NCH) for d in range(2)])
        if self.stop == (l, "ssd_state"):
            return
        wz, wzk = self.WQ, ("WQ",)
        self.load_w(wz[:, :, 0:256], self.w_in[l, :, SSD_OFF:SSD_OFF + 256], wzk)
        ysub = getattr(self, "ysub", 9)
        for bi in self.qblocks()[:getattr(self, "yblk", 9)]:
            s, W = BLKS[bi]
            for half in range(W // 256):
                hs = s + half * 256
                yb, ybk = self.bank("O")
                for ci in range(2):
                    ch = hs // 128 + ci
                    tok = slice(ch * 128, (ch + 1) * 128)
                    bG, bGk = self.bank("G")
                    czt, czk = self.wbt()
                    cz = czt[:, 0:256].rearrange("p (g n) -> p g n", g=2)
                    self.memset("pool", czt[:, 0:256], 0.0, [czk])
                    self.cp("act", cz[0:64, 0, :], XBC[0:64, 3, tok], [k("xbc", 3, bi), czk], [czk])
                    self.cp("act", cz[64:128, 1, :], XBC[64:128, 3, tok], [k("xbc", 3, bi), czk], [czk])
                    self.mm(bG[:, 0:256], XBC[:, 2, tok], czt[:, 0:256], True, True, [k("xbc", 2, bi), czk], [bGk])
                    xdt, xdk = self.wbt()
                    abcs, bCs, dfs, es = {}, {}, {}, {}
                    for d in range(2):
                        abc, abck = self.wk()
                        self.cp("act", abc[:, :].rearrange("p (h n) -> p h n", h=4),
                                at[:, ch, d * 4:(d + 1) * 4].unsqueeze(2).broadcast_to([128, 4, 128]), [k("at")], [abck])
                        abcs[d] = (abc, abck)
                    for d in range(2):
                        abc, abck = abcs[d]
                        bC, bCk = self.bank("S")
                        for hh in range(4):
                            self.mm(bC[:, hh * 128:(hh + 1) * 128], abc[:, hh * 128:(hh + 1) * 128],
                                    self.U if d == 0 else self.Lm, hh == 0, True, [abck, "cstf"], [bCk], skip_group_check=True)
                        bCs[d] = (bC, bCk)
                    for d in range(2):
                        bC, bCk = bCs[d]
                        df, dfk = self.wk()
                        for hh in range(4):
                            jj = d * 4 + hh
                            self.stt(df[:, hh * 128:(hh + 1) * 128], bC[:, hh * 128:(hh + 1) * 128],
                                     cst_[:, ch, jj:jj + 1], self.nmf if d == 0 else self.nmb, ALU.subtract, ALU.add,
                                     [bCk, k("cs"), "cstf"], [dfk])
                        dfs[d] = (df, dfk)
                    for d in range(2):
                        df, dfk = dfs[d]
                        self.act(df[:, :], df[:, :], AF.Exp, [dfk], [dfk])
                    for d in range(2):
                        bC, bCk = bCs[d]
                        e, ek = self.wk()
                        self.act(e[:, :], bC[:, :], AF.Exp, [bCk], [ek])
                        es[d] = (e, ek)
                    for d in range(2):
                        df, dfk = dfs[d]
                        self.tt("dve", MT[:, d * 4:(d + 1) * 4, :].rearrange("p (g h) n -> p g h n", g=2),
                                df[:, :].rearrange("p (g h n) -> p g h n", g=2, h=2),
                                bG[:, 0:256].rearrange("p (g n) -> p g n", g=2).unsqueeze(2).broadcast_to([128, 2, 2, 128]),
                                ALU.mult, [dfk, bGk], [MTk[d]])
                    for d in range(2):
                        e, ek = es[d]
                        self.tt("dve", CD[:, d * 4:(d + 1) * 4, :], e[:, :].rearrange("p (h n) -> p h n", h=4),
                                XBC[:, 3, tok].unsqueeze(1).broadcast_to([128, 4, 128]), ALU.mult,
                                [ek, k("xbc", 3, bi)], [CDk[d]])
                        self.memset("pool", CD[64:128, d * 4:d * 4 + 2, :], 0.0, [CDk[d]])
                        self.memset("pool", CD[0:64, d * 4 + 2:d * 4 + 4, :], 0.0, [CDk[d]])
                    for d in range(2):
                        self.tt("dve", xdt[:, d * 256:(d + 1) * 256].rearrange("p (a h e) -> p a h e", a=2, h=2),
                                xs_tok(ch).rearrange("p a (h e) -> p a h e", h=2),
                                dtt[:, ch, d * 4:(d + 1) * 4].rearrange("p (a h) -> p a h", a=2).unsqueeze(3).broadcast_to([128, 2, 2, 64]),
                                ALU.mult, [k("xst", ch), k("dt")], [xdk])
                    if l == 0:
                        self.dump("MT_%d" % ch, MT, MTk)
                        self.dump("CD_%d" % ch, CD, CDk)
                        self.dump("XD_%d" % ch, xdt[:, :], [xdk])
                        self.dump("CZ_%d" % ch, czt[:, 0:256], [czk])
                    ylev = getattr(self, "ylev", 9)
                    for h in range(4):
                        if ylev < 1:
                            break
                        hh, g, ytile = h % 2, h // 2, h // 2
                        oap = yb[64 * hh:64 * hh + 64, ci * 256 + ytile * 128:ci * 256 + (ytile + 1) * 128]
                        st = (ci == 0 and ytile == 0)
                        self.mm(oap, xs_tok_head(ch, h), dI[:, h, :], st, False, [k("xst", ch), k("dI")], [ybk],
                                tile_position=(0, 64 * hh), skip_group_check=True)
                        for d in range(2):
                            jj = d * 4 + h
                            if ylev < 2:
                                break
                            self.mm(oap, xdt[:, d * 256 + h * 64:d * 256 + (h + 1) * 64], MT[:, jj, :], False, False,
                                    [xdk, MTk[d]], [ybk], tile_position=(0, 64 * hh), skip_group_check=True)
                            if ylev < 3:
                                continue
                            self.mm(oap, HT[:, ch, 2 * d + hh, :], CD[:, jj, :], False,
                                    d == 1, [k("HT", ch, d), CDk[d]], [ybk], tile_position=(0, 64 * hh),
                                    skip_group_check=True)
                if ylev < 4:
                    continue
                if l == 0:
                    ybs, ybsk = self.wk()
                    if ("YB_%d" % (hs // 128)) in self.dumps:
                        self.cp("act", ybs[:, :], yb[:, :], [ybk], [ybsk])
                        self.dump("YB_%d" % (hs // 128), ybs[:, :], [ybsk])
                y4 = yb[:, :].rearrange("p (c t n) -> p c t n", c=2, t=2)
                gts = []
                for zt in range(2):
                    bZ, bZk = self.bank("G")
                    self.proj(bZ[:, :256], bZk, lambda kc: wz[:, kc, zt * 128:(zt + 1) * 128], [wzk], hs, 256, bi=bi)
                    zs, zsk = self.wk()
                    self.act(zs[:, :256], bZ[:, :256], AF.Silu, [bZk], [zsk])
                    self.tt("dve", zs[:, :256].rearrange("p (c n) -> p c n", c=2), zs[:, :256].rearrange("p (c n) -> p c n", c=2),
                            y4[:, :, zt, :], ALU.mult, [zsk, ybk], [zsk])
                    gts.append((zs, zsk))
                bN, bNk = self.bank("G")
                for zt in range(2):
                    sq, sqk = self.wbt()
                    self.act(sq[:, :256], gts[zt][0][:, :256], AF.Square, [gts[zt][1]], [sqk])
                    self.mm(bN[:, :256], self.ones_b, sq[:, :256], zt == 0, zt == 1, [sqk, "onesb"], [bNk])
                rs, rsk = self.rstd_from_bank(bN, bNk, 128, 256, 256)
                for zt in range(2):
                    self.stt(self.yT[:, zt, half * 256:(half + 1) * 256], gts[zt][0][:, :256], self.pp[:, l, 72 + zt:73 + zt],
                             rs[:, :256], ALU.mult, ALU.mult, [gts[zt][1], rsk, "pp"], [("yT", zt)])
            if ylev < 4:
                continue
            self.dump("ssd_yT_%d_%d" % (l, bi), self.yT[:, :, :W], [("yT", 0), ("yT", 1)])
            self.outproj(l, bi, wo_key, 2)

    def mlp(self, l):
        P = self.P
        P.barrier()
        hid = self.ar_bf(0, [128, 4, T])
        W2 = [self.ar_bf(4 * T + i * 4096, [128, 4, 1024]) for i in range(2)]
        blocks = self.qblocks()
        for f in range(8):
            sl = f % 2
            w2k = ("@", "w2", l, sl)
            if sl == 0:
                w1keys = [("WQ",)] * 4
                self.load_w(self.WQ[:], self.w1[l, :, f * 512:(f + 1) * 512], w1keys[0])
                w1aps = [self.WQ[:, :, mt * 128:(mt + 1) * 128] for mt in range(4)]
            else:
                self.load_w(self.WS[0][:], self.w1[l, :, f * 512:f * 512 + 256], ("WS", 0))
                self.load_w(self.WS[1][:], self.w1[l, :, f * 512 + 256:(f + 1) * 512], ("WS", 1))
                w1keys = [("WS", 0), ("WS", 0), ("WS", 1), ("WS", 1)]
                w1aps = [self.WS[mt // 2][:, :, (mt % 2) * 128:(mt % 2 + 1) * 128] for mt in range(4)]
            self.dma("pool", W2[sl], self.w2[l, f * 512:(f + 1) * 512, :].rearrange("(kt p) n -> p kt n", p=128),
                     (), [w2k], "W2_%d" % sl)
            for bi in blocks:
                s, W = BLKS[bi]
                for mt in range(4):
                    bk, bkey = self.bank("S")
                    wap = w1aps[mt]
                    self.proj(bk[:, :W], bkey, lambda kc, wap=wap: wap[:, kc, :], [w1keys[mt]], s, W)
                    r, rk = self.wbt()
                    self.act(r[:, :W], bk[:, :W], AF.Relu, [bkey], [rk])
                    self.tt("pool", hid[:, mt, s:s + W], r[:, :W], r[:, :W], ALU.mult, [rk], [("@", "hid", l, mt, bi)])
            for bi in blocks:
                s, W = BLKS[bi]
                j = 1 if bi == 0 else 0
                for d in range(KC):
                    bk, bkey = self.bank("G")
                    for mt in range(4):
                        self.mm(bk[:, :W], W2[sl][:, mt, d * 128:(d + 1) * 128], hid[:, mt, s:s + W], mt == 0, mt == 3,
                                [w2k, ("@", "hid", l, mt, bi)], [bkey])
                    self.stt(self.hT[:, d, s:s + W], bk[:, :W], self.vec[:, 5, d, j:j + 1], self.hT[:, d, s:s + W],
                             ALU.mult, ALU.add, [bkey, ("vec", 5), ("hT", d, bi)], [("hT", d, bi)])

    def final(self):
        P = self.P
        P.barrier()
        fg = self.ar_f32(0, [128, KC])
        self.dma("sp", fg, self.fgd, (), [("@", "fg")], "fg")
        ot = [self.ar_f32(64 + i * 2048, [128, D]) for i in range(2)]
        for bi in range(1, 5):
            s, W = BLKS[bi]
            bk, bkey = self.bank("G")
            for c in range(KC):
                sq, sqk = self.wbt()
                self.act(sq[:, :W], self.hT[:, c, s:s + W], AF.Square, [("hT", c, bi)], [sqk])
                self.mm(bk[:, :W], self.ones_b, sq[:, :W], c == 0, c == KC - 1, [sqk, "onesb"], [bkey])
            rs, rsk = self.rstd_from_bank(bk, bkey, 128, W, D, out=self.wkL(0))
            for c in range(KC):
                self.stt(self.hT[:, c, s:s + W], self.hT[:, c, s:s + W], fg[:, c:c + 1], rs[:, :W], ALU.mult, ALU.mult,
                         [("hT", c, bi), ("@", "fg"), rsk], [("hT", c, bi)])
            for sub in range(4):
                t0 = s + sub * 128
                i = self.rot("ot", 2)
                okey = ("@", "ot", i)
                for g in range(2):
                    bT, bTk = self.bank("S")
                    for c4 in range(4):
                        c = g * 4 + c4
                        self.tr(bT[:, c4 * 128:(c4 + 1) * 128], self.hT[:, c, t0:t0 + 128], self.idf,
                                [("hT", c, bi), "cstf"], [bTk])
                    self.cp("act" if g == 0 else "dve", ot[i][:, g * 512:(g + 1) * 512], bT[:], [bTk], [okey])
                self.dma("sp", self.out[t0 - CTX:t0 - CTX + 128, :], ot[i], [okey], [("out",)], "out%d" % i)


def _build(n_layers=DEPTH, dumps=(), stop=None):
    kb = KB(n_layers, dumps, stop)
    orig_alloc = kb.alloc

    def alloc2():
        orig_alloc()
        kb.epsT = kb.sb("epsT", [128, 1], F32)
        kb.oneT = kb.sb("oneT", [128, 1], F32)
        kb.WS_all = None
        kb.memset("dve", kb.epsT[:], EPS, ["epsT"])
        kb.memset("dve", kb.oneT[:], 1.0, ["oneT"])
    kb.alloc = alloc2
    nc = kb.build()
    return nc, kb


def _rope_tables():
    def tab(rot):
        rows = SEQ // 64
        row = np.repeat(np.arange(rows), 64).astype(np.float32)
        col = np.tile(np.arange(64), rows).astype(np.float32)
        nf = rot // 4
        inv = (10000.0 ** (-np.arange(nf, dtype=np.float32) / nf)).astype(np.float32)
        ang = np.concatenate([row[:, None] * inv, col[:, None] * inv], axis=-1).astype(np.float32)
        cos, sin = np.cos(ang), np.sin(ang)
        half = rot // 2
        cosT = np.zeros((128, SEQ), np.float32)
        sinT = np.zeros((128, SEQ), np.float32)
        for p in range(128):
            d = p % rot
            i = d % half
            cosT[p] = cos[:, i]
            sinT[p] = -sin[:, i] if d < half else sin[:, i]
        return cosT, sinT
    c32, s32 = tab(32)
    c64, s64 = tab(64)
    return np.stack([c32, s32, c64, s64]).astype(np.float32)


def _consts():
    c = np.zeros((128, 8, 128), np.float32)
    k = np.arange(128)
    c[:, 0, :] = np.eye(128)
    c[:, 1, :] = 1.0
    c[:, 2, :] = (k[:, None] <= k[None, :])
    c[:, 3, :] = (k[:, None] >= k[None, :])
    c[:, 4, :] = np.where(k[None, :] >= k[:, None], 0.0, NEG)
    c[:, 5, :] = np.where(k[None, :] <= k[:, None], 0.0, NEG)
    c[:, 6, :] = (k[:, None] // 64 == k[None, :] // 64)
    return c


def _prep_shared(inp):
    f = lambda a: np.ascontiguousarray(np.asarray(a, dtype=np.float32))
    pp = np.zeros((128, DEPTH, NPP), np.float32)
    pb = np.zeros((DEPTH, NPB), np.float32)
    p = np.arange(128)
    for l in range(DEPTH):
        pp[:, l, 0:48] = f(inp["mod_b"])[l].reshape(48, 128).T
        pp[:, l, 48:56] = f(inp["norm1_g"])[l].reshape(8, 128).T
        pp[:, l, 56:64] = f(inp["norm2_g"])[l].reshape(8, 128).T
        pp[:, l, 64:68] = f(inp["ssd_conv_b"])[l].reshape(4, 128).T
        dd = f(inp["ssd_d"])[l]
        for d in range(2):
            for t in range(2):
                pp[:, l, 68 + d * 2 + t] = dd[d][2 * t + p // 64]
        pp[:, l, 72:74] = f(inp["ssd_norm_g"])[l].reshape(2, 128).T
        gq = f(inp["gqa_q_norm"])[l]
        gk = f(inp["gqa_k_norm"])[l]
        pp[:, l, 74] = gq[p % 64]
        pp[:, l, 75] = gq[(p % 64 + 32) % 64]
        pp[:, l, 76] = gk[p % 64]
        pp[:, l, 77] = gk[(p % 64 + 32) % 64]
        mq = f(inp["mla_q_norm"])[l]
        pp[:, l, 78] = mq[0:128]
        pp[0:64, l, 79] = mq[128:192]
        pp[:, l, 80] = f(inp["mla_kv_norm"])[l]
        pp[:, l, 81] = f(inp["diff_norm_g"])[l][p % 64]
        pb[l, 0:8] = f(inp["ssd_dt_bias"])[l].reshape(8)
        pb[l, 8:16] = f(inp["ssd_a_log"])[l].reshape(8)
        pb[l, 16:144] = f(inp["diff_lambda"])[l].reshape(128)
        pb[l, 144:208] = f(inp["diff_norm_g"])[l]
        pb[l, 208:216] = f(inp["ssd_d"])[l].reshape(8)
    sh = {
        "mod_w": f(inp["mod_w"]), "pp": pp, "pb": pb,
        "conv_wT": np.ascontiguousarray(f(inp["ssd_conv_w"]).transpose(0, 2, 1)),
        "final_gT": np.ascontiguousarray(f(inp["final_norm_g"]).reshape(8, 128).T),
        "w_in": f(inp["w_in"]), "mla_w_uq": f(inp["mla_w_uq"]), "mla_w_ukv": f(inp["mla_w_ukv"]),
        "w_out": f(inp["w_out"]), "mlp_w1": f(inp["mlp_w1"]), "mlp_w2": f(inp["mlp_w2"]),
        "cst": _consts(), "rope": _rope_tables(),
    }
    return sh


def _prep_core(inp, b, sh):
    f = lambda a: np.ascontiguousarray(np.asarray(a, dtype=np.float32))
    m = dict(sh)
    m["xin"] = np.ascontiguousarray(np.concatenate([f(inp["ctx"])[b], f(inp["x"])[b]], axis=0))
    cc = np.stack([f(inp["c"])[b], f(inp["c_ctx"])], axis=0)
    m["ccT"] = np.ascontiguousarray(cc.reshape(2, 8, 128).transpose(2, 1, 0))
    return m


_NC_CACHE = {}


def kernel(**inputs):
    if "nc" not in _NC_CACHE:
        _NC_CACHE["nc"] = _build()[0]
    nc = _NC_CACHE["nc"]
    sh = _prep_shared(inputs)
    in_maps = [_prep_core(inputs, b, sh) for b in range(8)]
    res = run_bass_kernel_spmd(nc, in_maps, core_ids=list(range(8)))
    out = np.stack([np.asarray(r["out"], dtype=np.float32) for r in res.results], axis=0)
    return out
```

```python
import contextlib
import math
import numpy as np
import concourse.bass as bass
import concourse.mybir as mybir
from concourse.bass_utils import run_bass_kernel_spmd

F32 = mybir.dt.float32
BF16 = mybir.dt.bfloat16
AF = mybir.ActivationFunctionType
ALU = mybir.AluOpType
AX = mybir.AxisListType

D = 1024
KC = 8
CTX = 256
SEQ = 2048
T = CTX + SEQ
NCH = T // 128
DEPTH = 4
EPS = 1e-6
SSD_OFF, DIFF_OFF, GQA_OFF, MLA_OFF, IN_COLS = 0, 776, 1544, 2056, 2408
BLKS = [(0, 256)] + [(256 + 512 * i, 512) for i in range(4)]
NPP = 96
NPB = 216
NEG = -1.0e30


class Op:
    __slots__ = ("eng", "fn", "deps", "is_dma", "sem", "val", "needs_inc", "pseudo")


class Prog:
    ENGS = ("pe", "act", "dve", "pool", "sp")

    def __init__(self, nc):
        self.nc = nc
        self.ops = {e: [] for e in self.ENGS}
        self.last_w = {}
        self.readers = {}
        self.dma_cnt = {}
        self.barrier_ops = []
        self.dma_since = []
        self.seen = set()

    def barrier(self):
        b = []
        for e in self.ENGS:
            for o in reversed(self.ops[e]):
                if not o.is_dma:
                    b.append(o)
                    break
        b.extend(self.dma_since)
        self.dma_since = []
        self.barrier_ops = b

    def op(self, eng, fn, reads=(), writes=(), dma=None):
        o = Op()
        o.eng = eng
        o.fn = fn
        o.is_dma = dma is not None
        o.sem = dma
        o.val = 0
        o.needs_inc = o.is_dma
        ps_reads = [r for r in reads if isinstance(r, tuple) and r[0] == "ps" and r not in writes]
        o.pseudo = frozenset(ps_reads)
        writes = list(writes) + ps_reads
        deps = {}
        for r in reads:
            w = self.last_w.get(r)
            if w is not None:
                raw = r not in w.pseudo
                if id(w) not in deps or raw:
                    deps[id(w)] = (w, raw)
        for r in writes:
            if r not in self.seen:
                self.seen.add(r)
                if isinstance(r, tuple) and r[0] == "@":
                    for b in self.barrier_ops:
                        if id(b) not in deps:
                            deps[id(b)] = (b, True)
            w = self.last_w.get(r)
            if w is not None and id(w) not in deps:
                deps[id(w)] = (w, False)
            for rd in self.readers.get(r, ()):
                if id(rd) not in deps:
                    deps[id(rd)] = (rd, False)
        dl = []
        for w, raw in deps.values():
            if not w.is_dma and w.eng == eng and not o.is_dma:
                if eng == "pe":
                    continue
            dl.append(w)
            w.needs_inc = True
        o.deps = dl
        if o.is_dma:
            self.dma_cnt[dma] = self.dma_cnt.get(dma, 0) + 16
            o.val = self.dma_cnt[dma]
            self.dma_since.append(o)
        for r in reads:
            self.readers.setdefault(r, []).append(o)
        for r in writes:
            self.last_w[r] = o
            self.readers[r] = []
        self.ops[eng].append(o)
        return o

    def emit(self):
        nc = self.nc
        with contextlib.ExitStack() as es:
            esem = {e: es.enter_context(nc.semaphore("s_" + e)) for e in self.ENGS}
            dsem = {k: es.enter_context(nc.semaphore("d_%d" % i)) for i, k in enumerate(self.dma_cnt)}
            for e in self.ENGS:
                c = 0
                for o in self.ops[e]:
                    if o.is_dma:
                        continue
                    if o.needs_inc:
                        c += 1
                        o.val = c
            block = es.enter_context(nc.Block())
            prog = self

            def run(e, engobj):
                waited = {}
                for o in prog.ops[e]:
                    for w in o.deps:
                        if w.is_dma:
                            key = ("d", w.sem)
                            s = dsem[w.sem]
                        else:
                            key = ("e", w.eng)
                            s = esem[w.eng]
                        if waited.get(key, 0) < w.val:
                            engobj.wait_ge(s, w.val)
                            waited[key] = w.val
                    ins = o.fn(engobj)
                    if o.is_dma:
                        ins.then_inc(dsem[o.sem], 16)
                    elif o.needs_inc:
                        ins.then_inc(esem[e], 1)
                if e == "sp":
                    for k, c in prog.dma_cnt.items():
                        if waited.get(("d", k), 0) < c:
                            engobj.wait_ge(dsem[k], c)

            @block.tensor
            def _(eng):
                run("pe", eng)

            @block.scalar
            def _(eng):
                run("act", eng)

            @block.vector
            def _(eng):
                run("dve", eng)

            @block.gpsimd
            def _(eng):
                run("pool", eng)

            @block.sync
            def _(eng):
                run("sp", eng)


class KB:
    def __init__(self, n_layers=DEPTH, dumps=(), stop=None):
        self.n_layers = n_layers
        self.dumps = set(dumps)
        self.stop = stop
        self.nc = bass.Bass("TRN2", target_bir_lowering=False)
        self.P = Prog(self.nc)
        self.es = contextlib.ExitStack()
        self.dump_specs = {}
        self._rr = {}
        self.deferred = []
        self.bgseq = 0

    def dram_in(self, name, shape):
        return self.nc.dram_tensor(name, list(shape), F32, kind="ExternalInput").ap()

    def sb(self, name, shape, dt):
        return self.es.enter_context(self.nc.sbuf_tensor("sb_" + name, list(shape), dt))

    def rot(self, name, n):
        i = self._rr.get(name, 0)
        self._rr[name] = i + 1
        return i % n

    def mm(self, out, lhsT, rhs, start, stop, r, w, **kw):
        self.P.op("pe", lambda e: e.matmul(out, lhsT=lhsT, rhs=rhs, start=start, stop=stop, **kw), r, w)

    def tr(self, out, in_, ident, r, w):
        self.P.op("pe", lambda e: e.transpose(out=out, in_=in_, identity=ident), r, w)

    def act(self, out, in_, func, r, w, scale=None, bias=None, accum=None):
        kw = {}
        if scale is not None:
            kw["scale"] = scale
        if bias is not None:
            kw["bias"] = bias
        if accum is not None:
            kw["accum_out"] = accum
        self.P.op("act", lambda e: e.activation(out=out, in_=in_, func=func, **kw), r, w)

    POOL_ENG = "dve"

    def ts(self, eng, out, in0, s1, s2, op0, op1, r, w):
        if eng == "pool":
            eng = self.POOL_ENG
        if op1 is None:
            self.P.op(eng, lambda e: e.tensor_scalar(out=out, in0=in0, scalar1=s1, scalar2=None, op0=op0), r, w)
        else:
            self.P.op(eng, lambda e: e.tensor_scalar(out=out, in0=in0, scalar1=s1, scalar2=s2, op0=op0, op1=op1), r, w)

    def tt(self, eng, out, in0, in1, op, r, w):
        if eng == "pool":
            eng = self.POOL_ENG
        self.P.op(eng, lambda e: e.tensor_tensor(out=out, in0=in0, in1=in1, op=op), r, w)

    def stt(self, out, in0, scalar, in1, op0, op1, r, w):
        self.P.op("dve", lambda e: e.scalar_tensor_tensor(out=out, in0=in0, scalar=scalar, in1=in1, op0=op0, op1=op1), r, w)

    def cp(self, eng, out, in_, r, w):
        if eng == "pool":
            eng = self.POOL_ENG
        if eng == "act":
            self.P.op("act", lambda e: e.activation(out=out, in_=in_, func=AF.Copy), r, w)
        else:
            self.P.op(eng, lambda e: e.tensor_copy(out=out, in_=in_), r, w)

    def recip(self, out, in_, r, w):
        self.P.op("dve", lambda e: e.reciprocal(out=out, in_=in_), r, w)

    def memset(self, eng, ap, val, w):
        if eng == "pool":
            eng = self.POOL_ENG
        self.P.op(eng, lambda e: e.memset(ap, val), (), w)

    def dma(self, q, out, in_, r, w, sem):
        self.P.op(q, lambda e: e.dma_start(out=out, in_=in_), r, w, dma=sem)

    def dump(self, name, ap, reads):
        if name not in self.dumps:
            return
        dt = ap.dtype
        t = self.nc.dram_tensor("dbg_" + name, list(ap.shape), dt, kind="ExternalOutput").ap()
        self.dump_specs[name] = (list(ap.shape), dt)
        self.dma("sp", t, ap, reads, [("dbg", name)], "dbg_" + name)

    def wk(self):
        i = self.rot("wk", 4)
        return self.wks[i], ("wk", i)

    def wkL(self, i):
        return self.wks[4 + i], ("wk", 4 + i)

    def wbt(self):
        i = self.rot("wb", 4)
        return self.wbs[i], ("wb", i)

    def bank(self, grp):
        ids = {"S": (0, 1), "O": (2, 3, 4), "G": (5, 6, 7)}[grp]
        i = ids[self.rot("bank" + grp, len(ids))]
        return self.banks[i], ("ps", i)

    def blk_of(self, tok):
        return 0 if tok < 256 else 1 + (tok - 256) // 512

    def build(self):
        nc = self.nc
        with self.es:
            self.alloc()
            self.setup()
            for l in range(self.n_layers):
                self.layer(l)
                if self.stop is not None and self.stop[0] == l:
                    break
            if self.stop is None:
                self.final()
            self.P.emit()
        return nc

    def alloc(self):
        self.xin = self.dram_in("xin", [T, D])
        self.ccT = self.dram_in("ccT", [128, KC, 2])
        self.mod_w = self.dram_in("mod_w", [DEPTH, D, 6 * D])
        self.ppd = self.dram_in("pp", [128, DEPTH, NPP])
        self.pbd = self.dram_in("pb", [DEPTH, NPB])
        self.cwd = self.dram_in("conv_wT", [DEPTH, 3, 512])
        self.fgd = self.dram_in("final_gT", [128, KC])
        self.w_in = self.dram_in("w_in", [DEPTH, D, IN_COLS])
        self.w_uq = self.dram_in("mla_w_uq", [DEPTH, 192, 384])
        self.w_ukv = self.dram_in("mla_w_ukv", [DEPTH, 128, 512])
        self.w_out = self.dram_in("w_out", [DEPTH, D, D])
        self.w1 = self.dram_in("mlp_w1", [DEPTH, D, 4 * D])
        self.w2 = self.dram_in("mlp_w2", [DEPTH, 4 * D, D])
        self.cst = self.dram_in("cst", [128, 8, 128])
        self.ropd = self.dram_in("rope", [4, 128, SEQ])
        self.out = self.nc.dram_tensor("out", [SEQ, D], F32, kind="ExternalOutput").ap()
        sb = self.sb
        self.hT = sb("hT", [128, KC, T], F32)
        self.xnT = sb("xnT", [128, KC, T], BF16)
        self.cstf = sb("cstf", [128, 8, 128], F32)
        self.cstb = sb("cstb", [128, 3, 128], BF16)
        self.pp = sb("pp", [128, DEPTH, NPP], F32)
        self.pb = sb("pbb", [128, NPB], F32)
        self.condT = sb("condT", [128, KC, 2], F32)
        self.condb = sb("condb", [128, KC, 2], F32)
        self.modT = sb("modT", [128, 48, 2], F32)
        self.vec = sb("vec", [128, 6, KC, 2], F32)
        self.sm = sb("sm", [128, 64], F32)
        self.rt = [sb("rt%d" % i, [128, 2, 512], BF16) for i in range(2)]
        self.WS = [sb("WS%d" % i, [128, KC, 256], BF16) for i in range(2)]
        self.WQ = sb("WQ", [128, KC, 512], BF16)
        self.WO = sb("WO", [128, 4, 1024], BF16)
        self.PTall = sb("PTall", [128, 4, 512], BF16)
        self.PT = [self.PTall[:, i, :] for i in range(4)]
        self.wks = [sb("wk%d" % i, [128, 512], F32) for i in range(5)]
        self.wbs = [sb("wb%d" % i, [128, 512], BF16) for i in range(4)]
        self.yTh = sb("yTh", [64, 4, 512], BF16)
        self.yT = sb("yT", [128, 2, 512], BF16)
        self.ARENA = 20224
        self.arena = sb("arena", [128, self.ARENA], BF16)
        self.banks = [self.es.enter_context(self.nc.psum_tensor("bank%d" % i, [128, 512], F32)) for i in range(8)]
        self.idf = self.cstf[:, 0, :]
        self.ones_f = self.cstf[:, 1, :]
        self.U = self.cstf[:, 2, :]
        self.Lm = self.cstf[:, 3, :]
        self.nmf = self.cstf[:, 4, :]
        self.nmb = self.cstf[:, 5, :]
        self.idb = self.cstb[:, 0, :]
        self.ones_b = self.cstb[:, 1, :]
        self.bd64 = self.cstb[:, 2, :]

    def ar_bf(self, off, shape):
        n = int(np.prod(shape[1:]))
        assert off + n <= self.ARENA, (off, n)
        ap = self.arena[:, off:off + n]
        if len(shape) == 2:
            return ap
        names = " ".join("d%d" % i for i in range(1, len(shape)))
        kw = {"d%d" % i: shape[i] for i in range(1, len(shape))}
        return ap.rearrange("p (%s) -> p %s" % (names, names), **kw)

    def ar_f32(self, off, shape):
        n = int(np.prod(shape[1:])) * 2
        assert off % 2 == 0 and off + n <= self.ARENA, (off, n)
        ap = self.arena[:, off:off + n].bitcast(F32)
        if len(shape) == 2:
            return ap
        names = " ".join("d%d" % i for i in range(1, len(shape)))
        kw = {"d%d" % i: shape[i] for i in range(1, len(shape))}
        return ap.rearrange("p (%s) -> p %s" % (names, names), **kw)

    def setup(self):
        P = self.P
        self.dma("sp", self.cstf[:], self.cst, (), ["cstf"], "const0")
        self.dma("sp", self.pp[:], self.ppd, (), ["pp"], "const1")
        self.dma("sp", self.condT[:], self.ccT, (), ["condT"], "const2")
        self.cp("dve", self.cstb[:, 0, :], self.cstf[:, 0, :], ["cstf"], ["idb"])
        self.cp("dve", self.cstb[:, 1, :], self.cstf[:, 1, :], ["cstf"], ["onesb"])
        self.cp("dve", self.cstb[:, 2, :], self.cstf[:, 6, :], ["cstf"], ["bd64"])
        self.act(self.condb[:], self.condT[:], AF.Silu, ["condT"], ["condb"])
        P.barrier()
        xt = [self.ar_f32(i * 2048, [128, D]) for i in range(4)]
        for ch in range(NCH):
            s = ch % 4
            key = ("@", "xin", s)
            self.dma("sp", xt[s], self.xin[ch * 128:(ch + 1) * 128, :], (), [key], "xin%d" % s)
            bi = self.blk_of(ch * 128)
            for g in range(2):
                bk, bkey = self.bank("G")
                for i in range(4):
                    f = g * 4 + i
                    self.tr(bk[:, i * 128:(i + 1) * 128], xt[s][:, f * 128:(f + 1) * 128], self.idf, [key, "cstf"], [bkey])
                self.cp("dve" if g == 0 else "act",
                        self.hT[:, g * 4:(g + 1) * 4, ch * 128:(ch + 1) * 128],
                        bk[:].rearrange("p (a b) -> p a b", a=4), [bkey], [("hT", g * 4 + i, bi) for i in range(4)])

    def load_w(self, dst, src_rows_cols, key, q="pool"):
        self.dma(q, dst, src_rows_cols.rearrange("(kc p) n -> p kc n", p=128), (), [key], "W_" + str(key))

    def layer(self, l):
        self.with_ctx = l < DEPTH - 1
        self.mod(l)
        self.norm(l, 0)
        self.dump("xn1_%d" % l, self.xnT[:, :, 0:768], [("xn", c, b) for c in range(KC) for b in range(2)])
        if self.stop == (l, "norm1"):
            return
        self.gqa(l)
        self.dump("h_gqa_%d" % l, self.hT[:, :, :], [("hT", c, b) for c in range(KC) for b in range(5)])
        if self.stop == (l, "gqa"):
            return
        self.diff(l)
        self.dump("h_diff_%d" % l, self.hT[:, :, :], [("hT", c, b) for c in range(KC) for b in range(5)])
        if self.stop == (l, "diff"):
            return
        self.mla(l)
        self.dump("h_mla_%d" % l, self.hT[:, :, :], [("hT", c, b) for c in range(KC) for b in range(5)])
        if self.stop == (l, "mla"):
            return
        self.ssd(l)
        if self.stop is not None and self.stop[0] == l and self.stop[1].startswith("ssd_"):
            return
        self.dump("h_ssd_%d" % l, self.hT[:, :, :], [("hT", c, b) for c in range(KC) for b in range(5)])
        if self.stop == (l, "ssd"):
            return
        self.norm(l, 1)
        self.mlp(l)
        self.dump("h_mlp_%d" % l, self.hT[:, :, :], [("hT", c, b) for c in range(KC) for b in range(5)])

    def qblocks(self):
        return list(range(5)) if self.with_ctx else list(range(1, 5))

    def mod(self, l):
        P = self.P
        P.barrier()
        NPC = 256
        st = [self.ar_f32(i * (KC * NPC * 2), [128, KC, NPC]) for i in range(2)]
        bk, bkey = self.bank("G")
        bv = bk[:, 0:96].rearrange("p (a b) -> p a b", b=2)
        for pc in range(6 * D // NPC):
            s = pc % 2
            key = ("@", "modw", l, s)
            self.dma("sp", st[s], self.mod_w[l, :, pc * NPC:(pc + 1) * NPC].rearrange("(kc p) n -> p kc n", p=128),
                     (), [key], "modw%d" % s)
            br, brk = self.bank("S")
            for kc in range(KC):
                self.mm(br[0:2, 0:NPC], self.condb[:, kc, :], st[s][:, kc, :], kc == 0, kc == KC - 1,
                        [key, "condb"], [brk])
            row, rowk = self.wk()
            self.cp("dve", row[0:2, 0:NPC], br[0:2, 0:NPC], [brk], [rowk])
            for i in range(NPC // 128):
                m = pc * (NPC // 128) + i
                self.tr(bv[:, m, :], row[0:2, i * 128:(i + 1) * 128], self.idf[0:2, 0:2], [rowk, "cstf"], [bkey])
        self.tt("dve", self.modT[:], bv, self.pp[:, l, 0:48].unsqueeze(2).broadcast_to([128, 48, 2]), ALU.add,
                [bkey, "pp"], ["modT"])
        n1 = self.pp[:, l, 48:56].unsqueeze(2).broadcast_to([128, 8, 2])
        n2 = self.pp[:, l, 56:64].unsqueeze(2).broadcast_to([128, 8, 2])
        self.stt(self.vec[:, 0], self.modT[:, 8:16, :], 1.0, n1, ALU.add, ALU.mult, ["modT", "pp"], [("vec", 0)])
        self.cp("dve", self.vec[:, 1], self.modT[:, 0:8, :], ["modT"], [("vec", 1)])
        self.cp("dve", self.vec[:, 2], self.modT[:, 16:24, :], ["modT"], [("vec", 2)])
        self.stt(self.vec[:, 3], self.modT[:, 32:40, :], 1.0, n2, ALU.add, ALU.mult, ["modT", "pp"], [("vec", 3)])
        self.cp("dve", self.vec[:, 4], self.modT[:, 24:32, :], ["modT"], [("vec", 4)])
        self.cp("dve", self.vec[:, 5], self.modT[:, 40:48, :], ["modT"], [("vec", 5)])
        self.dma("sp", self.pb[:], self.pbd[l].partition_broadcast(128), (), ["pb"], "pb")

    def rstd_from_bank(self, bk, bkey, rows, W, n, out=None):
        sd, sdk = self.wk()
        self.act(sd[0:rows, :W], bk[0:rows, :W], AF.Ln, [bkey, "epsT"], [sdk], scale=1.0 / n, bias=self.epsT[0:rows, :])
        rs, rsk = out if out is not None else self.wk()
        self.act(rs[0:rows, :W], sd[0:rows, :W], AF.Exp, [sdk], [rsk], scale=-0.5)
        return rs, rsk

    def norm(self, l, which, blocks=None):
        si, bi_ = (0, 1) if which == 0 else (3, 4)
        for bi in (blocks if blocks is not None else range(5)):
            s, W = BLKS[bi]
            j = 1 if bi == 0 else 0
            bk, bkey = self.bank("G")
            for c in range(KC):
                sq, sqk = self.wbt()
                self.act(sq[:, :W], self.hT[:, c, s:s + W], AF.Square, [("hT", c, bi)], [sqk])
                self.mm(bk[:, :W], self.ones_b, sq[:, :W], c == 0, c == KC - 1, [sqk, "onesb"], [bkey])
            rs, rsk = self.rstd_from_bank(bk, bkey, 128, W, D, out=self.wkL(0))
            for c in range(KC):
                t, tk = self.wk()
                self.stt(t[:, :W], self.hT[:, c, s:s + W], self.vec[:, si, c, j:j + 1], rs[:, :W], ALU.mult, ALU.mult,
                         [("hT", c, bi), ("vec", si), rsk], [tk])
                self.act(self.xnT[:, c, s:s + W], t[:, :W], AF.Identity, [tk, ("vec", bi_)], [("xn", c, bi)],
                         bias=self.vec[:, bi_, c, j:j + 1])

    def proj(self, bk_ap, bkey, wfn, wkeys, s, W, start=True, stop=True, bi=None, **kw):
        if bi is None:
            bi = self.blk_of(s)
        for kc in range(KC):
            self.mm(bk_ap, wfn(kc), self.xnT[:, kc, s:s + W], start and kc == 0, stop and kc == KC - 1,
                    list(wkeys) + [("xn", kc, bi)], [bkey], **kw)

    def load_rope(self, which, s, W):
        i = self.rot("rt", 2)
        key = ("rt", i)
        src = self.ropd[2 * which:2 * which + 2, :, s - CTX:s - CTX + W].rearrange("a p n -> p a n")
        self.dma("pool", self.rt[i][:, :, :W], src, (), [key], "rt%d" % i)
        return self.rt[i], key

    def attention(self, q_ap, qkeys, k_fn, kkeys, v_fn, vkeys, pbase, W, nkc, scale, ob, obkey):
        LA = 1
        kw = {}
        if pbase != 0:
            kw["tile_position"] = (pbase, 0)
        sbs = {}

        def qk(kc):
            sb_, sk = self.bank("S")
            sbs[kc] = (sb_, sk)
            self.mm(sb_[:, :W], k_fn(kc), q_ap, True, True, list(kkeys) + list(qkeys), [sk], **kw)

        seq0 = self.bgseq
        for kc in range(min(LA, nkc)):
            qk(kc)
        for kc in range(nkc):
            if kc % 5 == 4:
                self.inject()
            if kc + LA < nkc:
                qk(kc + LA)
            sb_, sk = sbs.pop(kc)
            pi = self.rot("PT", 4)
            pt, pk = self.PT[pi], ("PT", pi)
            self.act(pt[:, :W], sb_[:, :W], AF.Exp, [sk], [pk], scale=scale)
            self.mm(ob[0:65, :W], v_fn(kc), pt[:, :W], kc == 0, kc == nkc - 1, [pk] + list(vkeys), [obkey])
        self.flush(older_than=seq0)

    def bcast_row(self, row_tile, row_key, W):
        bB, bBk = self.bank("G")
        self.mm(bB[0:64, :W], self.ones_f[64:65, 0:64], row_tile[64:65, :W], True, True, [row_key, "cstf"], [bBk],
                tile_position=(64, 0))
        c, ck = self.wk()
        self.cp("dve", c[0:64, :W], bB[0:64, :W], [bBk], [ck])
        return c, ck

    def defer(self, gen):
        self.bgseq += 1
        self.deferred.append((self.bgseq, gen))

    def inject(self):
        while self.deferred:
            try:
                next(self.deferred[0][1])
                return
            except StopIteration:
                self.deferred.pop(0)

    def flush(self, older_than=None):
        while self.deferred and (older_than is None or self.deferred[0][0] <= older_than):
            for _ in self.deferred[0][1]:
                pass
            self.deferred.pop(0)

    def finish_head_softmax(self, ob, obk, h, W):
        r, rk = self.wk()
        self.recip(r[64:65, :W], ob[64:65, :W], [obk], [rk])
        yield
        bB, bBk = self.bcast_mm(r, rk, W)
        yield
        c, ck = self.wk()
        self.cp("dve", c[0:64, :W], bB[0:64, :W], [bBk], [ck])
        yield
        self.tt("dve", self.yTh[0:64, h, :W], ob[0:64, :W], c[0:64, :W], ALU.mult, [obk, ck], [("yTh", h)])

    def bcast_mm(self, row_tile, row_key, W):
        bB, bBk = self.bank("G")
        self.mm(bB[0:64, :W], self.ones_f[64:65, 0:64], row_tile[64:65, :W], True, True, [row_key, "cstf"], [bBk],
                tile_position=(64, 0))
        return bB, bBk

    def outproj_heads_gen(self, l, bi, wo_key):
        s, W = BLKS[bi]
        j = 1 if bi == 0 else 0
        for d in range(KC):
            bk, bkey = self.bank("G")
            for h in range(4):
                self.mm(bk[:, :W], self.WO[0:64, h, d * 128:(d + 1) * 128], self.yTh[0:64, h, :W], h == 0, h == 3,
                        [wo_key, ("yTh", h)], [bkey])
            yield
            self.stt(self.hT[:, d, s:s + W], bk[:, :W], self.vec[:, 2, d, j:j + 1], self.hT[:, d, s:s + W],
                     ALU.mult, ALU.add, [bkey, ("vec", 2), ("hT", d, bi)], [("hT", d, bi)])

    def outproj_heads(self, l, bi, wo_key):
        s, W = BLKS[bi]
        j = 1 if bi == 0 else 0
        for d in range(KC):
            bk, bkey = self.bank("G")
            for h in range(4):
                self.mm(bk[:, :W], self.WO[0:64, h, d * 128:(d + 1) * 128], self.yTh[0:64, h, :W], h == 0, h == 3,
                        [wo_key, ("yTh", h)], [bkey])
            self.stt(self.hT[:, d, s:s + W], bk[:, :W], self.vec[:, 2, d, j:j + 1], self.hT[:, d, s:s + W],
                     ALU.mult, ALU.add, [bkey, ("vec", 2), ("hT", d, bi)], [("hT", d, bi)])

    def load_wo_heads(self, l, row0):
        key = ("WO",)
        self.dma("pool", self.WO[0:64, :, :], self.w_out[l, row0:row0 + 256, :].rearrange("(h p) n -> p h n", p=64),
                 (), [key], "WO")
        return key

    def finish_block(self, l, bi, nsub, wo_key):
        s, W = BLKS[bi]
        j = 1 if bi == 0 else 0
        for ft in range(2):
            bk, bkey = self.bank("G")
            bb = bk[:].bitcast(BF16)
            for sub in range(nsub):
                self.tr(bb[:, sub * 128:(sub + 1) * 128], self.ytok[:, sub, ft * 128:(ft + 1) * 128], self.idb,
                        ["ytok", "idb"], [bkey])
            self.cp("act" if ft == 0 else "dve", self.yT[:, ft, :W], bb[:, :W], [bkey], [("yT", ft)])
        self.outproj(l, bi, wo_key, 2)

    def outproj(self, l, bi, wo_key, gate):
        s, W = BLKS[bi]
        j = 1 if bi == 0 else 0
        for d in range(KC):
            bk, bkey = self.bank("G")
            for kt in range(2):
                self.mm(bk[:, :W], self.WO[:, kt, d * 128:(d + 1) * 128], self.yT[:, kt, :W], kt == 0, kt == 1,
                        [wo_key, ("yT", kt)], [bkey])
            self.stt(self.hT[:, d, s:s + W], bk[:, :W], self.vec[:, gate, d, j:j + 1], self.hT[:, d, s:s + W],
                     ALU.mult, ALU.add, [bkey, ("vec", gate), ("hT", d, bi)], [("hT", d, bi)])

    def load_wo(self, l, row0):
        key = ("WO",)
        self.dma("pool", self.WO[:, 0:2, :], self.w_out[l, row0:row0 + 256, :].rearrange("(kt p) n -> p kt n", p=128),
                 (), [key], "WO")
        return key

    def qk_norm_rope(self, l, bA, bAk, bB, bBk, W, seq, gcol, rt, rtk, out_ap, out_key, outs=None):
        if outs is None:
            outs = [(slice(0, 128), out_ap, out_key)]
        sq, sqk = self.wbt()
        self.act(sq[:, :W], bA[:, :W], AF.Square, [bAk], [sqk])
        bC, bCk = self.bank("G")
        self.mm(bC[:, :W], self.bd64, sq[:, :W], True, True, [sqk, "bd64"], [bCk])
        rs, rsk = self.rstd_from_bank(bC, bCk, 128, W, 64)
        g = self.pp[:, l, gcol:gcol + 1]
        gs = self.pp[:, l, gcol + 1:gcol + 2]
        if not seq:
            for rows, oap, okey in outs:
                self.stt(oap, bA[rows, :W], g[rows, :], rs[rows, :W], ALU.mult, ALU.mult, [bAk, rsk, "pp"], [okey])
            return
        a, ak = self.wk()
        self.stt(a[:, :W], bA[:, :W], g, rt[:, 0, :W], ALU.mult, ALU.mult, [bAk, rtk, "pp"], [ak])
        b, bk_ = self.wk()
        self.stt(b[:, :W], bB[:, :W], gs, rt[:, 1, :W], ALU.mult, ALU.mult, [bBk, rtk, "pp"], [bk_])
        self.tt("pool", a[:, :W], a[:, :W], b[:, :W], ALU.add, [ak, bk_], [ak])
        for rows, oap, okey in outs:
            self.tt("pool", oap, a[rows, :W], rs[rows, :W], ALU.mult, [ak, rsk], [okey])

    def qk_norm_rope_gen(self, l, bA, bAk, bB, bBk, W, seq, gcol, rt, rtk, outs):
        sq, sqk = self.wbt()
        self.act(sq[:, :W], bA[:, :W], AF.Square, [bAk], [sqk])
        yield
        bC, bCk = self.bank("G")
        self.mm(bC[:, :W], self.bd64, sq[:, :W], True, True, [sqk, "bd64"], [bCk])
        yield
        sd, sdk = self.wk()
        self.act(sd[:, :W], bC[:, :W], AF.Ln, [bCk, "epsT"], [sdk], scale=1.0 / 64, bias=self.epsT[:, :])
        yield
        rs, rsk = self.wk()
        self.act(rs[:, :W], sd[:, :W], AF.Exp, [sdk], [rsk], scale=-0.5)
        yield
        g = self.pp[:, l, gcol:gcol + 1]
        gs = self.pp[:, l, gcol + 1:gcol + 2]
        if not seq:
            for rows, oap, okey in outs:
                self.stt(oap, bA[rows, :W], g[rows, :], rs[rows, :W], ALU.mult, ALU.mult, [bAk, rsk, "pp"], [okey])
            return
        a, ak = self.wk()
        self.stt(a[:, :W], bA[:, :W], g, rt[:, 0, :W], ALU.mult, ALU.mult, [bAk, rtk, "pp"], [ak])
        b, bk_ = self.wk()
        self.stt(b[:, :W], bB[:, :W], gs, rt[:, 1, :W], ALU.mult, ALU.mult, [bBk, rtk, "pp"], [bk_])
        yield
        self.tt("pool", a[:, :W], a[:, :W], b[:, :W], ALU.add, [ak, bk_], [ak])
        yield
        for rows, oap, okey in outs:
            self.tt("pool", oap, a[rows, :W], rs[rows, :W], ALU.mult, [ak, rsk], [okey])

    def rope32_gen(self, bA, bAk, bB, bBk, W, seq, rt, rtk, outs):
        if not seq:
            for rws, oap, okey in outs:
                self.cp("act", oap, bA[rws, :W], [bAk], [okey])
            return
        a, ak = self.wk()
        self.tt("dve", a[:, :W], bA[:, :W], rt[:, 0, :W], ALU.mult, [bAk, rtk], [ak])
        b, bk_ = self.wk()
        self.tt("dve", b[:, :W], bB[:, :W], rt[:, 1, :W], ALU.mult, [bBk, rtk], [bk_])
        yield
        for rws, oap, okey in outs:
            self.tt("pool", oap, a[rws, :W], b[rws, :W], ALU.add, [ak, bk_], [okey])

    def run_gen(self, g):
        for _ in g:
            pass

    def gqa(self, l):
        P = self.P
        P.barrier()
        kT = self.ar_bf(0, [128, T])
        V = self.ar_bf(T, [128, NCH, 2, 65])
        Vk = ("@", "gqaV", l)
        self.memset("pool", V[:, :, :, 64:65], 1.0, [Vk])
        wkv, wkvk = self.WS[0], ("WS", 0)
        self.load_w(wkv[:], self.w_in[l, :, GQA_OFF + 256:GQA_OFF + 512], wkvk)
        wsw, wswk = self.WS[1], ("WS", 1)
        src = wkv[:, :, 0:128].rearrange("p k (h a d) -> p k h a d", h=2, a=2)
        dst = wsw[:, :, 0:128].rearrange("p k (h a d) -> p k h a d", h=2, a=2)
        self.cp("pool", dst[:, :, :, 0, :], src[:, :, :, 1, :], [wkvk], [wswk])
        self.cp("pool", dst[:, :, :, 1, :], src[:, :, :, 0, :], [wkvk], [wswk])
        wo_key = self.load_wo_heads(l, 512)
        for bi in range(5):
            s, W = BLKS[bi]
            seq = bi > 0
            bA, bAk = self.bank("G")
            self.proj(bA[:, :W], bAk, lambda kc: wkv[:, kc, 0:128], [wkvk], s, W)
            bB = bBk = rt = rtk = None
            if seq:
                bB, bBk = self.bank("G")
                self.proj(bB[:, :W], bBk, lambda kc: wsw[:, kc, 0:128], [wswk], s, W)
                rt, rtk = self.load_rope(1, s, W)
            self.qk_norm_rope(l, bA, bAk, bB, bBk, W, seq, 76, rt, rtk, kT[:, s:s + W], ("@", "gqak", l, bi))
            nchb = W // 128
            bV, bVk = self.bank("G")
            for i in range(nchb):
                ch = s // 128 + i
                for kc in range(KC):
                    self.mm(bV[:, i * 128:(i + 1) * 128], self.xnT[:, kc, ch * 128:(ch + 1) * 128], wkv[:, kc, 128:256],
                            i == 0 and kc == 0, kc == KC - 1, [("xn", kc, bi), wkvk], [bVk], skip_group_check=True)
            self.cp("act", V[:, s // 128:s // 128 + nchb, :, 0:64],
                    bV[:, :W].rearrange("p (c h d) -> p c h d", c=nchb, h=2), [bVk], [Vk])
        wq, wqk = self.WQ, ("WQ",)
        self.load_w(wq[:, :, 0:256], self.w_in[l, :, GQA_OFF:GQA_OFF + 256], wqk)
        wr, wrk = self.WS[0], ("WS", 0)
        wrs, wrsk = self.WS[1], ("WS", 1)
        srcq = wq[:, :, 0:256].rearrange("p k (h a d) -> p k h a d", h=4, a=2)
        for tile, heads in ((0, (0, 2)), (1, (1, 3))):
            for pos, h in enumerate(heads):
                o0 = tile * 128 + pos * 64
                self.cp("pool", wr[:, :, o0:o0 + 64], wq[:, :, h * 64:(h + 1) * 64], [wqk], [wrk])
                self.cp("pool", wrs[:, :, o0:o0 + 32], srcq[:, :, h, 1, :], [wqk], [wrsk])
                self.cp("pool", wrs[:, :, o0 + 32:o0 + 64], srcq[:, :, h, 0, :], [wqk], [wrsk])
        qms = [self.ar_bf(T + NCH * 130 + i * 2048, [128, 4, 512]) for i in range(2)]
        qmks = [("@", "gqaqm", l, i) for i in range(2)]
        for i in range(2):
            self.memset("pool", qms[i], 0.0, [qmks[i]])

        def qproj_gen(bi, buf):
            s, W = BLKS[bi]
            seq = bi > 0
            qm, qmk = qms[buf], qmks[buf]
            rt = rtk = None
            if seq:
                rt, rtk = self.load_rope(1, s, W)
            for qt in range(2):
                bA, bAk = self.bank("G")
                self.proj(bA[:, :W], bAk, lambda kc: wr[:, kc, qt * 128:(qt + 1) * 128], [wrk], s, W)
                yield
                bB = bBk = None
                if seq:
                    bB, bBk = self.bank("G")
                    self.proj(bB[:, :W], bBk, lambda kc: wrs[:, kc, qt * 128:(qt + 1) * 128], [wrsk], s, W)
                    yield
                yield from self.qk_norm_rope_gen(l, bA, bAk, bB, bBk, W, seq, 74, rt, rtk,
                                                 [(slice(0, 64), qm[0:64, qt, :W], qmk),
                                                  (slice(64, 128), qm[64:128, qt + 2, :W], qmk)])
                yield

        blocks = self.qblocks()
        self.run_gen(qproj_gen(blocks[0], 0))
        for idx, bi in enumerate(blocks):
            s, W = BLKS[bi]
            seq = bi > 0
            nkc = NCH if seq else 2
            buf = idx % 2
            qm, qmk = qms[buf], qmks[buf]
            kkeys = [("@", "gqak", l, b) for b in range(5 if seq else 1)]
            for h in range(4):
                if h == 1 and idx + 1 < len(blocks):
                    self.defer(qproj_gen(blocks[idx + 1], 1 - buf))
                ob, obk = self.bank("O")
                self.attention(qm[:, h, :W], [qmk],
                               lambda kc: kT[:, kc * 128:(kc + 1) * 128], kkeys,
                               lambda kc: V[:, kc, h // 2, :], [Vk], 0, W, nkc, 0.125, ob, obk)
                self.defer(self.finish_head_softmax(ob, obk, h, W))
            self.defer(self.outproj_heads_gen(l, bi, wo_key))
        self.flush()

    def rope32(self, bA, bAk, bB, bBk, W, seq, rt, rtk, out_ap, out_key, rows=slice(0, 128), outs=None):
        if outs is None:
            outs = [(rows, out_ap, out_key)]
        if not seq:
            for rws, oap, okey in outs:
                self.cp("act", oap, bA[rws, :W], [bAk], [okey])
            return
        a, ak = self.wk()
        self.tt("dve", a[rows, :W], bA[rows, :W], rt[rows, 0, :W], ALU.mult, [bAk, rtk], [ak])
        b, bk_ = self.wk()
        self.tt("dve", b[rows, :W], bB[rows, :W], rt[rows, 1, :W], ALU.mult, [bBk, rtk], [bk_])
        for rws, oap, okey in outs:
            self.tt("pool", oap, a[rws, :W], b[rws, :W], ALU.add, [ak, bk_], [okey])

    def swap16(self, dst, src, rkey, wkey, ncols):
        s5 = src.rearrange("p k (b a d) -> p k b a d", a=2, d=16)
        d5 = dst.rearrange("p k (b a d) -> p k b a d", a=2, d=16)
        self.cp("pool", d5[:, :, :, 0, :], s5[:, :, :, 1, :], [rkey], [wkey])
        self.cp("pool", d5[:, :, :, 1, :], s5[:, :, :, 0, :], [rkey], [wkey])

    def diff(self, l):
        P = self.P
        P.barrier()
        lam_init = 0.8 - 0.6 * math.exp(-0.3 * l)
        kT = self.ar_bf(0, [128, 2, T])
        V = self.ar_bf(2 * T, [128, NCH, 4, 65])
        Vk = ("@", "diffV", l)
        self.memset("pool", V[:, :, :, 64:65], 1.0, [Vk])
        lp = self.pb[:, 16:144].rearrange("p (a d) -> p a d", a=4)
        sm = self.sm
        t1, t1k = self.wk()
        self.tt("dve", t1[:, 0:32], lp[:, 0, :], lp[:, 1, :], ALU.mult, ["pb"], [t1k])
        self.tt("dve", t1[:, 32:64], lp[:, 2, :], lp[:, 3, :], ALU.mult, ["pb"], [t1k])
        self.P.op("dve", lambda e: e.tensor_reduce(out=sm[:, 0:2], in_=t1[:, 0:64].rearrange("p (a d) -> p a d", a=2),
                                                    axis=AX.X, op=ALU.add), [t1k], ["sm_lam"])
        self.act(sm[:, 2:4], sm[:, 0:2], AF.Exp, ["sm_lam"], ["sm_lam2"])
        self.tt("dve", sm[:, 4:5], sm[:, 3:4], sm[:, 2:3], ALU.subtract, ["sm_lam2"], ["sm_lam3"])
        self.ts("dve", sm[:, 5:6], sm[:, 4:5], -lam_init, None, ALU.add, None, ["sm_lam3"], ["neglam"])
        neglam = sm[:, 5:6]
        gdp = sm[:, 6:7]
        self.ts("dve", gdp, self.pp[:, l, 81:82], 1.0 - lam_init, None, ALU.mult, None, ["pp"], ["gdp"])
        wk_, wkk = self.WS[0], ("WS", 0)
        self.load_w(wk_[:], self.w_in[l, :, DIFF_OFF + 256:DIFF_OFF + 512], wkk)
        wsw, wswk = self.WS[1], ("WS", 1)
        self.swap16(wsw[:], wk_[:], wkk, wswk, 256)
        wv, wvk = self.WQ, ("WQ",)
        self.load_w(wv[:, :, 0:256], self.w_in[l, :, DIFF_OFF + 512:DIFF_OFF + 768], wvk)
        wo_key = self.load_wo_heads(l, 256)
        for bi in range(5):
            s, W = BLKS[bi]
            seq = bi > 0
            rt = rtk = None
            if seq:
                rt, rtk = self.load_rope(0, s, W)
            for kt in range(2):
                bA, bAk = self.bank("G")
                self.proj(bA[:, :W], bAk, lambda kc: wk_[:, kc, kt * 128:(kt + 1) * 128], [wkk], s, W)
                bB = bBk = None
                if seq:
                    bB, bBk = self.bank("G")
                    self.proj(bB[:, :W], bBk, lambda kc: wsw[:, kc, kt * 128:(kt + 1) * 128], [wswk], s, W)
                self.rope32(bA, bAk, bB, bBk, W, seq, rt, rtk, kT[:, kt, s:s + W], ("@", "diffk", l, kt, bi))
            nchb = W // 128
            for i2 in range(0, nchb, 2):
                bV, bVk = self.bank("G")
                for i in range(2):
                    ch = s // 128 + i2 + i
                    for kc in range(KC):
                        self.mm(bV[:, i * 256:(i + 1) * 256], self.xnT[:, kc, ch * 128:(ch + 1) * 128], wv[:, kc, 0:256],
                                i == 0 and kc == 0, kc == KC - 1, [("xn", kc, bi), wvk], [bVk], skip_group_check=True)
                c0 = s // 128 + i2
                self.cp("act", V[:, c0:c0 + 2, :, 0:64], bV[:].rearrange("p (c h d) -> p c h d", c=2, h=4), [bVk], [Vk])
        wq, wqk = self.WQ, ("WQ",)
        self.load_w(wq[:, :, 0:256], self.w_in[l, :, DIFF_OFF:DIFF_OFF + 256], wqk)
        self.swap16(wq[:, :, 256:512], wq[:, :, 0:256], wqk, wqk, 256)
        qms = [self.ar_bf(2 * T + NCH * 260 + i * 4096, [128, 8, 512]) for i in range(2)]
        qmks = [("@", "diffqm", l, i) for i in range(2)]
        for i in range(2):
            self.memset("pool", qms[i], 0.0, [qmks[i]])

        def qproj_gen(bi, buf):
            s, W = BLKS[bi]
            seq = bi > 0
            qm, qmk = qms[buf], qmks[buf]
            rt = rtk = None
            if seq:
                rt, rtk = self.load_rope(0, s, W)
            for qt in range(2):
                bA, bAk = self.bank("G")
                self.proj(bA[:, :W], bAk, lambda kc: wq[:, kc, qt * 128:(qt + 1) * 128], [wqk], s, W)
                yield
                bB = bBk = None
                if seq:
                    bB, bBk = self.bank("G")
                    self.proj(bB[:, :W], bBk, lambda kc: wq[:, kc, 256 + qt * 128:256 + (qt + 1) * 128], [wqk], s, W)
                    yield
                yield from self.rope32_gen(bA, bAk, bB, bBk, W, seq, rt, rtk,
                                           [(slice(32 * j, 32 * j + 32), qm[32 * j:32 * j + 32, 4 * qt + j, :W], qmk)
                                            for j in range(4)])
                yield

        def diff_finish(obs, h, W):
            (o1, o1k), (o2, o2k) = obs
            r, rk = self.wk()
            self.recip(r[64:65, :W], o1[64:65, :W], [o1k], [rk])
            r2, r2k = self.wk()
            self.recip(r2[64:65, :W], o2[64:65, :W], [o2k], [r2k])
            self.ts("dve", r2[64:65, :W], r2[64:65, :W], neglam[64:65, :], None, ALU.mult, None, [r2k, "neglam"], [r2k])
            yield
            b1, b1k = self.bcast_mm(r, rk, W)
            b2, b2k = self.bcast_mm(r2, r2k, W)
            yield
            c1, c1k = self.wk()
            self.cp("dve", c1[0:64, :W], b1[0:64, :W], [b1k], [c1k])
            c2, c2k = self.wk()
            self.cp("dve", c2[0:64, :W], b2[0:64, :W], [b2k], [c2k])
            yield
            self.tt("dve", c1[0:64, :W], o1[0:64, :W], c1[0:64, :W], ALU.mult, [o1k, c1k], [c1k])
            self.tt("dve", c2[0:64, :W], o2[0:64, :W], c2[0:64, :W], ALU.mult, [o2k, c2k], [c2k])
            yield
            self.tt("pool", c1[0:64, :W], c1[0:64, :W], c2[0:64, :W], ALU.add, [c1k, c2k], [c1k])
            yield
            sqb, sqbk = self.wbt()
            self.tt("pool", sqb[0:64, :W], c1[0:64, :W], c1[0:64, :W], ALU.mult, [c1k], [sqbk])
            yield
            bN, bNk = self.bank("G")
            self.mm(bN[0:64, :W], self.ones_b[0:64, 0:64], sqb[0:64, :W], True, True, [sqbk, "onesb"], [bNk])
            yield
            sd, sdk = self.wkL(0)
            self.act(sd[0:64, :W], bN[0:64, :W], AF.Ln, [bNk, "epsT"], [sdk], scale=1.0 / 64, bias=self.epsT[0:64, :])
            yield
            self.act(sd[0:64, :W], sd[0:64, :W], AF.Exp, [sdk], [sdk], scale=-0.5)
            yield
            self.stt(self.yTh[0:64, h, :W], c1[0:64, :W], gdp[0:64, :], sd[0:64, :W], ALU.mult, ALU.mult,
                     [c1k, sdk, "gdp"], [("yTh", h)])
        blocks = self.qblocks()
        self.run_gen(qproj_gen(blocks[0], 0))
        for idx, bi in enumerate(blocks):
            s, W = BLKS[bi]
            seq = bi > 0
            nkc = NCH if seq else 2
            buf = idx % 2
            qm, qmk = qms[buf], qmks[buf]
            for h in range(4):
                if h == 1 and idx + 1 < len(blocks):
                    self.defer(qproj_gen(blocks[idx + 1], 1 - buf))
                tile = h // 2
                kkeys = [("@", "diffk", l, tile, b) for b in range(5 if seq else 1)]
                obs = []
                for m in range(2):
                    ob, obk = self.bank("O")
                    self.attention(qm[:, 2 * h + m, :W], [qmk],
                                   lambda kc: kT[:, tile, kc * 128:(kc + 1) * 128], kkeys,
                                   lambda kc: V[:, kc, h, :], [Vk], 0, W, nkc, 32 ** -0.5, ob, obk)
                    obs.append((ob, obk))
                self.defer(diff_finish(obs, h, W))
            self.defer(self.outproj_heads_gen(l, bi, wo_key))
        self.flush()


    def mla(self, l):
        P = self.P
        P.barrier()
        kTh = self.ar_bf(0, [128, 4, T])
        V = self.ar_bf(4 * T, [128, NCH, 4, 65])
        Vk = ("@", "mlaV", l)
        self.memset("pool", V[:, :, :, 64:65], 1.0, [Vk])
        off = 4 * T + NCH * 260
        wukv = self.ar_bf(off, [128, 512]); off += 512
        wuq = self.ar_bf(off, [128, 2, 384]); off += 768
        wuqs = self.ar_bf(off, [128, 2, 128]); off += 256
        qTh = self.ar_bf(off, [128, 4, 512]); off += 2048
        cqn = self.ar_bf(off, [128, 2, 512]); off += 1024
        wukvk, wuqk, wuqsk = ("@", "wukv", l), ("@", "wuq", l), ("@", "wuqs", l)
        self.dma("pool", wukv, self.w_ukv[l], (), [wukvk], "wukv")
        self.dma("pool", wuq[:, 0, :], self.w_uq[l, 0:128, :], (), [wuqk], "wuq")
        self.dma("pool", wuq[0:64, 1, :], self.w_uq[l, 128:192, :], (), [wuqk], "wuq")
        for kt in range(2):
            rows = slice(0, 128) if kt == 0 else slice(0, 64)
            for h in range(4):
                c0 = h * 96 + 64
                self.cp("pool", wuqs[rows, kt, h * 32:h * 32 + 16], wuq[rows, kt, c0 + 16:c0 + 32], [wuqk], [wuqsk])
                self.cp("pool", wuqs[rows, kt, h * 32 + 16:h * 32 + 32], wuq[rows, kt, c0:c0 + 16], [wuqk], [wuqsk])
        wkv, wkvk = self.WS[0], ("WS", 0)
        self.load_w(wkv[:, :, 0:160], self.w_in[l, :, MLA_OFF + 192:MLA_OFF + 352], wkvk)
        self.cp("pool", wkv[:, :, 160:176], wkv[:, :, 144:160], [wkvk], [wkvk])
        self.cp("pool", wkv[:, :, 176:192], wkv[:, :, 128:144], [wkvk], [wkvk])
        wo_key = self.load_wo_heads(l, 768)
        gkv = self.pp[:, l, 80:81]
        for bi in range(5):
            s, W = BLKS[bi]
            seq = bi > 0
            bA, bAk = self.bank("G")
            self.proj(bA[:, :W], bAk, lambda kc: wkv[:, kc, 0:128], [wkvk], s, W)
            sq, sqk = self.wbt()
            self.act(sq[:, :W], bA[:, :W], AF.Square, [bAk], [sqk])
            bC, bCk = self.bank("G")
            self.mm(bC[:, :W], self.ones_b, sq[:, :W], True, True, [sqk, "onesb"], [bCk])
            rs, rsk = self.rstd_from_bank(bC, bCk, 128, W, 128)
            ckvn, ckvnk = self.wbt()
            self.stt(ckvn[:, :W], bA[:, :W], gkv, rs[:, :W], ALU.mult, ALU.mult, [bAk, rsk, "pp"], [ckvnk])
            for h in range(4):
                bK, bKk = self.bank("G")
                self.mm(bK[0:64, :W], wukv[:, h * 128:h * 128 + 64], ckvn[:, :W], True, True, [wukvk, ckvnk], [bKk])
                self.cp("act" if h % 2 == 0 else "dve", kTh[0:64, h, s:s + W], bK[0:64, :W], [bKk], [("@", "mlak", l, h, bi)])
            bR, bRk = self.bank("G")
            self.proj(bR[64:96, :W], bRk, lambda kc: wkv[:, kc, 128:160], [wkvk], s, W, tile_position=(0, 64))
            bR2 = bR2k = rt = rtk = None
            if seq:
                bR2, bR2k = self.bank("G")
                self.proj(bR2[64:96, :W], bR2k, lambda kc: wkv[:, kc, 160:192], [wkvk], s, W,
                          tile_position=(0, 64))
                rt, rtk = self.load_rope(0, s, W)
            self.rope32(bR, bRk, bR2, bR2k, W, seq, rt, rtk, kTh[64:96, 0, s:s + W], ("@", "mlakr", l, 0, bi),
                        rows=slice(64, 96))
            for h in range(1, 4):
                self.cp("pool", kTh[64:96, h, s:s + W], kTh[64:96, 0, s:s + W], [("@", "mlakr", l, 0, bi)],
                        [("@", "mlakr", l, h, bi)])
            nchb = W // 128
            for i2 in range(0, nchb, 2):
                bV, bVk = self.bank("G")
                for i in range(2):
                    cl = i2 + i
                    self.mm(bV[:, i * 256:(i + 1) * 256].rearrange("p (h d) -> p h d", h=4),
                            ckvn[:, cl * 128:(cl + 1) * 128],
                            wukv.rearrange("p (h a d) -> p h a d", h=4, a=2)[:, :, 1, :],
                            i == 0, True, [ckvnk, wukvk], [bVk], skip_group_check=True)
                c0 = s // 128 + i2
                self.cp("act", V[:, c0:c0 + 2, :, 0:64], bV[:].rearrange("p (c h d) -> p c h d", c=2, h=4), [bVk], [Vk])
        wq, wqk = self.WQ, ("WQ",)
        self.load_w(wq[:, :, 0:192], self.w_in[l, :, MLA_OFF:MLA_OFF + 192], wqk)
        for bi in self.qblocks():
            s, W = BLKS[bi]
            seq = bi > 0
            nkc = NCH if seq else 2
            nsub = W // 128
            bA, bAk = self.bank("G")
            self.proj(bA[:, :W], bAk, lambda kc: wq[:, kc, 0:128], [wqk], s, W)
            bB, bBk = self.bank("G")
            self.proj(bB[0:64, :W], bBk, lambda kc: wq[:, kc, 128:192], [wqk], s, W)
            sqA, sqAk = self.wbt()
            self.act(sqA[:, :W], bA[:, :W], AF.Square, [bAk], [sqAk])
            sqB, sqBk = self.wbt()
            self.act(sqB[0:64, :W], bB[0:64, :W], AF.Square, [bBk], [sqBk])
            bC, bCk = self.bank("G")
            self.mm(bC[:, :W], self.ones_b, sqA[:, :W], True, False, [sqAk, "onesb"], [bCk])
            self.mm(bC[:, :W], self.ones_b[0:64, :], sqB[0:64, :W], False, True, [sqBk, "onesb"], [bCk])
            rs, rsk = self.rstd_from_bank(bC, bCk, 128, W, 192)
            cqk = ("@", "cqn", l)
            self.stt(cqn[:, 0, :W], bA[:, :W], self.pp[:, l, 78:79], rs[:, :W], ALU.mult, ALU.mult, [bAk, rsk, "pp"], [cqk])
            self.stt(cqn[0:64, 1, :W], bB[0:64, :W], self.pp[0:64, l, 79:80], rs[0:64, :W], ALU.mult, ALU.mult,
                     [bBk, rsk, "pp"], [cqk])
            rt = rtk = None
            if seq:
                rt, rtk = self.load_rope(0, s, W)
            for h in range(4):
                bQ, bQk = self.bank("G")
                self.mm(bQ[0:96, :W], wuq[:, 0, h * 96:(h + 1) * 96], cqn[:, 0, :W], True, False, [wuqk, cqk], [bQk])
                self.mm(bQ[0:96, :W], wuq[0:64, 1, h * 96:(h + 1) * 96], cqn[0:64, 1, :W], False, True, [wuqk, cqk], [bQk])
                bQ2 = bQ2k = None
                if seq:
                    bQ2, bQ2k = self.bank("G")
                    self.mm(bQ2[64:96, :W], wuqs[:, 0, h * 32:(h + 1) * 32], cqn[:, 0, :W], True, False, [wuqsk, cqk], [bQ2k],
                            tile_position=(0, 64))
                    self.mm(bQ2[64:96, :W], wuqs[0:64, 1, h * 32:(h + 1) * 32], cqn[0:64, 1, :W], False, True,
                            [wuqsk, cqk], [bQ2k], tile_position=(0, 64))
                qk_ = ("@", "mlaq", l, h)
                self.cp("act", qTh[0:64, h, :W], bQ[0:64, :W], [bQk], [qk_])
                self.rope32(bQ, bQk, bQ2, bQ2k, W, seq, rt, rtk, qTh[64:96, h, :W], ("@", "mlaqr", l, h), rows=slice(64, 96))
            for h in range(4):
                kkeys = [("@", "mlak", l, h, b) for b in range(5 if seq else 1)] + \
                        [("@", "mlakr", l, h, b) for b in range(5 if seq else 1)]
                ob, obk = self.bank("O")
                self.attention(qTh[0:96, h, :W], [("@", "mlaq", l, h), ("@", "mlaqr", l, h)],
                               lambda kc: kTh[0:96, h, kc * 128:(kc + 1) * 128], kkeys,
                               lambda kc: V[:, kc, h, :], [Vk], 0, W, nkc, 96 ** -0.5, ob, obk)
                self.defer(self.finish_head_softmax(ob, obk, h, W))
            self.defer(self.outproj_heads_gen(l, bi, wo_key))
        self.flush()

    def ssd(self, l):
        P = self.P
        P.barrier()
        off = 0
        XBC = self.ar_bf(off, [128, 4, T]); off += 4 * T
        Bt = self.ar_bf(off, [128, NCH, 128]); off += NCH * 128
        HT = self.ar_bf(off, [128, NCH, 4, 64]); off += NCH * 256
        dI = self.ar_bf(off, [128, 4, 128]); off += 512
        dtt = self.ar_f32(off, [128, NCH, 8]); off += NCH * 16
        at = self.ar_f32(off, [128, NCH, 8]); off += NCH * 16
        cst_ = self.ar_f32(off, [128, NCH, 8]); off += NCH * 16
        tott = self.ar_f32(off, [128, NCH, 8]); off += NCH * 16
        dtw = self.ar_f32(off, [128, NCH, 8]); off += NCH * 16
        decp = self.ar_f32(off, [128, NCH, 2, 2]); off += NCH * 8
        Hc = self.ar_f32(off, [128, 4, 64]); off += 512
        cw = self.ar_f32(off, [128, 3, 128]); off += 768
        aneg = self.ar_f32(off, [128, 8]); off += 16
        dsb = self.ar_f32(off, [128, 4]); off += 8
        assert off <= self.ARENA, off
        MT = self.PTall[:, 0:2, :].rearrange("p a (j n) -> p (a j) n", j=4)
        CD = self.PTall[:, 2:4, :].rearrange("p a (j n) -> p (a j) n", j=4)
        MTk = [("PT", 0), ("PT", 1)]
        CDk = [("PT", 2), ("PT", 3)]
        k = lambda *a: ("@", "ssd", l) + a

        def xs_tok(ch):
            return XBC[:, 0:2, ch * 128:(ch + 1) * 128]

        def xs_tok_head(ch, h):
            return XBC[:, h // 2, ch * 128 + (h % 2) * 64:ch * 128 + (h % 2) * 64 + 64]

        self.act(aneg, self.pb[:, 8:16], AF.Exp, ["pb"], [k("aneg")])
        self.ts("dve", aneg, aneg, -1.0, None, ALU.mult, None, [k("aneg")], [k("aneg")])
        self.tt("dve", dsb, self.pb[:, 208:212], self.pb[:, 212:216], ALU.add, ["pb"], [k("dsb")])
        for h in range(4):
            self.ts("dve", dI[:, h, :], self.idf, dsb[:, h:h + 1], None, ALU.mult, None, ["cstf", k("dsb")], [k("dI")])
        wo_key = self.load_wo(l, 0)
        WB = self.WQ.rearrange("p k (a n) -> p k a n", a=4)
        for ti in range(4):
            ws, wsk = self.WS[ti % 2], ("WS", ti % 2)
            c0 = SSD_OFF + 256 + ti * 128
            self.load_w(ws[:, :, 0:128], self.w_in[l, :, c0:c0 + 128], wsk)
            self.dma("sp", cw, self.cwd[l, :, ti * 128:(ti + 1) * 128].partition_broadcast(128), (), [k("cw")], "cw")
            wbk = ("WQ",)
            for kk in range(3):
                self.tt("pool" if kk < 2 else "dve", WB[:, :, kk, :], ws[:, :, 0:128],
                        cw[:, kk, :].unsqueeze(1).broadcast_to([128, KC, 128]), ALU.mult, [wsk, k("cw")], [wbk])
            for bi in range(5):
                s, W = BLKS[bi]
                seg0, seg1 = (0, 256) if bi == 0 else (256, T)
                bk, bkey = self.bank("G")
                first = True
                for kk in (1, 0, 2):
                    lo, hi = 0, W
                    if kk == 0 and s == seg0:
                        lo = 1
                    if kk == 2 and s + W == seg1:
                        hi = W - 1
                    t0 = s + lo + kk - 1
                    n = hi - lo
                    bset = sorted({self.blk_of(t0), self.blk_of(t0 + n - 1)})
                    for kc in range(KC):
                        self.mm(bk[:, lo:hi], WB[:, kc, kk, :], self.xnT[:, kc, t0:t0 + n], first,
                                kk == 2 and kc == KC - 1, [wbk] + [("xn", kc, b) for b in bset], [bkey],
                                skip_group_check=True)
                        first = False
                self.act(XBC[:, ti, s:s + W], bk[:, :W], AF.Silu, [bkey, "pp"], [k("xbc", ti, bi)],
                         bias=self.pp[:, l, 64 + ti:65 + ti])
        self.dump("xbc_%d" % l, XBC[:, :, :], [k("xbc", ti, b) for ti in range(4) for b in range(5)])
        if self.stop == (l, "ssd_xbc"):
            return
        wdt, wdtk = self.WS[0], ("WS", 0)
        self.load_w(wdt[:, :, 0:8], self.w_in[l, :, SSD_OFF + 768:SSD_OFF + 776], wdtk)
        bk, bkey = self.bank("G")
        for ch in range(NCH):
            bi = self.blk_of(ch * 128)
            for kc in range(KC):
                self.mm(bk[:, ch * 8:(ch + 1) * 8], self.xnT[:, kc, ch * 128:(ch + 1) * 128], wdt[:, kc, 0:8],
                        ch == 0 and kc == 0, kc == KC - 1, [("xn", kc, bi), wdtk], [bkey], skip_group_check=True)
        b3 = bk[:, 0:NCH * 8].rearrange("p (c j) -> p c j", j=8)
        self.tt("dve", dtt, b3, self.pb[:, 0:8].unsqueeze(1).broadcast_to([128, NCH, 8]), ALU.add, [bkey, "pb"], [k("dt")])
        self.act(dtt, dtt, AF.Exp, [k("dt")], [k("dt")])
        self.act(dtt, dtt, AF.Ln, [k("dt"), "oneT"], [k("dt")], bias=self.oneT[:, :])
        self.tt("dve", at, dtt, aneg.unsqueeze(1).broadcast_to([128, NCH, 8]), ALU.mult, [k("dt"), k("aneg")], [k("at")])
        self.dump("dt_%d" % l, dtt, [k("dt")])
        if self.stop == (l, "ssd_dt"):
            return
        bk, bkey = self.bank("G")
        bk2, bkey2 = self.bank("G")
        c3 = bk[:, 0:NCH * 8].rearrange("p (c j) -> p c j", j=8)
        t3 = bk2[:, 0:NCH * 8].rearrange("p (c j) -> p c j", j=8)
        for ch in range(NCH):
            self.mm(c3[:, ch, 0:4], self.U, at[:, ch, 0:4], ch == 0, True, [k("at"), "cstf"], [bkey], skip_group_check=True)
            self.mm(c3[:, ch, 4:8], self.Lm, at[:, ch, 4:8], False, True, [k("at"), "cstf"], [bkey], skip_group_check=True)
            self.mm(t3[:, ch, :], self.ones_f, at[:, ch, :], ch == 0, True, [k("at"), "cstf"], [bkey2], skip_group_check=True)
        self.cp("dve", cst_, c3, [bkey], [k("cs")])
        self.cp("dve", tott, t3, [bkey2], [k("tot")])
        self.tt("dve", dtw, tott, cst_, ALU.subtract, [k("tot"), k("cs")], [k("dtw")])
        self.act(dtw, dtw, AF.Exp, [k("dtw")], [k("dtw")])
        self.tt("dve", dtw, dtw, dtt, ALU.mult, [k("dtw"), k("dt")], [k("dtw")])
        t4 = tott.rearrange("p c (d h) -> p c d h", d=2)
        self.act(decp[0:64], t4[0:64, :, :, 0:2], AF.Exp, [k("tot")], [k("decp")])
        self.act(decp[64:128], t4[64:128, :, :, 2:4], AF.Exp, [k("tot")], [k("decp")])
        self.dump("cs_%d" % l, cst_, [k("cs")])
        if self.stop == (l, "ssd_cs"):
            return
        for ch in range(NCH):
            bi = self.blk_of(ch * 128)
            bk, bkey = self.bank("G")
            bb = bk[:].bitcast(BF16)
            for ti in range(3):
                self.tr(bb[:, ti * 128:(ti + 1) * 128], XBC[:, ti, ch * 128:(ch + 1) * 128], self.idb,
                        [k("xbc", ti, bi), "idb"], [bkey])
            self.cp("act", xs_tok(ch), bb[:, 0:256].rearrange("p (a n) -> p a n", a=2), [bkey], [k("xst", ch)])
            self.cp("dve", Bt[:, ch, :], bb[:, 256:384], [bkey], [k("bt", ch)])
        self.dump("xst_%d" % l, XBC[:, 0:2, :], [k("xst", ch) for ch in range(NCH)])
        self.dump("bt_%d" % l, Bt, [k("bt", ch) for ch in range(NCH)])
        if self.stop == (l, "ssd_tr"):
            return
        self.memset("dve", Hc, 0.0, [k("Hc", 0), k("Hc", 1)])
        orders = [list(range(NCH)), [1, 0] + list(range(NCH - 1, 1, -1))]
        for d in range(2):
            for ch in orders[d]:
                xwt, xwk = self.wbt()
                xw = xwt[:, 0:256]
                self.tt("pool", xw.rearrange("p (a h e) -> p a h e", a=2, h=2),
                        xs_tok(ch).rearrange("p a (h e) -> p a h e", h=2),
                        dtw[:, ch, d * 4:(d + 1) * 4].rearrange("p (a h) -> p a h", a=2).unsqueeze(3).broadcast_to([128, 2, 2, 64]),
                        ALU.mult, [k("xst", ch), k("dtw")], [xwk])
                bk, bkey = self.bank("S")
                for g in range(2):
                    self.mm(bk[64 * g:64 * g + 64, 0:128], Bt[:, ch, 64 * g:64 * g + 64],
                            xw[:, g * 128:(g + 1) * 128], True, True, [k("bt", ch), xwk], [bkey],
                            tile_position=(0, 64 * g), skip_group_check=True)
                hc = Hc[:, 2 * d:2 * d + 2, :]
                self.cp("act", HT[:, ch, 2 * d:2 * d + 2, :], hc, [k("Hc", d)], [k("HT", ch, d)])
                self.tt("dve", hc, hc, decp[:, ch, d, :].unsqueeze(2).broadcast_to([128, 2, 64]), ALU.mult,
                        [k("Hc", d), k("decp")], [k("Hc", d)])
                self.tt("dve", hc, hc, bk[:, 0:128].rearrange("p (h e) -> p h e", h=2), ALU.add, [k("Hc", d), bkey],
                        [k("Hc", d)])
        self.dump("HT_%d" % l, HT, [k("HT", ch, d) for ch in range(NCH) for d in range(2)])
        if self.stop == (l, "ssd_state"):
            return
        wz, wzk = self.WQ, ("WQ",)
        self.load_w(wz[:, :, 0:256], self.w_in[l, :, SSD_OFF:SSD_OFF + 256], wzk)
        ysub = getattr(self, "ysub", 9)
        for bi in self.qblocks()[:getattr(self, "yblk", 9)]:
            s, W = BLKS[bi]
            for half in range(W // 256):
                hs = s + half * 256
                yb, ybk = self.bank("O")
                for ci in range(2):
                    ch = hs // 128 + ci
                    tok = slice(ch * 128, (ch + 1) * 128)
                    bG, bGk = self.bank("G")
                    czt, czk = self.wbt()
                    cz = czt[:, 0:256].rearrange("p (g n) -> p g n", g=2)
                    self.memset("pool", czt[:, 0:256], 0.0, [czk])
                    self.cp("act", cz[0:64, 0, :], XBC[0:64, 3, tok], [k("xbc", 3, bi), czk], [czk])
                    self.cp("act", cz[64:128, 1, :], XBC[64:128, 3, tok], [k("xbc", 3, bi), czk], [czk])
                    self.mm(bG[:, 0:256], XBC[:, 2, tok], czt[:, 0:256], True, True, [k("xbc", 2, bi), czk], [bGk])
                    xdt, xdk = self.wbt()
                    abcs, bCs, dfs, es = {}, {}, {}, {}
                    for d in range(2):
                        abc, abck = self.wk()
                        self.cp("act", abc[:, :].rearrange("p (h n) -> p h n", h=4),
                                at[:, ch, d * 4:(d + 1) * 4].unsqueeze(2).broadcast_to([128, 4, 128]), [k("at")], [abck])
                        abcs[d] = (abc, abck)
                    for d in range(2):
                        abc, abck = abcs[d]
                        bC, bCk = self.bank("S")
                        for hh in range(4):
                            self.mm(bC[:, hh * 128:(hh + 1) * 128], abc[:, hh * 128:(hh + 1) * 128],
                                    self.U if d == 0 else self.Lm, hh == 0, True, [abck, "cstf"], [bCk], skip_group_check=True)
                        bCs[d] = (bC, bCk)
                    for d in range(2):
                        bC, bCk = bCs[d]
                        df, dfk = self.wk()
                        for hh in range(4):
                            jj = d * 4 + hh
                            self.stt(df[:, hh * 128:(hh + 1) * 128], bC[:, hh * 128:(hh + 1) * 128],
                                     cst_[:, ch, jj:jj + 1], self.nmf if d == 0 else self.nmb, ALU.subtract, ALU.add,
                                     [bCk, k("cs"), "cstf"], [dfk])
                        dfs[d] = (df, dfk)
                    for d in range(2):
                        df, dfk = dfs[d]
                        self.act(df[:, :], df[:, :], AF.Exp, [dfk], [dfk])
                    for d in range(2):
                        bC, bCk = bCs[d]
                        e, ek = self.wk()
                        self.act(e[:, :], bC[:, :], AF.Exp, [bCk], [ek])
                        es[d] = (e, ek)
                    for d in range(2):
                        df, dfk = dfs[d]
                        self.tt("dve", MT[:, d * 4:(d + 1) * 4, :].rearrange("p (g h) n -> p g h n", g=2),
                                df[:, :].rearrange("p (g h n) -> p g h n", g=2, h=2),
                                bG[:, 0:256].rearrange("p (g n) -> p g n", g=2).unsqueeze(2).broadcast_to([128, 2, 2, 128]),
                                ALU.mult, [dfk, bGk], [MTk[d]])
                    for d in range(2):
                        e, ek = es[d]
                        self.tt("dve", CD[:, d * 4:(d + 1) * 4, :], e[:, :].rearrange("p (h n) -> p h n", h=4),
                                XBC[:, 3, tok].unsqueeze(1).broadcast_to([128, 4, 128]), ALU.mult,
                                [ek, k("xbc", 3, bi)], [CDk[d]])
                        self.memset("pool", CD[64:128, d * 4:d * 4 + 2, :], 0.0, [CDk[d]])
                        self.memset("pool", CD[0:64, d * 4 + 2:d * 4 + 4, :], 0.0, [CDk[d]])
                    for d in range(2):
                        self.tt("dve", xdt[:, d * 256:(d + 1) * 256].rearrange("p (a h e) -> p a h e", a=2, h=2),
                                xs_tok(ch).rearrange("p a (h e) -> p a h e", h=2),
                                dtt[:, ch, d * 4:(d + 1) * 4].rearrange("p (a h) -> p a h", a=2).unsqueeze(3).broadcast_to([128, 2, 2, 64]),
                                ALU.mult, [k("xst", ch), k("dt")], [xdk])
                    if l == 0:
                        self.dump("MT_%d" % ch, MT, MTk)
                        self.dump("CD_%d" % ch, CD, CDk)
                        self.dump("XD_%d" % ch, xdt[:, :], [xdk])
                        self.dump("CZ_%d" % ch, czt[:, 0:256], [czk])
                    ylev = getattr(self, "ylev", 9)
                    for h in range(4):
                        if ylev < 1:
                            break
                        hh, g, ytile = h % 2, h // 2, h // 2
                        oap = yb[64 * hh:64 * hh + 64, ci * 256 + ytile * 128:ci * 256 + (ytile + 1) * 128]
                        st = (ci == 0 and ytile == 0)
                        self.mm(oap, xs_tok_head(ch, h), dI[:, h, :], st, False, [k("xst", ch), k("dI")], [ybk],
                                tile_position=(0, 64 * hh), skip_group_check=True)
                        for d in range(2):
                            jj = d * 4 + h
                            if ylev < 2:
                                break
                            self.mm(oap, xdt[:, d * 256 + h * 64:d * 256 + (h + 1) * 64], MT[:, jj, :], False, False,
                                    [xdk, MTk[d]], [ybk], tile_position=(0, 64 * hh), skip_group_check=True)
                            if ylev < 3:
                                continue
                            self.mm(oap, HT[:, ch, 2 * d + hh, :], CD[:, jj, :], False,
                                    d == 1, [k("HT", ch, d), CDk[d]], [ybk], tile_position=(0, 64 * hh),
                                    skip_group_check=True)
                if ylev < 4:
                    continue
                if l == 0:
                    ybs, ybsk = self.wk()
                    if ("YB_%d" % (hs // 128)) in self.dumps:
                        self.cp("act", ybs[:, :], yb[:, :], [ybk], [ybsk])
                        self.dump("YB_%d" % (hs // 128), ybs[:, :], [ybsk])
                y4 = yb[:, :].rearrange("p (c t n) -> p c t n", c=2, t=2)
                gts = []
                for zt in range(2):
                    bZ, bZk = self.bank("G")
                    self.proj(bZ[:, :256], bZk, lambda kc: wz[:, kc, zt * 128:(zt + 1) * 128], [wzk], hs, 256, bi=bi)
                    zs, zsk = self.wk()
                    self.act(zs[:, :256], bZ[:, :256], AF.Silu, [bZk], [zsk])
                    self.tt("dve", zs[:, :256].rearrange("p (c n) -> p c n", c=2), zs[:, :256].rearrange("p (c n) -> p c n", c=2),
                            y4[:, :, zt, :], ALU.mult, [zsk, ybk], [zsk])
                    gts.append((zs, zsk))
                bN, bNk = self.bank("G")
                for zt in range(2):
                    sq, sqk = self.wbt()
                    self.act(sq[:, :256], gts[zt][0][:, :256], AF.Square, [gts[zt][1]], [sqk])
                    self.mm(bN[:, :256], self.ones_b, sq[:, :256], zt == 0, zt == 1, [sqk, "onesb"], [bNk])
                rs, rsk = self.rstd_from_bank(bN, bNk, 128, 256, 256)
                for zt in range(2):
                    self.stt(self.yT[:, zt, half * 256:(half + 1) * 256], gts[zt][0][:, :256], self.pp[:, l, 72 + zt:73 + zt],
                             rs[:, :256], ALU.mult, ALU.mult, [gts[zt][1], rsk, "pp"], [("yT", zt)])
            if ylev < 4:
                continue
            self.dump("ssd_yT_%d_%d" % (l, bi), self.yT[:, :, :W], [("yT", 0), ("yT", 1)])
            self.outproj(l, bi, wo_key, 2)

    def mlp(self, l):
        P = self.P
        P.barrier()
        hid = self.ar_bf(0, [128, 4, T])
        W2 = [self.ar_bf(4 * T + i * 4096, [128, 4, 1024]) for i in range(2)]
        blocks = self.qblocks()
        for f in range(8):
            sl = f % 2
            w2k = ("@", "w2", l, sl)
            if sl == 0:
                w1keys = [("WQ",)] * 4
                self.load_w(self.WQ[:], self.w1[l, :, f * 512:(f + 1) * 512], w1keys[0])
                w1aps = [self.WQ[:, :, mt * 128:(mt + 1) * 128] for mt in range(4)]
            else:
                self.load_w(self.WS[0][:], self.w1[l, :, f * 512:f * 512 + 256], ("WS", 0))
                self.load_w(self.WS[1][:], self.w1[l, :, f * 512 + 256:(f + 1) * 512], ("WS", 1))
                w1keys = [("WS", 0), ("WS", 0), ("WS", 1), ("WS", 1)]
                w1aps = [self.WS[mt // 2][:, :, (mt % 2) * 128:(mt % 2 + 1) * 128] for mt in range(4)]
            self.dma("pool", W2[sl], self.w2[l, f * 512:(f + 1) * 512, :].rearrange("(kt p) n -> p kt n", p=128),
                     (), [w2k], "W2_%d" % sl)
            for bi in blocks:
                s, W = BLKS[bi]
                for mt in range(4):
                    bk, bkey = self.bank("O")
                    wap = w1aps[mt]
                    self.proj(bk[:, :W], bkey, lambda kc, wap=wap: wap[:, kc, :], [w1keys[mt]], s, W)
                    r, rk = self.wbt()
                    self.act(r[:, :W], bk[:, :W], AF.Relu, [bkey], [rk])
                    self.tt("pool", hid[:, mt, s:s + W], r[:, :W], r[:, :W], ALU.mult, [rk], [("@", "hid", l, mt, bi)])
            for bi in blocks:
                s, W = BLKS[bi]
                j = 1 if bi == 0 else 0
                for d in range(KC):
                    bk, bkey = self.bank("G")
                    for mt in range(4):
                        self.mm(bk[:, :W], W2[sl][:, mt, d * 128:(d + 1) * 128], hid[:, mt, s:s + W], mt == 0, mt == 3,
                                [w2k, ("@", "hid", l, mt, bi)], [bkey])
                    self.stt(self.hT[:, d, s:s + W], bk[:, :W], self.vec[:, 5, d, j:j + 1], self.hT[:, d, s:s + W],
                             ALU.mult, ALU.add, [bkey, ("vec", 5), ("hT", d, bi)], [("hT", d, bi)])

    def final(self):
        P = self.P
        P.barrier()
        fg = self.ar_f32(0, [128, KC])
        self.dma("sp", fg, self.fgd, (), [("@", "fg")], "fg")
        ot = [self.ar_f32(64 + i * 2048, [128, D]) for i in range(2)]
        for bi in range(1, 5):
            s, W = BLKS[bi]
            bk, bkey = self.bank("G")
            for c in range(KC):
                sq, sqk = self.wbt()
                self.act(sq[:, :W], self.hT[:, c, s:s + W], AF.Square, [("hT", c, bi)], [sqk])
                self.mm(bk[:, :W], self.ones_b, sq[:, :W], c == 0, c == KC - 1, [sqk, "onesb"], [bkey])
            rs, rsk = self.rstd_from_bank(bk, bkey, 128, W, D, out=self.wkL(0))
            for c in range(KC):
                self.stt(self.hT[:, c, s:s + W], self.hT[:, c, s:s + W], fg[:, c:c + 1], rs[:, :W], ALU.mult, ALU.mult,
                         [("hT", c, bi), ("@", "fg"), rsk], [("hT", c, bi)])
            for sub in range(4):
                t0 = s + sub * 128
                i = self.rot("ot", 2)
                okey = ("@", "ot", i)
                for g in range(2):
                    bT, bTk = self.bank("S")
                    for c4 in range(4):
                        c = g * 4 + c4
                        self.tr(bT[:, c4 * 128:(c4 + 1) * 128], self.hT[:, c, t0:t0 + 128], self.idf,
                                [("hT", c, bi), "cstf"], [bTk])
                    self.cp("act" if g == 0 else "dve", ot[i][:, g * 512:(g + 1) * 512], bT[:], [bTk], [okey])
                self.dma("sp", self.out[t0 - CTX:t0 - CTX + 128, :], ot[i], [okey], [("out",)], "out%d" % i)


def _build(n_layers=DEPTH, dumps=(), stop=None):
    kb = KB(n_layers, dumps, stop)
    orig_alloc = kb.alloc

    def alloc2():
        orig_alloc()
        kb.epsT = kb.sb("epsT", [128, 1], F32)
        kb.oneT = kb.sb("oneT", [128, 1], F32)
        kb.WS_all = None
        kb.memset("dve", kb.epsT[:], EPS, ["epsT"])
        kb.memset("dve", kb.oneT[:], 1.0, ["oneT"])
    kb.alloc = alloc2
    nc = kb.build()
    return nc, kb


def _rope_tables():
    def tab(rot):
        rows = SEQ // 64
        row = np.repeat(np.arange(rows), 64).astype(np.float32)
        col = np.tile(np.arange(64), rows).astype(np.float32)
        nf = rot // 4
        inv = (10000.0 ** (-np.arange(nf, dtype=np.float32) / nf)).astype(np.float32)
        ang = np.concatenate([row[:, None] * inv, col[:, None] * inv], axis=-1).astype(np.float32)
        cos, sin = np.cos(ang), np.sin(ang)
        half = rot // 2
        cosT = np.zeros((128, SEQ), np.float32)
        sinT = np.zeros((128, SEQ), np.float32)
        for p in range(128):
            d = p % rot
            i = d % half
            cosT[p] = cos[:, i]
            sinT[p] = -sin[:, i] if d < half else sin[:, i]
        return cosT, sinT
    c32, s32 = tab(32)
    c64, s64 = tab(64)
    return np.stack([c32, s32, c64, s64]).astype(np.float32)


def _consts():
    c = np.zeros((128, 8, 128), np.float32)
    k = np.arange(128)
    c[:, 0, :] = np.eye(128)
    c[:, 1, :] = 1.0
    c[:, 2, :] = (k[:, None] <= k[None, :])
    c[:, 3, :] = (k[:, None] >= k[None, :])
    c[:, 4, :] = np.where(k[None, :] >= k[:, None], 0.0, NEG)
    c[:, 5, :] = np.where(k[None, :] <= k[:, None], 0.0, NEG)
    c[:, 6, :] = (k[:, None] // 64 == k[None, :] // 64)
    return c


def _prep_shared(inp):
    f = lambda a: np.ascontiguousarray(np.asarray(a, dtype=np.float32))
    pp = np.zeros((128, DEPTH, NPP), np.float32)
    pb = np.zeros((DEPTH, NPB), np.float32)
    p = np.arange(128)
    for l in range(DEPTH):
        pp[:, l, 0:48] = f(inp["mod_b"])[l].reshape(48, 128).T
        pp[:, l, 48:56] = f(inp["norm1_g"])[l].reshape(8, 128).T
        pp[:, l, 56:64] = f(inp["norm2_g"])[l].reshape(8, 128).T
        pp[:, l, 64:68] = f(inp["ssd_conv_b"])[l].reshape(4, 128).T
        dd = f(inp["ssd_d"])[l]
        for d in range(2):
            for t in range(2):
                pp[:, l, 68 + d * 2 + t] = dd[d][2 * t + p // 64]
        pp[:, l, 72:74] = f(inp["ssd_norm_g"])[l].reshape(2, 128).T
        gq = f(inp["gqa_q_norm"])[l]
        gk = f(inp["gqa_k_norm"])[l]
        pp[:, l, 74] = gq[p % 64]
        pp[:, l, 75] = gq[(p % 64 + 32) % 64]
        pp[:, l, 76] = gk[p % 64]
        pp[:, l, 77] = gk[(p % 64 + 32) % 64]
        mq = f(inp["mla_q_norm"])[l]
        pp[:, l, 78] = mq[0:128]
        pp[0:64, l, 79] = mq[128:192]
        pp[:, l, 80] = f(inp["mla_kv_norm"])[l]
        pp[:, l, 81] = f(inp["diff_norm_g"])[l][p % 64]
        pb[l, 0:8] = f(inp["ssd_dt_bias"])[l].reshape(8)
        pb[l, 8:16] = f(inp["ssd_a_log"])[l].reshape(8)
        pb[l, 16:144] = f(inp["diff_lambda"])[l].reshape(128)
        pb[l, 144:208] = f(inp["diff_norm_g"])[l]
        pb[l, 208:216] = f(inp["ssd_d"])[l].reshape(8)
    sh = {
        "mod_w": f(inp["mod_w"]), "pp": pp, "pb": pb,
        "conv_wT": np.ascontiguousarray(f(inp["ssd_conv_w"]).transpose(0, 2, 1)),
        "final_gT": np.ascontiguousarray(f(inp["final_norm_g"]).reshape(8, 128).T),
        "w_in": f(inp["w_in"]), "mla_w_uq": f(inp["mla_w_uq"]), "mla_w_ukv": f(inp["mla_w_ukv"]),
        "w_out": f(inp["w_out"]), "mlp_w1": f(inp["mlp_w1"]), "mlp_w2": f(inp["mlp_w2"]),
        "cst": _consts(), "rope": _rope_tables(),
    }
    return sh


def _prep_core(inp, b, sh):
    f = lambda a: np.ascontiguousarray(np.asarray(a, dtype=np.float32))
    m = dict(sh)
    m["xin"] = np.ascontiguousarray(np.concatenate([f(inp["ctx"])[b], f(inp["x"])[b]], axis=0))
    cc = np.stack([f(inp["c"])[b], f(inp["c_ctx"])], axis=0)
    m["ccT"] = np.ascontiguousarray(cc.reshape(2, 8, 128).transpose(2, 1, 0))
    return m


_NC_CACHE = {}


def kernel(**inputs):
    if "nc" not in _NC_CACHE:
        _NC_CACHE["nc"] = _build()[0]
    nc = _NC_CACHE["nc"]
    sh = _prep_shared(inputs)
    in_maps = [_prep_core(inputs, b, sh) for b in range(8)]
    res = run_bass_kernel_spmd(nc, in_maps, core_ids=list(range(8)))
    out = np.stack([np.asarray(r["out"], dtype=np.float32) for r in res.results], axis=0)
    return out
```
